# Optimizing a Trainium2 kernel written in Bass

```python
import jax, jax.numpy as jnp
from jax import lax
import numpy as np

D_MODEL = 2048
BATCH = 4
SEQ = 4096
DEPTH = 2
DEC_BATCH = 8
DEC_SEQ = 16
PAST_LEN = 4096

CHUNK = 64
HEAD_DIM = 128
D_MIX = D_MODEL
GDN_WIDTH = D_MIX // 4
GDN_HEADS = GDN_WIDTH // HEAD_DIM
GDN_CONV = 4
FOX_WIDTH = D_MIX // 2
FOX_HEADS = FOX_WIDTH // HEAD_DIM
Q_BLOCK = 128
LRU_WIDTH = D_MIX - GDN_WIDTH - FOX_WIDTH
LRU_BLOCKS = 4
LRU_BLOCK_W = LRU_WIDTH // LRU_BLOCKS
LRU_CONV = 4
LRU_C = 8.0
D_FF = (D_MODEL * 11 // 4) // 128 * 128
FFN_CONV = 3
EPS = 1e-6
IN_SIZES = (GDN_WIDTH, GDN_WIDTH, GDN_WIDTH, GDN_WIDTH, GDN_HEADS, GDN_HEADS,
            FOX_WIDTH, FOX_WIDTH, FOX_WIDTH, FOX_HEADS, LRU_WIDTH, LRU_WIDTH)
IN_WIDTH = sum(IN_SIZES)

kernel_name = 'hybrid_stream_encoder_step'

F32 = jnp.float32


def rmsnorm(x, g):
    xf = x.astype(F32)
    y = xf * lax.rsqrt(jnp.mean(xf * xf, axis=-1, keepdims=True) + EPS)
    return (y * g.astype(F32)).astype(x.dtype)


def l2norm(t):
    return t * lax.rsqrt(jnp.sum(t * t, axis=-1, keepdims=True) + EPS)


def causal_conv(x, past, w):
    width = w.shape[0]
    L = x.shape[1]
    xp = jnp.concatenate([past.astype(x.dtype), x], axis=1)
    y = sum(xp[:, j:j + L] * w[j] for j in range(width))
    return y, xp[:, xp.shape[1] - (width - 1):]


def gdn_chunk(S, blk):
    q, k, v, g, beta = blk
    c = q.shape[-2]
    G = jnp.cumsum(g, axis=-1)
    pos = jnp.arange(c)
    diff = G[..., :, None] - G[..., None, :]
    d_strict = jnp.exp(jnp.where(pos[:, None] > pos[None, :], diff, -jnp.inf))
    d_incl = jnp.exp(jnp.where(pos[:, None] >= pos[None, :], diff, -jnp.inf))
    kb = k * beta[..., None]
    m = jnp.einsum('bhtd,bhsd->bhts', kb, k) * d_strict
    rhs = jnp.concatenate([v * beta[..., None], kb * jnp.exp(G)[..., None]], axis=-1)
    sol = lax.linalg.triangular_solve(m + jnp.eye(c, dtype=F32), rhs, left_side=True,
                                      lower=True, unit_diagonal=True)
    u, w = sol[..., :HEAD_DIM], sol[..., HEAD_DIM:]
    delta = u - jnp.einsum('bhtk,bhkv->bhtv', w, S)
    o = (jnp.einsum('bhtk,bhkv->bhtv', q * jnp.exp(G)[..., None], S)
         + jnp.einsum('bhts,bhsv->bhtv', jnp.einsum('bhtd,bhsd->bhts', q, k) * d_incl, delta))
    g_last = G[..., -1]
    S_new = (jnp.exp(g_last)[..., None, None] * S
             + jnp.einsum('bhtk,bhtv->bhkv', k * jnp.exp(g_last[..., None] - G)[..., None], delta))
    return S_new, o


def gdn_mixer(q, k, v, z, b, a, conv_past, S0, conv_w, a_log, dt_bias, norm_g):
    B, L, _ = q.shape
    dt = q.dtype
    qkv, conv_new = causal_conv(jnp.concatenate([q, k, v], axis=-1), conv_past, conv_w)
    qkv = jax.nn.silu(qkv.astype(F32))
    q, k, v = [t.reshape(B, L, GDN_HEADS, HEAD_DIM) for t in jnp.split(qkv, 3, axis=-1)]
    q = l2norm(q) * HEAD_DIM ** -0.5
    k = l2norm(k)
    beta = jax.nn.sigmoid(b.astype(F32))
    g = -jnp.exp(a_log.astype(F32)) * jax.nn.softplus(a.astype(F32) + dt_bias.astype(F32))
    c = min(L, CHUNK)
    n = L // c

    def to_chunks(t):
        t = jnp.moveaxis(t, 2, 1)
        t = t.reshape(t.shape[:2] + (n, c) + t.shape[3:])
        return jnp.moveaxis(t, 2, 0)

    S_fin, o = lax.scan(gdn_chunk, S0.astype(F32), [to_chunks(t) for t in (q, k, v, g, beta)])
    o = jnp.moveaxis(o, 0, 2).reshape(B, GDN_HEADS, L, HEAD_DIM).transpose(0, 2, 1, 3)
    o = rmsnorm(o, norm_g) * jax.nn.silu(z.astype(F32).reshape(B, L, GDN_HEADS, HEAD_DIM))
    return o.reshape(B, L, GDN_WIDTH).astype(dt), conv_new, S_fin


def fox_attend(q, fq, qpos, k, v, fk, kpos):
    s = jnp.einsum('bqhd,bkhd->bhqk', q, k) * HEAD_DIM ** -0.5
    s = s + jnp.moveaxis(fq, 1, 2)[..., :, None] - jnp.moveaxis(fk, 1, 2)[..., None, :]
    s = jnp.where(kpos[None, :] <= qpos[:, None], s, -jnp.inf)
    p = jax.nn.softmax(s, axis=-1)
    return jnp.einsum('bhqk,bkhd->bqhd', p, v)


def fox_prompt(q, k, v, logf):
    L = q.shape[1]
    qb = min(Q_BLOCK, L)
    nb = L // qb
    F = jnp.cumsum(logf, axis=1)
    kpos = jnp.arange(L)

    def block(i):
        st = i * qb
        qs = lax.dynamic_slice_in_dim(q, st, qb, axis=1)
        fq = lax.dynamic_slice_in_dim(F, st, qb, axis=1)
        return fox_attend(qs, fq, st + jnp.arange(qb), k, v, F, kpos)

    o = lax.map(block, jnp.arange(nb))
    return jnp.moveaxis(o, 0, 1).reshape(q.shape)


def fox_sample(q, k, v, logf, ck, cv, clogf):
    P = ck.shape[1]
    L = q.shape[1]
    k_all = jnp.concatenate([ck, k], axis=1)
    v_all = jnp.concatenate([cv, v], axis=1)
    F = jnp.cumsum(jnp.concatenate([clogf, logf], axis=1), axis=1)
    return fox_attend(q, F[:, P:], P + jnp.arange(L), k_all, v_all, F, jnp.arange(P + L))


def lru_mixer(xb, gate, conv_past, h0, conv_w, conv_b, w_a, b_a, w_x, b_x, lam, norm_g):
    B, L, _ = xb.shape
    dt = xb.dtype
    xc, conv_new = causal_conv(xb, conv_past, conv_w)
    xc = (xc + conv_b).astype(F32)
    xblk = xc.reshape(B, L, LRU_BLOCKS, LRU_BLOCK_W)
    r = jax.nn.sigmoid(jnp.einsum('blnc,ncd->blnd', xblk, w_a.astype(F32)).reshape(B, L, LRU_WIDTH) + b_a)
    i = jax.nn.sigmoid(jnp.einsum('blnc,ncd->blnd', xblk, w_x.astype(F32)).reshape(B, L, LRU_WIDTH) + b_x)
    log_a = -LRU_C * r * jax.nn.softplus(-lam.astype(F32))
    a = jnp.exp(log_a)
    u = jnp.sqrt(-jnp.expm1(2.0 * log_a)) * (i * xc)
    u = u.at[:, 0].add(a[:, 0] * h0.astype(F32))

    def combine(e1, e2):
        a1, b1 = e1
        a2, b2 = e2
        return a1 * a2, a2 * b1 + b2

    _, h = lax.associative_scan(combine, (a, u), axis=1)
    y = rmsnorm(h.reshape(B, L, LRU_BLOCKS, LRU_BLOCK_W), norm_g.reshape(LRU_BLOCKS, LRU_BLOCK_W))
    y = y.reshape(B, L, LRU_WIDTH) * jax.nn.gelu(gate.astype(F32))
    return y.astype(dt), conv_new, h[:, -1]


def conv_ffn(h, conv_past, w_up, conv_w, w_down):
    gp, val = jnp.split(h @ w_up, 2, axis=-1)
    gc, conv_new = causal_conv(gp, conv_past, conv_w)
    return (jax.nn.silu(gc) * val) @ w_down, conv_new


def layer(x, l, p, fox_cache, gdn_S, gdn_conv, lru_h, lru_conv, ffn_conv):
    B, L, _ = x.shape
    dt = x.dtype
    h = rmsnorm(x, p['norm_mix_g'][l])
    proj = h @ p['w_in'][l]
    gq, gk, gv, gz, gb, ga, fq, fk, fv, ff, lx, lg = jnp.split(
        proj, np.cumsum(IN_SIZES)[:-1].tolist(), axis=-1)
    o_gdn, gdn_conv_new, gdn_S_new = gdn_mixer(
        gq, gk, gv, gz, gb, ga, gdn_conv, gdn_S, p['gdn_conv_w'][l], p['gdn_a_log'][l],
        p['gdn_dt_bias'][l], p['gdn_norm_g'][l])
    q4 = fq.reshape(B, L, FOX_HEADS, HEAD_DIM).astype(F32)
    k4 = fk.reshape(B, L, FOX_HEADS, HEAD_DIM).astype(F32)
    v4 = fv.reshape(B, L, FOX_HEADS, HEAD_DIM).astype(F32)
    logf = jax.nn.log_sigmoid(ff.astype(F32) + p['fox_f_bias'][l])
    if fox_cache is None:
        o = fox_prompt(q4, k4, v4, logf)
    else:
        ck, cv, cl = fox_cache
        o = fox_sample(q4, k4, v4, logf, ck.astype(F32), cv.astype(F32), cl.astype(F32))
    o_fox = rmsnorm(o, p['fox_norm_g'][l]).reshape(B, L, FOX_WIDTH).astype(dt)
    o_lru, lru_conv_new, lru_h_new = lru_mixer(
        lx, lg, lru_conv, lru_h, p['lru_conv_w'][l], p['lru_conv_b'][l], p['lru_w_a'][l],
        p['lru_b_a'][l], p['lru_w_x'][l], p['lru_b_x'][l], p['lru_lambda'][l], p['lru_norm_g'][l])
    x = x + jnp.concatenate([o_gdn, o_fox, o_lru], axis=-1) @ p['w_out'][l]
    f_out, ffn_conv_new = conv_ffn(rmsnorm(x, p['norm_ffn_g'][l]), ffn_conv, p['ffn_w_up'][l],
                                   p['ffn_conv_w'][l], p['ffn_w_down'][l])
    x = x + f_out
    new = [k4.astype(dt), v4.astype(dt), logf.astype(dt), gdn_S_new.astype(dt), gdn_conv_new,
           lru_h_new.astype(dt), lru_conv_new, ffn_conv_new]
    return x, new


def run_group(x, p, fox_k, fox_v, fox_logf, gdn_S, gdn_conv, lru_h, lru_conv, ffn_conv):
    outs = [[] for _ in range(8)]
    for l in range(DEPTH):
        fox_cache = None if fox_k is None else (fox_k[l], fox_v[l], fox_logf[l])
        x, new = layer(x, l, p, fox_cache, gdn_S[l], gdn_conv[l], lru_h[l], lru_conv[l], ffn_conv[l])
        for lst, arr in zip(outs, new):
            lst.append(arr)
    return rmsnorm(x, p['final_norm_g']), [jnp.stack(lst) for lst in outs]


def setup_inputs(seed: int = 0) -> dict:
    key = jax.random.key(seed)
    ks = jax.random.split(key, 40)
    nrm = lambda k, s, sc=1.0: sc * jax.random.normal(k, s, F32)
    unif = lambda k, s, lo, hi: jax.random.uniform(k, s, F32, lo, hi)
    dt = jnp.exp(unif(ks[14], (DEPTH, GDN_HEADS), float(np.log(1e-3)), float(np.log(1e-1))))
    lru_base = unif(ks[20], (DEPTH, LRU_WIDTH), 0.9, 0.999) ** (1.0 / LRU_C)
    return {
        'x_prompt': nrm(ks[0], (BATCH, SEQ, D_MODEL)),
        'x_sample': nrm(ks[1], (DEC_BATCH, DEC_SEQ, D_MODEL)),
        'cache_fox_k': nrm(ks[2], (DEPTH, DEC_BATCH, PAST_LEN, FOX_HEADS, HEAD_DIM)),
        'cache_fox_v': nrm(ks[3], (DEPTH, DEC_BATCH, PAST_LEN, FOX_HEADS, HEAD_DIM)),
        'cache_fox_logf': jax.nn.log_sigmoid(2.5 + nrm(ks[4], (DEPTH, DEC_BATCH, PAST_LEN, FOX_HEADS))),
        'state_gdn': nrm(ks[5], (DEPTH, DEC_BATCH, GDN_HEADS, HEAD_DIM, HEAD_DIM), 0.1),
        'state_gdn_conv': nrm(ks[6], (DEPTH, DEC_BATCH, GDN_CONV - 1, 3 * GDN_WIDTH)),
        'state_lru': nrm(ks[7], (DEPTH, DEC_BATCH, LRU_WIDTH), 0.5),
        'state_lru_conv': nrm(ks[8], (DEPTH, DEC_BATCH, LRU_CONV - 1, LRU_WIDTH)),
        'state_ffn_conv': nrm(ks[9], (DEPTH, DEC_BATCH, FFN_CONV - 1, D_FF)),
        'norm_mix_g': 1.0 + nrm(ks[10], (DEPTH, D_MODEL), 0.02),
        'w_in': nrm(ks[11], (DEPTH, D_MODEL, IN_WIDTH), D_MODEL ** -0.5),
        'gdn_conv_w': nrm(ks[12], (DEPTH, GDN_CONV, 3 * GDN_WIDTH), GDN_CONV ** -0.5),
        'gdn_a_log': jnp.log(unif(ks[13], (DEPTH, GDN_HEADS), 1.0, 16.0)),
        'gdn_dt_bias': dt + jnp.log(-jnp.expm1(-dt)),
        'gdn_norm_g': 1.0 + nrm(ks[15], (DEPTH, HEAD_DIM), 0.02),
        'fox_f_bias': unif(ks[16], (DEPTH, FOX_HEADS), 1.0, 4.0),
        'fox_norm_g': 1.0 + nrm(ks[17], (DEPTH, FOX_HEADS, HEAD_DIM), 0.02),
        'lru_conv_w': nrm(ks[18], (DEPTH, LRU_CONV, LRU_WIDTH), LRU_CONV ** -0.5),
        'lru_conv_b': nrm(ks[19], (DEPTH, LRU_WIDTH), 0.02),
        'lru_w_a': nrm(ks[21], (DEPTH, LRU_BLOCKS, LRU_BLOCK_W, LRU_BLOCK_W), LRU_BLOCK_W ** -0.5),
        'lru_b_a': nrm(ks[22], (DEPTH, LRU_WIDTH), 0.02),
        'lru_w_x': nrm(ks[23], (DEPTH, LRU_BLOCKS, LRU_BLOCK_W, LRU_BLOCK_W), LRU_BLOCK_W ** -0.5),
        'lru_b_x': nrm(ks[24], (DEPTH, LRU_WIDTH), 0.02),
        'lru_lambda': jnp.log(lru_base / (1.0 - lru_base)),
        'lru_norm_g': 1.0 + nrm(ks[25], (DEPTH, LRU_WIDTH), 0.02),
        'w_out': nrm(ks[26], (DEPTH, D_MIX, D_MODEL), D_MIX ** -0.5),
        'norm_ffn_g': 1.0 + nrm(ks[27], (DEPTH, D_MODEL), 0.02),
        'ffn_w_up': nrm(ks[28], (DEPTH, D_MODEL, 2 * D_FF), D_MODEL ** -0.5),
        'ffn_conv_w': nrm(ks[29], (DEPTH, FFN_CONV, D_FF), FFN_CONV ** -0.5),
        'ffn_w_down': nrm(ks[30], (DEPTH, D_FF, D_MODEL), D_FF ** -0.5),
        'final_norm_g': 1.0 + nrm(ks[31], (D_MODEL,), 0.02),
    }


def reference(x_prompt, x_sample, cache_fox_k, cache_fox_v, cache_fox_logf, state_gdn, state_gdn_conv,
              state_lru, state_lru_conv, state_ffn_conv, norm_mix_g, w_in, gdn_conv_w, gdn_a_log,
              gdn_dt_bias, gdn_norm_g, fox_f_bias, fox_norm_g, lru_conv_w, lru_conv_b, lru_w_a, lru_b_a,
              lru_w_x, lru_b_x, lru_lambda, lru_norm_g, w_out, norm_ffn_g, ffn_w_up, ffn_conv_w,
              ffn_w_down, final_norm_g):
    p = dict(norm_mix_g=norm_mix_g, w_in=w_in, gdn_conv_w=gdn_conv_w, gdn_a_log=gdn_a_log,
             gdn_dt_bias=gdn_dt_bias, gdn_norm_g=gdn_norm_g, fox_f_bias=fox_f_bias, fox_norm_g=fox_norm_g,
             lru_conv_w=lru_conv_w, lru_conv_b=lru_conv_b, lru_w_a=lru_w_a, lru_b_a=lru_b_a,
             lru_w_x=lru_w_x, lru_b_x=lru_b_x, lru_lambda=lru_lambda, lru_norm_g=lru_norm_g,
             w_out=w_out, norm_ffn_g=norm_ffn_g, ffn_w_up=ffn_w_up, ffn_conv_w=ffn_conv_w,
             ffn_w_down=ffn_w_down, final_norm_g=final_norm_g)
    bp = x_prompt.shape[0]
    dtp = x_prompt.dtype
    y_prompt, sp = run_group(
        x_prompt, p, None, None, None,
        jnp.zeros((DEPTH, bp, GDN_HEADS, HEAD_DIM, HEAD_DIM), dtp),
        jnp.zeros((DEPTH, bp, GDN_CONV - 1, 3 * GDN_WIDTH), dtp),
        jnp.zeros((DEPTH, bp, LRU_WIDTH), dtp),
        jnp.zeros((DEPTH, bp, LRU_CONV - 1, LRU_WIDTH), dtp),
        jnp.zeros((DEPTH, bp, FFN_CONV - 1, D_FF), dtp))
    y_sample, ss = run_group(x_sample, p, cache_fox_k, cache_fox_v, cache_fox_logf, state_gdn,
                             state_gdn_conv, state_lru, state_lru_conv, state_ffn_conv)
    fk_p, fv_p, fl_p, gs_p, gc_p, lh_p, lc_p, fc_p = sp
    fk_s, fv_s, fl_s, gs_s, gc_s, lh_s, lc_s, fc_s = ss
    return (y_prompt, y_sample, fk_p, fv_p, fl_p, gs_p, gc_p, lh_p, lc_p, fc_p,
            fk_s, fv_s, fl_s, gs_s, gc_s, lh_s, lc_s, fc_s)
```

```python
import numpy as np
from contextlib import ExitStack
import concourse.bass as bass
import concourse.mybir as mybir
from concourse.bass_utils import run_bass_kernel_spmd

F32 = mybir.dt.float32
BF16 = mybir.dt.bfloat16
AF = mybir.ActivationFunctionType
ALU = mybir.AluOpType

ENGS = ("pe", "act", "dve", "pool", "sp")
D = 2048
DFF = 5632
INW = 6160
EPS = 1e-6
NPK = 1816
PK_GMIX, PK_GFFN, PK_GFIN, PK_GCW, PK_LCW, PK_LCB, PK_LBA, PK_LBX, PK_LAM, PK_LNG, PK_FCW = 0, 16, 32, 48, 96, 112, 116, 120, 124, 128, 132
PK_GNG4, PK_FNG, PK_ALOG, PK_DTB, PK_FB = 264, 776, 1800, 1804, 1808
C_ID, C_TRI, C_ONE, C_MLS = 0, 128, 256, 384
NCST = 512


class Buf:
    __slots__ = ("name", "w", "r", "excl")

    def __init__(self, name="", excl=False):
        self.name = name
        self.w = None
        self.r = {}
        self.excl = excl


class Op:
    __slots__ = ("eng", "fn", "deps", "dma", "tok", "need_sig", "idx")

    def __init__(self, eng, fn, dma):
        self.eng = eng
        self.fn = fn
        self.dma = dma
        self.deps = set()
        self.tok = None
        self.need_sig = False


class Prog:
    def __init__(self, nc, n_dma_sems=12):
        self.nc = nc
        self.ops = []
        self.n_dma_sems = n_dma_sems
        self.guard = None
        self.stopped = False
        self.nfence = 0

    def add(self, eng, fn, reads=(), writes=(), dma=False, guard=True):
        import os as _os
        if self.stopped or len(self.ops) >= int(_os.environ.get("STOPOPS", "100000000")):
            return None
        op = Op(eng, fn, dma)
        idx = len(self.ops)
        op.idx = idx
        reads = list(reads)
        if guard and self.guard is not None:
            reads.append(self.guard)
        for b in reads:
            if b.w is not None:
                op.deps.add(b.w)
            if b.excl:
                for key, ridx in b.r.items():
                    if key != eng:
                        op.deps.add(ridx)
        for b in writes:
            if b.w is not None:
                op.deps.add(b.w)
            for ridx in b.r.values():
                op.deps.add(ridx)
        op.deps.discard(idx)
        if eng == "pe" and not dma:
            op.deps = {d for d in op.deps if not (self.ops[d].eng == "pe" and not self.ops[d].dma)}
        for d in op.deps:
            self.ops[d].need_sig = True
        self.ops.append(op)
        for b in reads:
            key = ("d", idx) if dma else eng
            b.r[key] = idx
        for b in writes:
            b.w = idx
            b.r = {}
        return op

    def emit(self):
        nc = self.nc
        ops = self.ops
        nds = self.n_dma_sems
        dslot = {}
        dcount = {q: [0] * nds for q in ("sp", "pool")}
        drr = {q: 0 for q in ("sp", "pool")}
        last_on_slot = {}
        prewait = {}
        for op in ops:
            if op.dma:
                i = drr[op.eng] % nds
                drr[op.eng] += 1
                if (op.eng, i) in last_on_slot:
                    prewait[op.idx] = last_on_slot[(op.eng, i)]
                last_on_slot[(op.eng, i)] = op.idx
                dcount[op.eng][i] += 1
                dslot[op.idx] = (op.eng, i, dcount[op.eng][i] * 16)
        ordn = {}
        ecnt = {e: 0 for e in ENGS}
        prev_on = {e: None for e in ENGS}
        vc = [None] * len(ops)

        def merge(a, b):
            for kk, vv in b.items():
                if a.get(kk, 0) < vv:
                    a[kk] = vv
        for op in ops:
            v = {}
            for d in op.deps:
                merge(v, vc[d])
            if op.dma:
                q, i, val = dslot[op.idx]
                v[("d", q, i)] = max(v.get(("d", q, i), 0), val)
            else:
                ecnt[op.eng] += 1
                ordn[op.idx] = ecnt[op.eng]
                if prev_on[op.eng] is not None:
                    merge(v, vc[prev_on[op.eng]])
                v[("e", op.eng)] = ecnt[op.eng]
                prev_on[op.eng] = op.idx
            vc[op.idx] = v
        per_eng = {e: [op for op in ops if op.eng == e] for e in ENGS}

        def implied(known, d):
            if ops[d].dma:
                q, i, val = dslot[d]
                return known.get(("d", q, i), 0) >= val
            return known.get(("e", ops[d].eng), 0) >= ordn[d]

        def plan(e):
            known = {}
            out = []
            for op in per_eng[e]:
                ws = []
                cand = sorted(op.deps, reverse=True)
                if op.idx in prewait:
                    cand.append(prewait[op.idx])
                for d in cand:
                    if not implied(known, d):
                        ws.append(d)
                        merge(known, vc[d])
                out.append((op, ws))
            return out
        plans = {e: plan(e) for e in ENGS}
        for op in ops:
            op.need_sig = op.dma
        for e in ENGS:
            for op, ws in plans[e]:
                for d in ws:
                    ops[d].need_sig = True
        with ExitStack() as es:
            esem = {e: es.enter_context(nc.semaphore("s_" + e)) for e in ENGS}
            dsem = {q: [es.enter_context(nc.semaphore("d_%s%d" % (q, i))) for i in range(nds)] for q in ("sp", "pool")}
            ecount = {e: 0 for e in ENGS}
            for op in ops:
                if op.dma:
                    q, i, val = dslot[op.idx]
                    op.tok = (dsem[q][i], val)
                elif op.need_sig:
                    ecount[op.eng] += 1
                    op.tok = (esem[op.eng], ecount[op.eng])
            block = es.enter_context(nc.Block())

            def run(e, eng):
                for op, ws in plans[e]:
                    for d in ws:
                        sem, val = ops[d].tok
                        eng.wait_ge(sem, val)
                    ins = op.fn(eng)
                    if op.need_sig:
                        ins.then_inc(op.tok[0], 16 if op.dma else 1)
                if e == "sp":
                    for q in dsem:
                        for i in range(nds):
                            if dcount[q][i] > 0:
                                eng.wait_ge(dsem[q][i], dcount[q][i] * 16)

            @block.tensor
            def _(eng):
                run("pe", eng)

            @block.scalar
            def _(eng):
                run("act", eng)

            @block.vector
            def _(eng):
                run("dve", eng)

            @block.gpsimd
            def _(eng):
                run("pool", eng)

            @block.sync
            def _(eng):
                run("sp", eng)


class T_:
    def __init__(self, t, name, excl=False):
        self.t = t
        self.b = Buf(name, excl)

    def __getitem__(self, k):
        return self.t[k]


def build(SEQ=4096, PAST=4096, TP=512, DEPTH=2):
    nc = bass.Bass("TRN2", target_bir_lowering=False)
    NT = SEQ // TP
    PB = PAST // 128
    LS = 16

    def din(name, shape):
        return nc.dram_tensor(name, list(shape), F32, kind="ExternalInput").ap()

    def dout(name, shape):
        return nc.dram_tensor(name, list(shape), F32, kind="ExternalOutput").ap()

    x_p = din("x_p", [SEQ, D]); x_s = din("x_s", [2, LS, D])
    ck = din("ck", [DEPTH, 2, PAST, 1024]); cv = din("cv", [DEPTH, 2, PAST, 1024]); cl = din("cl", [DEPTH, 2, PAST, 8])
    sg = din("sg", [DEPTH, 2, 4, 128, 128]); sgc = din("sgc", [DEPTH, 2, 128, 36])
    sl = din("sl", [DEPTH, 2, 128, 4]); slc = din("slc", [DEPTH, 2, 128, 12]); sfc = din("sfc", [DEPTH, 2, 128, 88])
    w_in = din("w_in", [DEPTH, D, INW]); w_out = din("w_out", [DEPTH, D, D])
    w_up = din("w_up", [DEPTH, D, 2 * DFF]); w_dn = din("w_dn", [DEPTH, DFF, D])
    lwa = din("lwa", [DEPTH, 4, 128, 128]); lwx = din("lwx", [DEPTH, 4, 128, 128])
    pk_d = din("pk", [DEPTH, 128, NPK]); cst_d = din("cst", [128, NCST])

    y_p = dout("y_p", [SEQ, D]); y_s = dout("y_s", [2, LS, D])
    fk_p = dout("fk_p", [DEPTH, SEQ, 1024]); fv_p = dout("fv_p", [DEPTH, SEQ, 1024]); fl_p = dout("fl_p", [DEPTH, SEQ, 8])
    gs_p = dout("gs_p", [DEPTH, 4, 128, 128]); gc_p = dout("gc_p", [DEPTH, 128, 36]); lh_p = dout("lh_p", [DEPTH, 128, 4])
    lc_p = dout("lc_p", [DEPTH, 128, 12]); fc_p = dout("fc_p", [DEPTH, 128, 88])
    fk_s = dout("fk_s", [DEPTH, 2, LS, 1024]); fv_s = dout("fv_s", [DEPTH, 2, LS, 1024]); fl_s = dout("fl_s", [DEPTH, 2, LS, 8])
    gs_s = dout("gs_s", [DEPTH, 2, 4, 128, 128]); gc_s = dout("gc_s", [DEPTH, 2, 128, 36]); lh_s = dout("lh_s", [DEPTH, 2, 128, 4])
    lc_s = dout("lc_s", [DEPTH, 2, 128, 12]); fc_s = dout("fc_s", [DEPTH, 2, 128, 88])

    P = Prog(nc)
    sb_base = (nc.sbuf_base + 63) // 64 * 64
    sb_top = nc.sbuf_top
    cur = [sb_base]
    cnt = [0]

    def alloc(shape, dt, name="t"):
        nbytes = int(np.prod(shape[1:])) * (4 if dt == F32 else 2)
        nbytes = (nbytes + 63) // 64 * 64
        off = cur[0]
        cur[0] += nbytes
        assert cur[0] <= sb_top, ("SBUF overflow", name, cur[0], sb_top)
        cnt[0] += 1
        t = nc.alloc_sbuf_tensor_at("%s_%d" % (name, cnt[0]), list(shape), dt, offset=off)
        return T_(t, name)

    ps = []
    for i in range(8):
        ps.append(T_(nc.alloc_psum_tensor("ps%d" % i, [128, 512], F32), "ps%d" % i, excl=True))

    def bl(xs):
        return [x.b if isinstance(x, T_) else x for x in xs]

    def ACT(out, in_, func, R, W, **kw):
        P.add("act", lambda e: e.activation(out=out, in_=in_, func=func, **kw), bl(R), bl(W))

    def TT(out, in0, in1, op, R, W, eng="dve"):
        P.add(eng, lambda e: e.tensor_tensor(out=out, in0=in0, in1=in1, op=op), bl(R), bl(W))

    def TS(out, in0, s1, s2, op0, op1, R, W, eng="dve"):
        if s2 is None:
            P.add(eng, lambda e: e.tensor_scalar(out=out, in0=in0, scalar1=s1, scalar2=None, op0=op0), bl(R), bl(W))
        else:
            P.add(eng, lambda e: e.tensor_scalar(out=out, in0=in0, scalar1=s1, scalar2=s2, op0=op0, op1=op1), bl(R), bl(W))

    def STT(out, in0, sc, in1, op0, op1, R, W):
        P.add("dve", lambda e: e.scalar_tensor_tensor(out=out, in0=in0, scalar=sc, in1=in1, op0=op0, op1=op1), bl(R), bl(W))

    def CP(out, in_, R, W, eng="dve"):
        if eng == "act":
            P.add(eng, lambda e: e.activation(out=out, in_=in_, func=AF.Copy), bl(R), bl(W))
        else:
            P.add(eng, lambda e: e.tensor_copy(out=out, in_=in_), bl(R), bl(W))

    def RCP(out, in_, R, W):
        P.add("dve", lambda e: e.reciprocal(out=out, in_=in_), bl(R), bl(W))

    def MM(out, lhsT, rhs, start, stop, R, W):
        P.add("pe", lambda e: e.matmul(out, lhsT=lhsT, rhs=rhs, start=start, stop=stop), bl(R), bl(W))

    def DMA(q, out, in_, R, W, guard=True, slow=False):
        if slow:
            P.add(q, lambda e: e.dma_start(out=out, in_=in_, allow_slow_non_contiguous=True), bl(R), bl(W), dma=True, guard=guard)
        else:
            P.add(q, lambda e: e.dma_start(out=out, in_=in_), bl(R), bl(W), dma=True, guard=guard)

    def MEMSET(ap, val, W, eng="dve"):
        import os as _os
        if _os.environ.get("SKIP_POOLMS") and eng == "pool":
            eng = "dve"
        P.add(eng, lambda e: e.memset(ap, val), [], bl(W))

    cst = alloc([128, NCST], F32, "cst")
    cbf = alloc([128, 384], BF16, "cbf")
    pk = [alloc([128, NPK], F32, "pk%d" % l) for l in range(DEPTH)]
    lw = [[alloc([128, 4, 128], BF16, "lwa%d" % l), alloc([128, 4, 128], BF16, "lwx%d" % l)] for l in range(DEPTH)]
    c1 = [alloc([128, 4], F32, "c1_%d" % l) for l in range(DEPTH)]
    nalog = [alloc([128, 4], F32, "nalog%d" % l) for l in range(DEPTH)]
    xT = [alloc([128, TP], F32, "xT%d" % k) for k in range(16)]
    hT = [alloc([128, TP], BF16, "hT%d" % k) for k in range(16)]
    oT = [alloc([128, TP], BF16, "oT%d" % k) for k in range(16)]
    NW = 2
    wb = [alloc([128, 16, 512], BF16, "wb%d" % i) for i in range(NW)]
    wsm = alloc([128, 16, 16], BF16, "wsm")
    rstd = alloc([128, TP], F32, "rstd")
    sqs = [alloc([128, TP], BF16, "sq%d" % i) for i in range(2)]
    tiny = alloc([128, 64], F32, "tiny")
    ffr = alloc([128, 4, 8], F32, "ffr")
    junk = alloc([128, 128], F32, "junk")
    sqg = [alloc([128, TP], BF16, "sqg%d" % i) for i in range(2)]

    DMA("sp", cst[:], cst_d, [], [cst], guard=False)
    CP(cbf[:, 0:384], cst[:, 0:384], [cst], [cbf])
    ident_f = cst.t[:, C_ID:C_ID + 128]
    tri_f = cst.t[:, C_TRI:C_TRI + 128]
    ones_f = cst.t[:, C_ONE:C_ONE + 128]
    mls_f = cst.t[:, C_MLS:C_MLS + 128]
    ident_b = cbf.t[:, 0:128]
    tri_b = cbf.t[:, 128:256]
    ones_b = cbf.t[:, 256:384]
    for l in range(DEPTH):
        DMA("sp", pk[l][:], pk_d[l], [], [pk[l]], guard=False)
        DMA("pool", lw[l][0][:], lwa[l].rearrange("n c d -> c n d"), [], [lw[l][0]], guard=False)
        DMA("pool", lw[l][1][:], lwx[l].rearrange("n c d -> c n d"), [], [lw[l][1]], guard=False)
        import os as _os
        if _os.environ.get("SKIP_ACT"):
            continue
        o_ = 16 * l
        ACT(tiny[:, o_:o_ + 4], pk[l][:, PK_LAM:PK_LAM + 4], AF.Exp, [pk[l]], [tiny], scale=-1.0)
        ACT(tiny[:, o_ + 4:o_ + 8], tiny[:, o_:o_ + 4], AF.Ln, [tiny], [tiny], bias=1.0)
        ACT(c1[l][:], tiny[:, o_ + 4:o_ + 8], AF.Copy, [tiny], [c1[l]], scale=-8.0)
        ACT(tiny[:, o_ + 8:o_ + 12], pk[l][:, PK_ALOG:PK_ALOG + 4], AF.Exp, [pk[l]], [tiny])
        ACT(nalog[l][:], tiny[:, o_ + 8:o_ + 12], AF.Copy, [tiny], [nalog[l]], scale=-1.0)

    class Seq:
        pass

    def mkseq(name, is_prompt, sidx):
        s = Seq()
        s.name = name
        s.prompt = is_prompt
        s.sidx = sidx
        s.S = [alloc([128, 4, 128], F32, name + "S") for _ in range(DEPTH)]
        s.gch = [alloc([128, 12, 3], F32, name + "gch") for _ in range(DEPTH)]
        s.lh = [alloc([128, 4], F32, name + "lh") for _ in range(DEPTH)]
        s.lch = [alloc([128, 4, 3], F32, name + "lch") for _ in range(DEPTH)]
        s.fch = [alloc([128, 44, 2], F32, name + "fch") for _ in range(DEPTH)]
        s.Fb = [alloc([128, 8], F32, name + "Fb") for _ in range(DEPTH)]
        if is_prompt:
            s.Fk = [alloc([128, SEQ // 128, 8], F32, name + "Fk") for _ in range(DEPTH)]
            s.Fref = [alloc([128, SEQ // 128, 8], F32, name + "Fref") for _ in range(DEPTH)]
        return s

    seq_p = mkseq("p", True, 0)
    seq_s = [mkseq("s%d" % i, False, i) for i in range(2)]
    for l in range(DEPTH):
        for t in (seq_p.S[l], seq_p.gch[l], seq_p.lh[l], seq_p.lch[l], seq_p.fch[l], seq_p.Fb[l]):
            MEMSET(t[:], 0.0, [t], eng="pool")
        for i, s in enumerate(seq_s):
            DMA("sp", s.S[l][:], sg[l, i].rearrange("h k v -> k h v"), [], [s.S[l]], guard=False)
            DMA("sp", s.gch[l][:], sgc[l, i].rearrange("p (c j) -> p c j", j=3), [], [s.gch[l]], guard=False)
            DMA("sp", s.lh[l][:], sl[l, i], [], [s.lh[l]], guard=False)
            DMA("sp", s.lch[l][:], slc[l, i].rearrange("p (c j) -> p c j", j=3), [], [s.lch[l]], guard=False)
            DMA("sp", s.fch[l][:], sfc[l, i].rearrange("p (c j) -> p c j", j=2), [], [s.fch[l]], guard=False)
            MEMSET(s.Fb[l][:], 0.0, [s.Fb[l]], eng="pool")

    region0 = cur[0]
    regbuf = Buf("region")
    P.guard = regbuf
    kvK = [[Buf("kvK%d_%d" % (l, i)) for i in range(NT)] for l in range(DEPTH)]
    kvV = [[Buf("kvV%d_%d" % (l, i)) for i in range(NT)] for l in range(DEPTH)]

    def fence(keep=None):
        cur[0] = region0 if keep is None else keep
        P.nfence += 1
        import os as _os
        if P.nfence > int(_os.environ.get("STOPF", "100000")):
            P.stopped = True
            return
        P.add("dve", lambda e: e.memset(tiny[:, 60:61], 0.0), [], [regbuf], guard=False)

    wrr = [0]

    def wload(src_rows_ap, nk, ncols):
        w = wb[wrr[0] % NW]
        wrr[0] += 1
        DMA("pool", w[:, 0:nk, 0:ncols], src_rows_ap.rearrange("(k p) n -> p k n", p=128), [], [w], guard=False)
        return w

    psrr = [0]

    def bank(group=(0, 1, 2, 3)):
        b = ps[group[psrr[0] % len(group)]]
        psrr[0] += 1
        return b

    def rmsnorm_to_hT(T, gcol, l):
        pb = bank()
        for k in range(16):
            sq = sqs[k % 2]
            ACT(sq[:, :T], xT[k][:, :T], AF.Square, [xT[k]], [sq])
            MM(pb[:, :T], ones_b, sq[:, :T], k == 0, k == 15, [sq, cbf], [pb])
        ACT(rstd[:, :T], pb[:, :T], AF.Sqrt, [pb], [rstd], bias=EPS, scale=1.0 / D)
        RCP(rstd[:, :T], rstd[:, :T], [rstd], [rstd])
        for k in range(16):
            STT(hT[k][:, :T], xT[k][:, :T], pk[l][:, gcol + k:gcol + k + 1], rstd[:, :T], ALU.mult, ALU.mult,
                [xT[k], rstd, pk[l]], [hT[k]])

    def fm_cols(w, T, ncc, evac, src=None):
        src = src or hT
        for cc in range(ncc):
            pb = bank()
            for k in range(16):
                MM(pb[:, :T], w[:, k, cc * 128:(cc + 1) * 128], src[k][:, :T], k == 0, k == 15, [w, src[k]], [pb])
            evac(cc, pb)

    def tm_cols(w, cols, ncols, c, evac):
        pb = bank()
        for k in range(16):
            MM(pb[:c, :ncols], hT[k][:, cols], w[:, k, 0:ncols], k == 0, k == 15, [w, hT[k]], [pb])
        evac(pb)

    def run_tile(segs, T, c, l, src_loader, last_layer, y_writer, is_last_tile):
        nseg = len(segs)
        L = segs[0]["L"]
        nsub = L // c
        win = w_in[l]
        pkl = pk[l]
        if src_loader is not None:
            fence()
            src_loader()
        fence()
        rmsnorm_to_hT(T, PK_GMIX, l)
        fence()
        qkvT = [alloc([128, 4, T], BF16, "gq"), alloc([128, 4, T], BF16, "gk"), alloc([128, 4, T], BF16, "gv")]
        zg = alloc([128, nseg * nsub, 512], BF16, "zg")
        bt = alloc([128, nseg * nsub, 4], F32, "beta")
        gg = alloc([128, nseg * nsub, 4], F32, "gg")
        tg = alloc([128, 16], F32, "tg")
        gdn_mark = cur[0]
        raws = [alloc([128, nseg, 3 + L], F32, "raw%d" % i) for i in range(2)]
        cvs = [alloc([128, nseg, L], F32, "cv%d" % i) for i in range(2)]
        sil = [alloc([128, T], F32, "sil%d" % i) for i in range(2)]
        rq = alloc([128, T], F32, "rq")
        for grp in range(3):
            w = wload(win[:, grp * 512:(grp + 1) * 512], 16, 512)

            def ev(cc, pb, grp=grp):
                ch = grp * 4 + cc
                raw = raws[ch % 2]
                cv_ = cvs[ch % 2]
                sl_ = sil[ch % 2]
                for si, sgm in enumerate(segs):
                    gch = sgm["seq"].gch[l]
                    CP(raw[:, si, 0:3], gch[:, ch, :], [gch], [raw], eng="pool")
                P.add("act", lambda e: e.activation(out=raw[:, :, 3:3 + L], in_=pb[:, :T].rearrange("p (s l) -> p s l", s=nseg), func=AF.Copy),
                      bl([pb]) + [regbuf], bl([raw]), guard=False)
                for si, sgm in enumerate(segs):
                    gch = sgm["seq"].gch[l]
                    CP(gch[:, ch, :], raw[:, si, L:L + 3], [raw], [gch], eng="pool")
                cw = PK_GCW + ch * 4
                TS(cv_[:], raw[:, :, 0:L], pkl[:, cw:cw + 1], None, ALU.mult, None, [raw, pkl], [cv_])
                for j in range(1, 4):
                    STT(cv_[:], raw[:, :, j:j + L], pkl[:, cw + j:cw + j + 1], cv_[:], ALU.mult, ALU.add, [raw, pkl, cv_], [cv_])
                dst = qkvT[grp]
                if grp == 2:
                    ACT(dst[:, cc, :], cv_[:].rearrange("p s l -> p (s l)"), AF.Silu, [cv_], [dst])
                else:
                    ACT(sl_[:, :T], cv_[:].rearrange("p s l -> p (s l)"), AF.Silu, [cv_], [sl_])
                    sq = sqg[ch % 2]
                    ACT(sq[:, :T], sl_[:, :T], AF.Square, [sl_], [sq])
                    pb2 = bank((4, 5))
                    MM(pb2[:, :T], ones_b, sq[:, :T], True, True, [sq, cbf], [pb2])
                    ACT(rq[:, :T], pb2[:, :T], AF.Sqrt, [pb2], [rq], bias=EPS, scale=1.0)
                    RCP(rq[:, :T], rq[:, :T], [rq], [rq])
                    STT(dst[:, cc, :], sl_[:, :T], (128.0 ** -0.5) if grp == 0 else 1.0, rq[:, :T], ALU.mult, ALU.mult, [sl_, rq], [dst])
            fm_cols(w, T, 4, ev)
        w = wload(win[:, 1536:2048], 16, 512)
        DMA("pool", wsm[:, :, 0:8], win[:, 2048:2056].rearrange("(k p) n -> p k n", p=128), [], [wsm], guard=False)
        DMA("pool", wsm[:, :, 8:16], win[:, 5128:5136].rearrange("(k p) n -> p k n", p=128), [], [wsm], guard=False)
        for si, sgm in enumerate(segs):
            for s in range(nsub):
                cols = slice(sgm["col0"] + s * c, sgm["col0"] + (s + 1) * c)
                bi = si * nsub + s

                def evz(pb, bi=bi):
                    ACT(zg[:c, bi, :], pb[:c, :], AF.Silu, [pb], [zg])
                    TT(zg[:c, bi, :], zg[:c, bi, :], pkl[:c, PK_GNG4:PK_GNG4 + 512], ALU.mult, [zg, pkl], [zg])
                tm_cols(w, cols, 512, c, evz)
                pbs = bank()
                for k in range(16):
                    MM(pbs[:c, :16], hT[k][:, cols], wsm[:, k, :], k == 0, k == 15, [wsm, hT[k]], [pbs])
                ACT(bt[:c, bi, :], pbs[:c, 0:4], AF.Sigmoid, [pbs], [bt])
                TT(tg[:c, 0:4], pbs[:c, 4:8], pkl[:c, PK_DTB:PK_DTB + 4], ALU.add, [pbs, pkl], [tg])
                ACT(tg[:c, 4:8], tg[:c, 0:4], AF.Exp, [tg], [tg])
                ACT(tg[:c, 8:12], tg[:c, 4:8], AF.Ln, [tg], [tg], bias=1.0)
                TT(gg[:c, bi, :], tg[:c, 8:12], nalog[l][:c, :], ALU.mult, [tg, nalog[l]], [gg])
                CP(ffr[:c, bi, :], pbs[:c, 8:16], [pbs], [ffr])
        fence(keep=gdn_mark)
        c4 = 4 * c
        gmat = alloc([128, c4], F32, "gmat")
        Gcol = alloc([128, 4], F32, "Gcol")
        Gb = alloc([128, c4], F32, "Gb")
        EGb = alloc([128, c4], F32, "EGb")
        tmpm = [alloc([128, c], F32, "tmpm%d" % i) for i in range(2)]
        DLs = alloc([128, c4], F32, "DLs")
        DUm = alloc([128, c4], F32, "DUm")
        Lm = [alloc([128, c4], F32, "Lm%d" % i) for i in range(2)]
        Um = [alloc([128, c4], F32, "Um%d" % i) for i in range(2)]
        Pm = alloc([128, c4], F32, "Pm")
        MTm = alloc([128, c4], F32, "MTm")
        kbeg = alloc([128, 4, 128], F32, "kbeg")
        kd = alloc([128, 4, 128], F32, "kd")
        vb = alloc([128, 4, 128], F32, "vb")
        nwT = alloc([128, c4], F32, "nwT")
        delta = alloc([128, 4, 128], F32, "delta")
        qgT = alloc([128, c4], F32, "qgT")
        sm = alloc([128, 32], F32, "sm")
        ytok = alloc([128, 512], BF16, "ytok")
        nstep = int(np.log2(c)) - 1
        for si, sgm in enumerate(segs):
            S = sgm["seq"].S[l]
            for s in range(nsub):
                t0 = sgm["col0"] + s * c
                cols = slice(t0, t0 + c)
                bi = si * nsub + s
                qT_, kT_, vT_ = qkvT
                pk_tok = ps[4]
                pv_tok = ps[5]
                for h in range(4):
                    MM(pk_tok[:c, h * 128:(h + 1) * 128], kT_[:, h, cols], ident_b, True, True, [kT_, cbf], [pk_tok])
                    MM(pv_tok[:c, h * 128:(h + 1) * 128], vT_[:, h, cols], ident_b, True, True, [vT_, cbf], [pv_tok])
                pg = bank()
                MM(pg[:c, 0:4], tri_f[:c, :c], gg[:c, bi, :], True, True, [cst, gg], [pg])
                CP(Gcol[:c, :], pg[:c, 0:4], [pg], [Gcol])
                for h in range(4):
                    TS(gmat[:c, h * c:(h + 1) * c], tri_f[:c, :c], gg[:c, bi, h:h + 1], None, ALU.mult, None, [cst, gg], [gmat])
                pgb = bank()
                MM(pgb[:, :c4], ones_f[:c, :], gmat[:c, :c4], True, True, [cst, gmat], [pgb])
                CP(Gb[:, :c4], pgb[:, :c4], [pgb], [Gb])
                ACT(EGb[:, :c4], pgb[:, :c4], AF.Exp, [pgb], [EGb])
                ACT(sm[:c, 0:4], Gcol[:c, :], AF.Exp, [Gcol], [sm])
                TT(sm[:c, 4:8], sm[:c, 0:4], bt[:c, bi, :], ALU.mult, [sm, bt], [sm])
                for h in range(4):
                    ACT(sm[:c, 8 + h:9 + h], Gcol[:c, h:h + 1], AF.Exp, [Gcol, Gb], [sm], scale=-1.0, bias=Gb[:c, h * c + c - 1:h * c + c])
                for h in range(4):
                    TS(kbeg[:c, h, :], pk_tok[:c, h * 128:(h + 1) * 128], sm[:c, 4 + h:5 + h], None, ALU.mult, None, [pk_tok, sm], [kbeg])
                    TS(kd[:c, h, :], pk_tok[:c, h * 128:(h + 1) * 128], sm[:c, 8 + h:9 + h], None, ALU.mult, None, [pk_tok, sm], [kd])
                    TS(vb[:c, h, :], pv_tok[:c, h * 128:(h + 1) * 128], bt[:c, bi, h:h + 1], None, ALU.mult, None, [pv_tok, bt], [vb])
                for h in range(4):
                    hc = slice(h * c, (h + 1) * c)
                    tm = tmpm[0]
                    TS(tm[:c, :], Gb[:c, hc], Gcol[:c, h:h + 1], 0.0, ALU.subtract, ALU.max, [Gb, Gcol], [tm])
                    ACT(tm[:c, :], tm[:c, :], AF.Exp, [tm], [tm], scale=-1.0)
                    STT(DLs[:c, hc], tm[:c, :], bt[:c, bi, h:h + 1], mls_f[:c, :c], ALU.mult, ALU.mult, [tm, bt, cst], [DLs])
                    tm2 = tmpm[1]
                    TS(tm2[:c, :], Gb[:c, hc], Gcol[:c, h:h + 1], 0.0, ALU.subtract, ALU.min, [Gb, Gcol], [tm2])
                    ACT(tm2[:c, :], tm2[:c, :], AF.Exp, [tm2], [tm2])
                    TT(DUm[:c, hc], tm2[:c, :], tri_f[:c, :c], ALU.mult, [tm2, cst], [DUm])
                pkk = bank()
                pqk = bank()
                for h in range(4):
                    hc = slice(h * c, (h + 1) * c)
                    MM(pkk[:c, hc], kT_[:, h, cols], kT_[:, h, cols], True, True, [kT_], [pkk])
                    MM(pqk[:c, hc], kT_[:, h, cols], qT_[:, h, cols], True, True, [kT_, qT_], [pqk])
                L0 = Lm[0]
                TT(L0[:c, :c4], pkk[:c, :c4], DLs[:c, :c4], ALU.mult, [pkk, DLs], [L0])
                TT(MTm[:c, :c4], pqk[:c, :c4], DUm[:c, :c4], ALU.mult, [pqk, DUm], [MTm])
                pB = bank()
                for h in range(4):
                    hc = slice(h * c, (h + 1) * c)
                    MM(pB[:c, hc], L0[:c, hc], ident_f[:c, :c], True, True, [L0, cst], [pB])
                U0 = Um[0]
                CP(U0[:c, :c4], pB[:c, :c4], [pB], [U0], eng="act")
                for h in range(4):
                    hc = slice(h * c, (h + 1) * c)
                    TT(Pm[:c, hc], ident_f[:c, :c], pB[:c, hc], ALU.subtract, [cst, pB], [Pm])
                Lc, Uc = L0, U0
                for it in range(1, nstep + 1):
                    Ln_, Un_ = Lm[it % 2], Um[it % 2]
                    pU = bank()
                    pL = bank()
                    for h in range(4):
                        hc = slice(h * c, (h + 1) * c)
                        MM(pL[:c, hc], Uc[:c, hc], Lc[:c, hc], True, True, [Uc, Lc], [pL])
                        if it < nstep:
                            MM(pU[:c, hc], Lc[:c, hc], Uc[:c, hc], True, True, [Uc, Lc], [pU])
                    CP(Ln_[:c, :c4], pL[:c, :c4], [pL], [Ln_])
                    if it < nstep:
                        CP(Un_[:c, :c4], pU[:c, :c4], [pU], [Un_], eng="act")
                    pP = bank()
                    for h in range(4):
                        hc = slice(h * c, (h + 1) * c)
                        MM(pP[:c, hc], Ln_[:c, hc], Pm[:c, hc], True, True, [Ln_, Pm], [pP])
                    TT(Pm[:c, :c4], Pm[:c, :c4], pP[:c, :c4], ALU.add, [Pm, pP], [Pm])
                    Lc, Uc = Ln_, Un_
                pW = bank()
                for h in range(4):
                    hc = slice(h * c, (h + 1) * c)
                    MM(pW[:, hc], kbeg[:c, h, :], Pm[:c, hc], True, True, [kbeg, Pm], [pW])
                ACT(nwT[:, :c4], pW[:, :c4], AF.Copy, [pW], [nwT], scale=-1.0)
                pD = bank()
                for h in range(4):
                    hc = slice(h * c, (h + 1) * c)
                    MM(pD[:c, h * 128:(h + 1) * 128], Pm[:c, hc], vb[:c, h, :], True, False, [Pm, vb], [pD])
                    MM(pD[:c, h * 128:(h + 1) * 128], nwT[:, hc], S[:, h, :], False, True, [nwT, S], [pD])
                CP(delta[:c, :, :], pD[:c, :].rearrange("p (h v) -> p h v", h=4), [pD], [delta])
                for h in range(4):
                    hc = slice(h * c, (h + 1) * c)
                    TT(qgT[:, hc], qT_[:, h, cols], EGb[:, hc], ALU.mult, [qT_, EGb], [qgT])
                pO = bank()
                for h in range(4):
                    hc = slice(h * c, (h + 1) * c)
                    MM(pO[:c, h * 128:(h + 1) * 128], qgT[:, hc], S[:, h, :], True, False, [qgT, S], [pO])
                    MM(pO[:c, h * 128:(h + 1) * 128], MTm[:c, hc], delta[:c, h, :], False, True, [MTm, delta], [pO])
                pS = bank()
                for h in range(4):
                    MM(pS[:, h * 128:(h + 1) * 128], kd[:c, h, :], delta[:c, h, :], True, True, [kd, delta], [pS])
                for h in range(4):
                    STT(S[:, h, :], S[:, h, :], EGb[:, h * c + c - 1:h * c + c], pS[:, h * 128:(h + 1) * 128], ALU.mult, ALU.add, [S, EGb, pS], [S])
                for h in range(4):
                    ACT(junk[:c, :], pO[:c, h * 128:(h + 1) * 128], AF.Square, [pO], [junk, sm], accum_out=sm[:c, 12 + h:13 + h])
                ACT(sm[:c, 16:20], sm[:c, 12:16], AF.Sqrt, [sm], [sm], bias=EPS, scale=1.0 / 128)
                RCP(sm[:c, 16:20], sm[:c, 16:20], [sm], [sm])
                for h in range(4):
                    STT(ytok[:c, h * 128:(h + 1) * 128], pO[:c, h * 128:(h + 1) * 128], sm[:c, 16 + h:17 + h], zg[:c, bi, h * 128:(h + 1) * 128],
                        ALU.mult, ALU.mult, [pO, sm, zg], [ytok])
                pT = bank()
                for h in range(4):
                    MM(pT[:, h * c:(h + 1) * c], ytok[:c, h * 128:(h + 1) * 128], ident_b[:c, :c], True, True, [ytok, cbf], [pT])
                for h in range(4):
                    CP(oT[h][:, cols], pT[:, h * c:(h + 1) * c], [pT], [oT[h]], eng="act" if h % 2 else "dve")
        if is_last_tile:
            for si, sgm in enumerate(segs):
                sq_ = sgm["seq"]
                DMA("sp", sgm["gs_out"][l].rearrange("h k v -> k h v"), sq_.S[l][:], [sq_.S[l]], [])
                DMA("sp", sgm["gc_out"][l].rearrange("p (c j) -> p c j", j=3), sq_.gch[l][:], [sq_.gch[l]], [])

        fence()
        QT = alloc([128, 8, T], BF16, "QT")
        KT = alloc([128, 8, T], BF16, "KT")
        Va = alloc([128, nseg * nsub, 8, 129], BF16, "Va")
        stg = [alloc([128, 512], F32, "stg%d" % i) for i in range(3)]
        strr = [0]
        MEMSET(Va[:, :, :, 128:129], 1.0, [Va], eng="pool")
        for half in range(2):
            w = wload(win[:, 2056 + half * 512:2056 + (half + 1) * 512], 16, 512)
            fm_cols(w, T, 4, lambda cc, pb, half=half: ACT(QT[:, half * 4 + cc, :], pb[:, :T], AF.Copy, [pb], [QT], scale=128.0 ** -0.5))
        for half in range(2):
            w = wload(win[:, 3080 + half * 512:3080 + (half + 1) * 512], 16, 512)
            fm_cols(w, T, 4, lambda cc, pb, half=half: CP(KT[:, half * 4 + cc, :], pb[:, :T], [pb], [KT]))
            for si, sgm in enumerate(segs):
                for s in range(nsub):
                    cols = slice(sgm["col0"] + s * c, sgm["col0"] + (s + 1) * c)

                    def evk(pb, sgm=sgm, s=s, half=half):
                        st = stg[strr[0] % 3]
                        strr[0] += 1
                        ACT(st[:c, :], pb[:c, :], AF.Copy, [pb], [st])
                        DMA("sp", sgm["kout"][l][s * c:(s + 1) * c, half * 512:(half + 1) * 512], st[:c, :], [st], [sgm["kbuf"][l]])
                    tm_cols(w, cols, 512, c, evk)
        for half in range(2):
            w = wload(win[:, 4104 + half * 512:4104 + (half + 1) * 512], 16, 512)
            for si, sgm in enumerate(segs):
                for s in range(nsub):
                    cols = slice(sgm["col0"] + s * c, sgm["col0"] + (s + 1) * c)
                    bi = si * nsub + s

                    def evv(pb, sgm=sgm, s=s, half=half, bi=bi):
                        st = stg[strr[0] % 3]
                        strr[0] += 1
                        ACT(st[:c, :], pb[:c, :], AF.Copy, [pb], [st])
                        CP(Va[:c, bi, half * 4:(half + 1) * 4, 0:128], pb[:c, :].rearrange("p (h d) -> p h d", h=4), [pb], [Va])
                        DMA("sp", sgm["vout"][l][s * c:(s + 1) * c, half * 512:(half + 1) * 512], st[:c, :], [st], [sgm["vbuf"][l]])
                    tm_cols(w, cols, 512, c, evv)
        lf = alloc([128, nseg * nsub, 8], F32, "lf")
        t8 = alloc([128, 16], F32, "t8")
        for si, sgm in enumerate(segs):
            sq_ = sgm["seq"]
            Fb = sq_.Fb[l]
            if not sq_.prompt:
                nblk = PB + 1
                sq_.Fk_l = alloc([128, nblk, 8], F32, "sFk")
                sq_.Fref_l = alloc([128, nblk, 8], F32, "sFref")
                clg = alloc([128, PB, 8], F32, "clg")
                for q4 in range(0, PB, 8):
                    n = min(8, PB - q4)
                    DMA("sp", clg[:, q4:q4 + n, :], cl[l, sq_.sidx, q4 * 128:(q4 + n) * 128, :].rearrange("(b p) h -> p b h", p=128), [], [clg])
                for kb in range(PB):
                    pf = bank()
                    MM(pf[:, 0:8], tri_f, clg[:, kb, :], True, True, [cst, clg], [pf])
                    MM(pf[:, 8:16], ones_f, clg[:, kb, :], True, True, [cst, clg], [pf])
                    TT(sq_.Fk_l[:, kb, :], pf[:, 0:8], Fb[:, :], ALU.add, [pf, Fb], [sq_.Fk_l])
                    TT(Fb[:, :], Fb[:, :], pf[:, 8:16], ALU.add, [Fb, pf], [Fb])
                Fk, Fref = sq_.Fk_l, sq_.Fref_l
            else:
                Fk, Fref = sq_.Fk[l], sq_.Fref[l]
            for s in range(nsub):
                bi = si * nsub + s
                blk = sgm["blk0"] + s
                TT(t8[:c, 0:8], ffr[:c, bi, :], pkl[:c, PK_FB:PK_FB + 8], ALU.add, [ffr, pkl], [t8])
                ACT(t8[:c, 8:16], t8[:c, 0:8], AF.Exp, [t8], [t8], scale=-1.0)
                ACT(t8[:c, 0:8], t8[:c, 8:16], AF.Ln, [t8], [t8], bias=1.0)
                TS(lf[:c, bi, :], t8[:c, 0:8], -1.0, None, ALU.mult, None, [t8], [lf])
                DMA("sp", sgm["lout"][l][s * c:(s + 1) * c, :], lf[:c, bi, :], [lf], [])
                pf = bank()
                MM(pf[:c, 0:8], tri_f[:c, :c], lf[:c, bi, :], True, True, [cst, lf], [pf])
                MM(pf[:, 8:16], ones_f[:c, :], lf[:c, bi, :], True, True, [cst, lf], [pf])
                CP(Fref[:, blk, :], Fb[:, :], [Fb], [Fref])
                TT(Fk[:c, blk, :], pf[:c, 0:8], Fb[:c, :], ALU.add, [pf, Fb], [Fk])
                TT(Fb[:, :], Fb[:, :], pf[:, 8:16], ALU.add, [Fb, pf], [Fb])
        Kh = [alloc([128, 4, 128], BF16, "Kh%d" % i) for i in range(2)]
        Vh = [alloc([128, 4, 129], BF16, "Vh%d" % i) for i in range(2)]
        for v_ in Vh:
            MEMSET(v_[:, :, 128:129], 1.0, [v_], eng="pool")
        KTh = [alloc([128, 512], BF16, "KTh%d" % i) for i in range(2)]
        PT = [alloc([128, 512], BF16, "PT%d" % i) for i in range(3)]
        bias4 = [alloc([128, 4], F32, "bias4_%d" % i) for i in range(4)]
        yf = alloc([128, nseg * nsub, 1024], BF16, "yf")
        fsm = alloc([128, 16], F32, "fsm")
        hrr = [0]
        prr = [0]
        brr = [0]
        for si, sgm in enumerate(segs):
            sq_ = sgm["seq"]
            if sq_.prompt:
                Fk, Fref = sq_.Fk[l], sq_.Fref[l]
                nhist = sgm["blk0"] // 4
            else:
                Fk, Fref = sq_.Fk_l, sq_.Fref_l
                nhist = PB // 4
            q0 = sgm["col0"]
            qb0 = sgm["blk0"]
            for h in range(8):
                started = [False] * nsub
                Oacc = [ps[4 + qs] for qs in range(nsub)]
                for j in range(nhist):
                    kh = Kh[hrr[0] % 2]
                    vh = Vh[hrr[0] % 2]
                    kth = KTh[hrr[0] % 2]
                    hrr[0] += 1
                    if sq_.prompt:
                        ksrc = fk_p[l, j * 512:(j + 1) * 512, h * 128:(h + 1) * 128]
                        vsrc = fv_p[l, j * 512:(j + 1) * 512, h * 128:(h + 1) * 128]
                        kdep, vdep = [kvK[l][j]], [kvV[l][j]]
                    else:
                        ksrc = ck[l, sq_.sidx, j * 512:(j + 1) * 512, h * 128:(h + 1) * 128]
                        vsrc = cv[l, sq_.sidx, j * 512:(j + 1) * 512, h * 128:(h + 1) * 128]
                        kdep, vdep = [], []
                    DMA("pool", kh[:, :, :], ksrc.rearrange("(b p) d -> p b d", p=128), kdep, [kh])
                    DMA("pool", vh[:, :, 0:128], vsrc.rearrange("(b p) d -> p b d", p=128), vdep, [vh])
                    ptr = ps[3]
                    for kb in range(4):
                        MM(ptr[:, kb * 128:(kb + 1) * 128], kh[:, kb, :], ident_b, True, True, [kh, cbf], [ptr])
                    CP(kth[:, :], ptr[:, :], [ptr], [kth])
                    for kb in range(4):
                        kblk = j * 4 + kb
                        pst = bank((0, 1, 2))
                        MM(pst[:, :L], kth[:, kb * 128:(kb + 1) * 128], QT[:, h, q0:q0 + L], True, True, [kth, QT], [pst])
                        b4 = bias4[brr[0] % 4]
                        brr[0] += 1
                        TS(b4[:, 0:nsub], Fref[:, qb0:qb0 + nsub, h], Fk[:, kblk, h:h + 1], None, ALU.subtract, None, [Fref, Fk], [b4])
                        pt = PT[prr[0] % 3]
                        prr[0] += 1
                        for qs in range(nsub):
                            ACT(pt[:, qs * c:(qs + 1) * c], pst[:, qs * c:(qs + 1) * c], AF.Exp, [pst, b4], [pt], bias=b4[:, qs:qs + 1])
                        for qs in range(nsub):
                            MM(Oacc[qs][:c, 0:129], pt[:, qs * c:(qs + 1) * c], vh[:, kb, :], not started[qs], False, [pt, vh], [Oacc[qs]])
                            started[qs] = True
                for kb in range(nsub):
                    bi = si * nsub + kb
                    kblk = qb0 + kb
                    nq = nsub - kb
                    pst = bank((0, 1, 2))
                    MM(pst[:c, :nq * c], KT[:, h, q0 + kb * c:q0 + (kb + 1) * c], QT[:, h, q0 + kb * c:q0 + L], True, True, [KT, QT], [pst])
                    b4 = bias4[brr[0] % 4]
                    brr[0] += 1
                    TS(b4[:c, 0:nq], Fref[:c, qb0 + kb:qb0 + nsub, h], Fk[:c, kblk, h:h + 1], None, ALU.subtract, None, [Fref, Fk], [b4])
                    pt = PT[prr[0] % 3]
                    prr[0] += 1
                    for qi in range(nq):
                        ACT(pt[:c, qi * c:(qi + 1) * c], pst[:c, qi * c:(qi + 1) * c], AF.Exp, [pst, b4], [pt], bias=b4[:c, qi:qi + 1])
                    TT(pt[:c, 0:c], pt[:c, 0:c], tri_b[:c, :c], ALU.mult, [pt, cbf], [pt])
                    for qi in range(nq):
                        qs = kb + qi
                        MM(Oacc[qs][:c, 0:129], pt[:c, qi * c:(qi + 1) * c], Va[:c, bi, h, :], not started[qs], qs == kb, [pt, Va], [Oacc[qs]])
                        started[qs] = True
                for qs in range(nsub):
                    bi = si * nsub + qs
                    O = Oacc[qs]
                    RCP(fsm[:c, 0:1], O[:c, 128:129], [O], [fsm])
                    ACT(junk[:c, :], O[:c, 0:128], AF.Square, [O, fsm], [junk, fsm], scale=fsm[:c, 0:1], accum_out=fsm[:c, 1:2])
                    ACT(fsm[:c, 2:3], fsm[:c, 1:2], AF.Sqrt, [fsm], [fsm], bias=EPS, scale=1.0 / 128)
                    RCP(fsm[:c, 2:3], fsm[:c, 2:3], [fsm], [fsm])
                    TT(fsm[:c, 3:4], fsm[:c, 2:3], fsm[:c, 0:1], ALU.mult, [fsm], [fsm])
                    STT(yf[:c, bi, h * 128:(h + 1) * 128], O[:c, 0:128], fsm[:c, 3:4], pkl[:c, PK_FNG + h * 128:PK_FNG + (h + 1) * 128],
                        ALU.mult, ALU.mult, [O, fsm, pkl], [yf])
            for qs in range(nsub):
                bi = si * nsub + qs
                cols = slice(q0 + qs * c, q0 + (qs + 1) * c)
                for hg in range(2):
                    pT = bank((0, 1, 2, 3))
                    for hh in range(4):
                        h = hg * 4 + hh
                        MM(pT[:, hh * c:(hh + 1) * c], yf[:c, bi, h * 128:(h + 1) * 128], ident_b[:c, :c], True, True, [yf, cbf], [pT])
                    for hh in range(4):
                        h = hg * 4 + hh
                        CP(oT[4 + h][:, cols], pT[:, hh * c:(hh + 1) * c], [pT], [oT[4 + h]], eng="act" if hh % 2 else "dve")

        fence()
        lxr = alloc([128, 4, nseg, 3 + L], F32, "lxr")
        gel = alloc([128, 4, T], BF16, "gel")
        xc = alloc([128, 4, nseg, L], F32, "xc")
        xcb = alloc([128, 4, T], BF16, "xcb")
        lt = [alloc([128, T], F32, "lt%d" % i) for i in range(6)]
        hs = alloc([128, nseg, L], F32, "hs")
        w = wload(win[:, 5136:5648], 16, 512)

        def evlx(cc, pb):
            for si, sgm in enumerate(segs):
                lch = sgm["seq"].lch[l]
                CP(lxr[:, cc, si, 0:3], lch[:, cc, :], [lch], [lxr], eng="pool")
            P.add("act", lambda e: e.activation(out=lxr[:, cc, :, 3:3 + L], in_=pb[:, :T].rearrange("p (s l) -> p s l", s=nseg), func=AF.Copy),
                  bl([pb]) + [regbuf], bl([lxr]), guard=False)
            for si, sgm in enumerate(segs):
                lch = sgm["seq"].lch[l]
                CP(lch[:, cc, :], lxr[:, cc, si, L:L + 3], [lxr], [lch], eng="pool")
        fm_cols(w, T, 4, evlx)
        w = wload(win[:, 5648:6160], 16, 512)

        def evg(cc, pb):
            g1 = lt[cc % 2]
            ACT(g1[:, :T], pb[:, :T], AF.Square, [pb], [g1])
            TS(g1[:, :T], g1[:, :T], 0.044715, 1.0, ALU.mult, ALU.add, [g1], [g1])
            TT(g1[:, :T], g1[:, :T], pb[:, :T], ALU.mult, [g1, pb], [g1])
            ACT(g1[:, :T], g1[:, :T], AF.Sigmoid, [g1], [g1], scale=1.5957691216057308)
            TT(gel[:, cc, :], g1[:, :T], pb[:, :T], ALU.mult, [g1, pb], [gel])
        fm_cols(w, T, 4, evg)
        for n in range(4):
            cw = PK_LCW + n * 4
            TS(xc[:, n, :, :], lxr[:, n, :, 0:L], pkl[:, cw:cw + 1], pkl[:, PK_LCB + n:PK_LCB + n + 1], ALU.mult, ALU.add, [lxr, pkl], [xc])
            for j in range(1, 4):
                STT(xc[:, n, :, :], lxr[:, n, :, j:j + L], pkl[:, cw + j:cw + j + 1], xc[:, n, :, :], ALU.mult, ALU.add, [lxr, pkl, xc], [xc])
            xcn = xc[:, n, :, :].rearrange("p s l -> p (s l)")
            ACT(xcb[:, n, :], xcn, AF.Copy, [xc], [xcb])
            pr = bank()
            MM(pr[:, :T], lw[l][0][:, n, :], xcb[:, n, :], True, True, [lw[l][0], xcb], [pr])
            pi = bank()
            MM(pi[:, :T], lw[l][1][:, n, :], xcb[:, n, :], True, True, [lw[l][1], xcb], [pi])
            r_, i_, a_, a2_, u_, t_ = lt
            ACT(r_[:, :T], pr[:, :T], AF.Sigmoid, [pr, pkl], [r_], bias=pkl[:, PK_LBA + n:PK_LBA + n + 1])
            ACT(i_[:, :T], pi[:, :T], AF.Sigmoid, [pi, pkl], [i_], bias=pkl[:, PK_LBX + n:PK_LBX + n + 1])
            ACT(a_[:, :T], r_[:, :T], AF.Exp, [r_, c1[l]], [a_], scale=c1[l][:, n:n + 1])
            TT(a2_[:, :T], a_[:, :T], a_[:, :T], ALU.mult, [a_], [a2_])
            ACT(a2_[:, :T], a2_[:, :T], AF.Sqrt, [a2_], [a2_], scale=-1.0, bias=1.0)
            TT(u_[:, :T], i_[:, :T], xcn, ALU.mult, [i_, xc], [u_])
            TT(u_[:, :T], u_[:, :T], a2_[:, :T], ALU.mult, [u_, a2_], [u_])
            for si, sgm in enumerate(segs):
                lh = sgm["seq"].lh[l]
                c0 = sgm["col0"]
                P.add("dve", lambda e, si=si, c0=c0, lh=lh, n=n: e.tensor_tensor_scan(out=hs[:, si, :], data0=a_[:, c0:c0 + L], data1=u_[:, c0:c0 + L],
                                                                                initial=lh[:, n:n + 1], op0=ALU.mult, op1=ALU.add),
                      bl([a_, u_, lh]), bl([hs]))
                CP(lh[:, n:n + 1], hs[:, si, L - 1:L], [hs], [lh])
            hsn = hs[:].rearrange("p s l -> p (s l)")
            sq = sqg[0]
            ACT(sq[:, :T], hsn, AF.Square, [hs], [sq])
            pn = bank()
            MM(pn[:, :T], ones_b, sq[:, :T], True, True, [sq, cbf], [pn])
            ACT(t_[:, :T], pn[:, :T], AF.Sqrt, [pn], [t_], bias=EPS, scale=1.0 / 128)
            RCP(t_[:, :T], t_[:, :T], [t_], [t_])
            STT(t_[:, :T], hsn, pkl[:, PK_LNG + n:PK_LNG + n + 1], t_[:, :T], ALU.mult, ALU.mult, [hs, pkl, t_], [t_])
            TT(oT[12 + n][:, :T], t_[:, :T], gel[:, n, :], ALU.mult, [t_, gel], [oT[12 + n]])
        if is_last_tile:
            for si, sgm in enumerate(segs):
                sq_ = sgm["seq"]
                DMA("sp", sgm["lh_out"][l], sq_.lh[l][:], [sq_.lh[l]], [])
                DMA("sp", sgm["lc_out"][l].rearrange("p (c j) -> p c j", j=3), sq_.lch[l][:], [sq_.lch[l]], [])

        fence()
        for g in range(4):
            w = wload(w_out[l][:, g * 512:(g + 1) * 512], 16, 512)
            fm_cols(w, T, 4, lambda cc, pb, g=g: TT(xT[g * 4 + cc][:, :T], xT[g * 4 + cc][:, :T], pb[:, :T], ALU.add, [xT[g * 4 + cc], pb], [xT[g * 4 + cc]]),
                    src=oT)

        fence()
        rmsnorm_to_hT(T, PK_GFFN, l)
        actT = alloc([128, 44, T], BF16, "actT")
        gpr = [alloc([128, nseg, 2 + L], F32, "gpr%d" % i) for i in range(2)]
        gcv = [alloc([128, nseg, L], F32, "gcv%d" % i) for i in range(2)]
        sg_ = [alloc([128, T], F32, "sg%d" % i) for i in range(2)]
        for j in range(11):
            wg = wload(w_up[l][:, j * 512:(j + 1) * 512], 16, 512)
            wv = wload(w_up[l][:, DFF + j * 512:DFF + (j + 1) * 512], 16, 512)
            for cc in range(4):
                m = j * 4 + cc
                pg = bank()
                pv = bank()
                for k in range(16):
                    MM(pg[:, :T], wg[:, k, cc * 128:(cc + 1) * 128], hT[k][:, :T], k == 0, k == 15, [wg, hT[k]], [pg])
                for k in range(16):
                    MM(pv[:, :T], wv[:, k, cc * 128:(cc + 1) * 128], hT[k][:, :T], k == 0, k == 15, [wv, hT[k]], [pv])
                gp = gpr[m % 2]
                gc_ = gcv[m % 2]
                s_ = sg_[m % 2]
                for si, sgm in enumerate(segs):
                    fch = sgm["seq"].fch[l]
                    CP(gp[:, si, 0:2], fch[:, m, :], [fch], [gp], eng="pool")
                P.add("act", lambda e, gp=gp, pg=pg: e.activation(out=gp[:, :, 2:2 + L], in_=pg[:, :T].rearrange("p (s l) -> p s l", s=nseg), func=AF.Copy),
                      bl([pg]) + [regbuf], bl([gp]), guard=False)
                for si, sgm in enumerate(segs):
                    fch = sgm["seq"].fch[l]
                    CP(fch[:, m, :], gp[:, si, L:L + 2], [gp], [fch], eng="pool")
                cw = PK_FCW + m * 3
                TS(gc_[:], gp[:, :, 0:L], pkl[:, cw:cw + 1], None, ALU.mult, None, [gp, pkl], [gc_])
                for jj in range(1, 3):
                    STT(gc_[:], gp[:, :, jj:jj + L], pkl[:, cw + jj:cw + jj + 1], gc_[:], ALU.mult, ALU.add, [gp, pkl, gc_], [gc_])
                ACT(s_[:, :T], gc_[:].rearrange("p s l -> p (s l)"), AF.Silu, [gc_], [s_])
                TT(actT[:, m, :], s_[:, :T], pv[:, :T], ALU.mult, [s_, pv], [actT])
        if is_last_tile:
            for si, sgm in enumerate(segs):
                sq_ = sgm["seq"]
                DMA("sp", sgm["fc_out"][l].rearrange("p (c j) -> p c j", j=2), sq_.fch[l][:], [sq_.fch[l]], [])
        kgroups = [(0, 16), (16, 16), (32, 12)]
        for g in range(4):
            acc = [ps[4 + cc] for cc in range(4)]
            for gi, (k0, nk) in enumerate(kgroups):
                w = wload(w_dn[l][k0 * 128:(k0 + nk) * 128, g * 512:(g + 1) * 512], nk, 512)
                for cc in range(4):
                    for k in range(nk):
                        MM(acc[cc][:, :T], w[:, k, cc * 128:(cc + 1) * 128], actT[:, k0 + k, :], (gi == 0 and k == 0), (gi == 2 and k == nk - 1),
                           [w, actT], [acc[cc]])
            for cc in range(4):
                xk = xT[g * 4 + cc]
                TT(xk[:, :T], xk[:, :T], acc[cc][:, :T], ALU.add, [xk, acc[cc]], [xk])

        if last_layer:
            fence()
            pb = bank()
            for k in range(16):
                sq = sqs[k % 2]
                ACT(sq[:, :T], xT[k][:, :T], AF.Square, [xT[k]], [sq])
                MM(pb[:, :T], ones_b, sq[:, :T], k == 0, k == 15, [sq, cbf], [pb])
            ACT(rstd[:, :T], pb[:, :T], AF.Sqrt, [pb], [rstd], bias=EPS, scale=1.0 / D)
            RCP(rstd[:, :T], rstd[:, :T], [rstd], [rstd])
            yT = [alloc([128, T], F32, "yT%d" % i) for i in range(4)]
            ytk = [alloc([128, 512], F32, "ytk%d" % i) for i in range(3)]
            yrr = [0]
            for g in range(4):
                for cc in range(4):
                    k = g * 4 + cc
                    STT(yT[cc][:, :T], xT[k][:, :T], pkl[:, PK_GFIN + k:PK_GFIN + k + 1], rstd[:, :T], ALU.mult, ALU.mult, [xT[k], pkl, rstd], [yT[cc]])
                for si, sgm in enumerate(segs):
                    for s in range(nsub):
                        t0 = sgm["col0"] + s * c
                        pT = bank()
                        for cc in range(4):
                            MM(pT[:c, cc * 128:(cc + 1) * 128], yT[cc][:, t0:t0 + c], ident_f, True, True, [yT[cc], cst], [pT])
                        yt = ytk[yrr[0] % 3]
                        yrr[0] += 1
                        CP(yt[:c, :], pT[:c, :], [pT], [yt], eng="act" if yrr[0] % 2 else "dve")
                        DMA("sp", sgm["yout"][s * c:(s + 1) * c, g * 512:(g + 1) * 512], yt[:c, :], [yt], [])

    def load_x(segs, c, nsub):
        xs = [alloc([128, D], F32, "xs%d" % i) for i in range(2)]
        rr = 0
        for si, sgm in enumerate(segs):
            for s in range(nsub):
                t0 = sgm["col0"] + s * c
                st = xs[rr % 2]
                rr += 1
                DMA("sp", st[:c, :], sgm["xin"][s * c:(s + 1) * c, :], [], [st])
                for g in range(4):
                    pT = bank()
                    for cc in range(4):
                        k = g * 4 + cc
                        MM(pT[:, cc * c:(cc + 1) * c], st[:c, k * 128:(k + 1) * 128], ident_f[:c, :c], True, True, [st, cst], [pT])
                    for cc in range(4):
                        k = g * 4 + cc
                        CP(xT[k][:, t0:t0 + c], pT[:, cc * c:(cc + 1) * c], [pT], [xT[k]], eng="act" if cc % 2 else "dve")

    for i in range(NT):
        sgm = dict(seq=seq_p, col0=0, L=TP, blk0=i * (TP // 128),
                   xin=x_p[i * TP:(i + 1) * TP, :], yout=y_p[i * TP:(i + 1) * TP, :],
                   kout=[fk_p[l, i * TP:(i + 1) * TP, :] for l in range(DEPTH)],
                   vout=[fv_p[l, i * TP:(i + 1) * TP, :] for l in range(DEPTH)],
                   lout=[fl_p[l, i * TP:(i + 1) * TP, :] for l in range(DEPTH)],
                   kbuf=[kvK[l][i] for l in range(DEPTH)], vbuf=[kvV[l][i] for l in range(DEPTH)],
                   gs_out=[gs_p[l] for l in range(DEPTH)], gc_out=[gc_p[l] for l in range(DEPTH)],
                   lh_out=[lh_p[l] for l in range(DEPTH)], lc_out=[lc_p[l] for l in range(DEPTH)], fc_out=[fc_p[l] for l in range(DEPTH)])
        for l in range(DEPTH):
            run_tile([sgm], TP, 128, l, (lambda sgm=sgm: load_x([sgm], 128, TP // 128)) if l == 0 else None,
                     l == DEPTH - 1, None, i == NT - 1)
    dummyK = [Buf("dk") for _ in range(DEPTH)]
    dummyV = [Buf("dv") for _ in range(DEPTH)]
    ssegs = []
    for si in range(2):
        ssegs.append(dict(seq=seq_s[si], col0=si * LS, L=LS, blk0=PB,
                          xin=x_s[si], yout=y_s[si],
                          kout=[fk_s[l, si] for l in range(DEPTH)], vout=[fv_s[l, si] for l in range(DEPTH)],
                          lout=[fl_s[l, si] for l in range(DEPTH)], kbuf=dummyK, vbuf=dummyV,
                          gs_out=[gs_s[l, si] for l in range(DEPTH)], gc_out=[gc_s[l, si] for l in range(DEPTH)],
                          lh_out=[lh_s[l, si] for l in range(DEPTH)], lc_out=[lc_s[l, si] for l in range(DEPTH)],
                          fc_out=[fc_s[l, si] for l in range(DEPTH)]))
    for l in range(DEPTH):
        run_tile(ssegs, 2 * LS, LS, l, (lambda: load_x(ssegs, LS, 1)) if l == 0 else None, l == DEPTH - 1, None, True)
    P.emit()
    return nc


def _pack_small(inp, l):
    pk = np.zeros((128, NPK), np.float32)

    def fm(v, n):
        return np.ascontiguousarray(v.reshape(n, 128).T)
    pk[:, PK_GMIX:PK_GMIX + 16] = fm(inp["norm_mix_g"][l], 16)
    pk[:, PK_GFFN:PK_GFFN + 16] = fm(inp["norm_ffn_g"][l], 16)
    pk[:, PK_GFIN:PK_GFIN + 16] = fm(inp["final_norm_g"], 16)
    pk[:, PK_GCW:PK_GCW + 48] = inp["gdn_conv_w"][l].reshape(4, 12, 128).transpose(2, 1, 0).reshape(128, 48)
    pk[:, PK_LCW:PK_LCW + 16] = inp["lru_conv_w"][l].reshape(4, 4, 128).transpose(2, 1, 0).reshape(128, 16)
    pk[:, PK_LCB:PK_LCB + 4] = fm(inp["lru_conv_b"][l], 4)
    pk[:, PK_LBA:PK_LBA + 4] = fm(inp["lru_b_a"][l], 4)
    pk[:, PK_LBX:PK_LBX + 4] = fm(inp["lru_b_x"][l], 4)
    pk[:, PK_LAM:PK_LAM + 4] = fm(inp["lru_lambda"][l], 4)
    pk[:, PK_LNG:PK_LNG + 4] = fm(inp["lru_norm_g"][l], 4)
    pk[:, PK_FCW:PK_FCW + 132] = inp["ffn_conv_w"][l].reshape(3, 44, 128).transpose(2, 1, 0).reshape(128, 132)
    pk[:, PK_GNG4:PK_GNG4 + 512] = np.tile(inp["gdn_norm_g"][l], 4)[None, :]
    pk[:, PK_FNG:PK_FNG + 1024] = inp["fox_norm_g"][l].reshape(1024)[None, :]
    pk[:, PK_ALOG:PK_ALOG + 4] = inp["gdn_a_log"][l][None, :]
    pk[:, PK_DTB:PK_DTB + 4] = inp["gdn_dt_bias"][l][None, :]
    pk[:, PK_FB:PK_FB + 8] = inp["fox_f_bias"][l][None, :]
    return pk


def _consts():
    c = np.zeros((128, NCST), np.float32)
    c[:, C_ID:C_ID + 128] = np.eye(128)
    c[:, C_TRI:C_TRI + 128] = np.triu(np.ones((128, 128)))
    c[:, C_ONE:C_ONE + 128] = 1.0
    c[:, C_MLS:C_MLS + 128] = np.tril(np.ones((128, 128)), -1)
    return c


_NC_CACHE = {}


def run(inp, n_cores=8, SEQ=4096, PAST=4096):
    inp = {k: np.asarray(v) for k, v in inp.items()}
    DEPTH = 2
    key = (SEQ, PAST)
    if key not in _NC_CACHE:
        _NC_CACHE[key] = build(SEQ=SEQ, PAST=PAST)
    nc = _NC_CACHE[key]
    BP = inp["x_prompt"].shape[0]
    BS = inp["x_sample"].shape[0]
    pk = np.stack([_pack_small(inp, l) for l in range(DEPTH)])
    cst = _consts()
    c32 = np.ascontiguousarray

    def fmT(a, n, j):
        sh = a.shape[:-2]
        return c32(a.reshape(sh + (j, n, 128)).transpose(tuple(range(len(sh))) + (len(sh) + 2, len(sh) + 1, len(sh))).reshape(sh + (128, n * j)))

    in_maps = []
    for core in range(n_cores):
        b = core % BP
        ss = [(2 * (core % (BS // 2))), (2 * (core % (BS // 2)) + 1)]
        m = {
            "x_p": c32(inp["x_prompt"][b]),
            "x_s": c32(inp["x_sample"][ss]),
            "ck": c32(inp["cache_fox_k"][:, ss].reshape(DEPTH, 2, PAST, 1024)),
            "cv": c32(inp["cache_fox_v"][:, ss].reshape(DEPTH, 2, PAST, 1024)),
            "cl": c32(inp["cache_fox_logf"][:, ss]),
            "sg": c32(inp["state_gdn"][:, ss]),
            "sgc": fmT(inp["state_gdn_conv"][:, ss], 12, 3),
            "sl": fmT(inp["state_lru"][:, ss][:, :, None, :], 4, 1),
            "slc": fmT(inp["state_lru_conv"][:, ss], 4, 3),
            "sfc": fmT(inp["state_ffn_conv"][:, ss], 44, 2),
            "w_in": inp["w_in"], "w_out": inp["w_out"], "w_up": inp["ffn_w_up"], "w_dn": inp["ffn_w_down"],
            "lwa": inp["lru_w_a"], "lwx": inp["lru_w_x"], "pk": pk, "cst": cst,
        }
        in_maps.append(m)
    res = run_bass_kernel_spmd(nc, in_maps, core_ids=list(range(n_cores)))
    R = res.results

    def unfm(a, n, j):
        sh = a.shape[:-2]
        return c32(a.reshape(sh + (128, n, j)).transpose(tuple(range(len(sh))) + (len(sh) + 2, len(sh) + 1, len(sh))).reshape(sh + (j, n * 128)))

    pc = list(range(BP))
    sc = list(range(BS // 2))
    y_p = np.stack([R[c]["y_p"] for c in pc])
    y_s = np.concatenate([R[c]["y_s"] for c in sc])
    fk_p = np.stack([R[c]["fk_p"] for c in pc], 1).reshape(DEPTH, BP, SEQ, 8, 128)
    fv_p = np.stack([R[c]["fv_p"] for c in pc], 1).reshape(DEPTH, BP, SEQ, 8, 128)
    fl_p = np.stack([R[c]["fl_p"] for c in pc], 1)
    gs_p = np.stack([R[c]["gs_p"] for c in pc], 1)
    gc_p = unfm(np.stack([R[c]["gc_p"] for c in pc], 1), 12, 3)
    lh_p = unfm(np.stack([R[c]["lh_p"] for c in pc], 1), 4, 1)[:, :, 0, :]
    lc_p = unfm(np.stack([R[c]["lc_p"] for c in pc], 1), 4, 3)
    fc_p = unfm(np.stack([R[c]["fc_p"] for c in pc], 1), 44, 2)
    fk_s = np.concatenate([R[c]["fk_s"] for c in sc], 1).reshape(DEPTH, BS, 16, 8, 128)
    fv_s = np.concatenate([R[c]["fv_s"] for c in sc], 1).reshape(DEPTH, BS, 16, 8, 128)
    fl_s = np.concatenate([R[c]["fl_s"] for c in sc], 1)
    gs_s = np.concatenate([R[c]["gs_s"] for c in sc], 1)
    gc_s = unfm(np.concatenate([R[c]["gc_s"] for c in sc], 1), 12, 3)
    lh_s = unfm(np.concatenate([R[c]["lh_s"] for c in sc], 1), 4, 1)[:, :, 0, :]
    lc_s = unfm(np.concatenate([R[c]["lc_s"] for c in sc], 1), 4, 3)
    fc_s = unfm(np.concatenate([R[c]["fc_s"] for c in sc], 1), 44, 2)
    outs = (y_p, y_s, fk_p, fv_p, fl_p, gs_p, gc_p, lh_p, lc_p, fc_p, fk_s, fv_s, fl_s, gs_s, gc_s, lh_s, lc_s, fc_s)
    return tuple(np.ascontiguousarray(o, dtype=np.float32) for o in outs)


def kernel(**inputs):
    return run(inputs, n_cores=8, SEQ=4096, PAST=4096)
```

```python
import numpy as np
from contextlib import ExitStack
import concourse.bass as bass
import concourse.mybir as mybir
from concourse.bass_utils import run_bass_kernel_spmd

F32 = mybir.dt.float32
BF16 = mybir.dt.bfloat16
AF = mybir.ActivationFunctionType
ALU = mybir.AluOpType

ENGS = ("pe", "act", "dve", "pool", "sp")
D = 2048
DFF = 5632
INW = 6160
EPS = 1e-6
NPK = 1816
PK_GMIX, PK_GFFN, PK_GFIN, PK_GCW, PK_LCW, PK_LCB, PK_LBA, PK_LBX, PK_LAM, PK_LNG, PK_FCW = 0, 16, 32, 48, 96, 112, 116, 120, 124, 128, 132
PK_GNG4, PK_FNG, PK_ALOG, PK_DTB, PK_FB = 264, 776, 1800, 1804, 1808
C_ID, C_TRI, C_ONE, C_MLS = 0, 128, 256, 384
NCST = 512


class Buf:
    __slots__ = ("name", "w", "r", "excl")

    def __init__(self, name="", excl=False):
        self.name = name
        self.w = None
        self.r = {}
        self.excl = excl


class Op:
    __slots__ = ("eng", "fn", "deps", "dma", "tok", "need_sig", "idx")

    def __init__(self, eng, fn, dma):
        self.eng = eng
        self.fn = fn
        self.dma = dma
        self.deps = set()
        self.tok = None
        self.need_sig = False


class Prog:
    def __init__(self, nc, n_dma_sems=12):
        self.nc = nc
        self.ops = []
        self.n_dma_sems = n_dma_sems
        self.guard = None
        self.stopped = False
        self.nfence = 0

    def add(self, eng, fn, reads=(), writes=(), dma=False, guard=True):
        import os as _os
        if self.stopped or len(self.ops) >= int(_os.environ.get("STOPOPS", "100000000")):
            return None
        op = Op(eng, fn, dma)
        idx = len(self.ops)
        op.idx = idx
        reads = list(reads)
        if guard and self.guard is not None:
            reads.append(self.guard)
        for b in reads:
            if b.w is not None:
                op.deps.add(b.w)
            if b.excl:
                for key, ridx in b.r.items():
                    if key != eng:
                        op.deps.add(ridx)
        for b in writes:
            if b.w is not None:
                op.deps.add(b.w)
            for ridx in b.r.values():
                op.deps.add(ridx)
        op.deps.discard(idx)
        if eng == "pe" and not dma:
            op.deps = {d for d in op.deps if not (self.ops[d].eng == "pe" and not self.ops[d].dma)}
        for d in op.deps:
            self.ops[d].need_sig = True
        self.ops.append(op)
        for b in reads:
            key = ("d", idx) if dma else eng
            b.r[key] = idx
        for b in writes:
            b.w = idx
            b.r = {}
        return op

    def emit(self):
        nc = self.nc
        ops = self.ops
        nds = self.n_dma_sems
        dslot = {}
        dcount = {q: [0] * nds for q in ("sp", "pool")}
        drr = {q: 0 for q in ("sp", "pool")}
        last_on_slot = {}
        prewait = {}
        for op in ops:
            if op.dma:
                i = drr[op.eng] % nds
                drr[op.eng] += 1
                if (op.eng, i) in last_on_slot:
                    prewait[op.idx] = last_on_slot[(op.eng, i)]
                last_on_slot[(op.eng, i)] = op.idx
                dcount[op.eng][i] += 1
                dslot[op.idx] = (op.eng, i, dcount[op.eng][i] * 16)
        ordn = {}
        ecnt = {e: 0 for e in ENGS}
        prev_on = {e: None for e in ENGS}
        vc = [None] * len(ops)

        def merge(a, b):
            for kk, vv in b.items():
                if a.get(kk, 0) < vv:
                    a[kk] = vv
        for op in ops:
            v = {}
            for d in op.deps:
                merge(v, vc[d])
            if op.dma:
                q, i, val = dslot[op.idx]
                v[("d", q, i)] = max(v.get(("d", q, i), 0), val)
            else:
                ecnt[op.eng] += 1
                ordn[op.idx] = ecnt[op.eng]
                if prev_on[op.eng] is not None:
                    merge(v, vc[prev_on[op.eng]])
                v[("e", op.eng)] = ecnt[op.eng]
                prev_on[op.eng] = op.idx
            vc[op.idx] = v
        per_eng = {e: [op for op in ops if op.eng == e] for e in ENGS}

        def implied(known, d):
            if ops[d].dma:
                q, i, val = dslot[d]
                return known.get(("d", q, i), 0) >= val
            return known.get(("e", ops[d].eng), 0) >= ordn[d]

        def plan(e):
            known = {}
            out = []
            for op in per_eng[e]:
                ws = []
                cand = sorted(op.deps, reverse=True)
                if op.idx in prewait:
                    cand.append(prewait[op.idx])
                for d in cand:
                    if not implied(known, d):
                        ws.append(d)
                        merge(known, vc[d])
                out.append((op, ws))
            return out
        plans = {e: plan(e) for e in ENGS}
        for op in ops:
            op.need_sig = op.dma
        for e in ENGS:
            for op, ws in plans[e]:
                for d in ws:
                    ops[d].need_sig = True
        with ExitStack() as es:
            esem = {e: es.enter_context(nc.semaphore("s_" + e)) for e in ENGS}
            dsem = {q: [es.enter_context(nc.semaphore("d_%s%d" % (q, i))) for i in range(nds)] for q in ("sp", "pool")}
            ecount = {e: 0 for e in ENGS}
            for op in ops:
                if op.dma:
                    q, i, val = dslot[op.idx]
                    op.tok = (dsem[q][i], val)
                elif op.need_sig:
                    ecount[op.eng] += 1
                    op.tok = (esem[op.eng], ecount[op.eng])
            block = es.enter_context(nc.Block())

            def run(e, eng):
                for op, ws in plans[e]:
                    for d in ws:
                        sem, val = ops[d].tok
                        eng.wait_ge(sem, val)
                    ins = op.fn(eng)
                    if op.need_sig:
                        ins.then_inc(op.tok[0], 16 if op.dma else 1)
                if e == "sp":
                    for q in dsem:
                        for i in range(nds):
                            if dcount[q][i] > 0:
                                eng.wait_ge(dsem[q][i], dcount[q][i] * 16)

            @block.tensor
            def _(eng):
                run("pe", eng)

            @block.scalar
            def _(eng):
                run("act", eng)

            @block.vector
            def _(eng):
                run("dve", eng)

            @block.gpsimd
            def _(eng):
                run("pool", eng)

            @block.sync
            def _(eng):
                run("sp", eng)


class T_:
    def __init__(self, t, name, excl=False):
        self.t = t
        self.b = Buf(name, excl)

    def __getitem__(self, k):
        return self.t[k]


def build(SEQ=4096, PAST=4096, TP=512, DEPTH=2):
    nc = bass.Bass("TRN2", target_bir_lowering=False)
    NT = SEQ // TP
    PB = PAST // 128
    LS = 16

    def din(name, shape):
        return nc.dram_tensor(name, list(shape), F32, kind="ExternalInput").ap()

    def dout(name, shape):
        return nc.dram_tensor(name, list(shape), F32, kind="ExternalOutput").ap()

    x_p = din("x_p", [SEQ, D]); x_s = din("x_s", [2, LS, D])
    ck = din("ck", [DEPTH, 2, PAST, 1024]); cv = din("cv", [DEPTH, 2, PAST, 1024]); cl = din("cl", [DEPTH, 2, PAST, 8])
    sg = din("sg", [DEPTH, 2, 4, 128, 128]); sgc = din("sgc", [DEPTH, 2, 128, 36])
    sl = din("sl", [DEPTH, 2, 128, 4]); slc = din("slc", [DEPTH, 2, 128, 12]); sfc = din("sfc", [DEPTH, 2, 128, 88])
    w_in = din("w_in", [DEPTH, D, INW]); w_out = din("w_out", [DEPTH, D, D])
    w_up = din("w_up", [DEPTH, D, 2 * DFF]); w_dn = din("w_dn", [DEPTH, DFF, D])
    lwa = din("lwa", [DEPTH, 4, 128, 128]); lwx = din("lwx", [DEPTH, 4, 128, 128])
    pk_d = din("pk", [DEPTH, 128, NPK]); cst_d = din("cst", [128, NCST])

    y_p = dout("y_p", [SEQ, D]); y_s = dout("y_s", [2, LS, D])
    fk_p = dout("fk_p", [DEPTH, SEQ, 1024]); fv_p = dout("fv_p", [DEPTH, SEQ, 1024]); fl_p = dout("fl_p", [DEPTH, SEQ, 8])
    gs_p = dout("gs_p", [DEPTH, 4, 128, 128]); gc_p = dout("gc_p", [DEPTH, 128, 36]); lh_p = dout("lh_p", [DEPTH, 128, 4])
    lc_p = dout("lc_p", [DEPTH, 128, 12]); fc_p = dout("fc_p", [DEPTH, 128, 88])
    fk_s = dout("fk_s", [DEPTH, 2, LS, 1024]); fv_s = dout("fv_s", [DEPTH, 2, LS, 1024]); fl_s = dout("fl_s", [DEPTH, 2, LS, 8])
    gs_s = dout("gs_s", [DEPTH, 2, 4, 128, 128]); gc_s = dout("gc_s", [DEPTH, 2, 128, 36]); lh_s = dout("lh_s", [DEPTH, 2, 128, 4])
    lc_s = dout("lc_s", [DEPTH, 2, 128, 12]); fc_s = dout("fc_s", [DEPTH, 2, 128, 88])

    P = Prog(nc)
    sb_base = (nc.sbuf_base + 63) // 64 * 64
    sb_top = nc.sbuf_top
    cur = [sb_base]
    cnt = [0]

    def alloc(shape, dt, name="t"):
        nbytes = int(np.prod(shape[1:])) * (4 if dt == F32 else 2)
        nbytes = (nbytes + 63) // 64 * 64
        off = cur[0]
        cur[0] += nbytes
        assert cur[0] <= sb_top, ("SBUF overflow", name, cur[0], sb_top)
        cnt[0] += 1
        t = nc.alloc_sbuf_tensor_at("%s_%d" % (name, cnt[0]), list(shape), dt, offset=off)
        return T_(t, name)

    ps = []
    for i in range(8):
        ps.append(T_(nc.alloc_psum_tensor("ps%d" % i, [128, 512], F32), "ps%d" % i, excl=True))

    def bl(xs):
        return [x.b if isinstance(x, T_) else x for x in xs]

    def ACT(out, in_, func, R, W, **kw):
        P.add("act", lambda e: e.activation(out=out, in_=in_, func=func, **kw), bl(R), bl(W))

    def TT(out, in0, in1, op, R, W, eng="dve"):
        P.add(eng, lambda e: e.tensor_tensor(out=out, in0=in0, in1=in1, op=op), bl(R), bl(W))

    def TS(out, in0, s1, s2, op0, op1, R, W, eng="dve"):
        if s2 is None:
            P.add(eng, lambda e: e.tensor_scalar(out=out, in0=in0, scalar1=s1, scalar2=None, op0=op0), bl(R), bl(W))
        else:
            P.add(eng, lambda e: e.tensor_scalar(out=out, in0=in0, scalar1=s1, scalar2=s2, op0=op0, op1=op1), bl(R), bl(W))

    def STT(out, in0, sc, in1, op0, op1, R, W):
        P.add("dve", lambda e: e.scalar_tensor_tensor(out=out, in0=in0, scalar=sc, in1=in1, op0=op0, op1=op1), bl(R), bl(W))

    def CP(out, in_, R, W, eng="dve"):
        if eng == "act":
            P.add(eng, lambda e: e.activation(out=out, in_=in_, func=AF.Copy), bl(R), bl(W))
        else:
            P.add(eng, lambda e: e.tensor_copy(out=out, in_=in_), bl(R), bl(W))

    def RCP(out, in_, R, W):
        P.add("dve", lambda e: e.reciprocal(out=out, in_=in_), bl(R), bl(W))

    def MM(out, lhsT, rhs, start, stop, R, W):
        P.add("pe", lambda e: e.matmul(out, lhsT=lhsT, rhs=rhs, start=start, stop=stop), bl(R), bl(W))

    def DMA(q, out, in_, R, W, guard=True, slow=False):
        if slow:
            P.add(q, lambda e: e.dma_start(out=out, in_=in_, allow_slow_non_contiguous=True), bl(R), bl(W), dma=True, guard=guard)
        else:
            P.add(q, lambda e: e.dma_start(out=out, in_=in_), bl(R), bl(W), dma=True, guard=guard)

    def MEMSET(ap, val, W, eng="dve"):
        import os as _os
        if _os.environ.get("SKIP_POOLMS") and eng == "pool":
            eng = "dve"
        P.add(eng, lambda e: e.memset(ap, val), [], bl(W))

    cst = alloc([128, NCST], F32, "cst")
    cbf = alloc([128, 384], BF16, "cbf")
    pk = [alloc([128, NPK], F32, "pk%d" % l) for l in range(DEPTH)]
    lw = [[alloc([128, 4, 128], BF16, "lwa%d" % l), alloc([128, 4, 128], BF16, "lwx%d" % l)] for l in range(DEPTH)]
    c1 = [alloc([128, 4], F32, "c1_%d" % l) for l in range(DEPTH)]
    nalog = [alloc([128, 4], F32, "nalog%d" % l) for l in range(DEPTH)]
    xT = [alloc([128, TP], F32, "xT%d" % k) for k in range(16)]
    hT = [alloc([128, TP], BF16, "hT%d" % k) for k in range(16)]
    oT = [alloc([128, TP], BF16, "oT%d" % k) for k in range(16)]
    NW = 2
    wb = [alloc([128, 16, 512], BF16, "wb%d" % i) for i in range(NW)]
    wsm = alloc([128, 16, 16], BF16, "wsm")
    rstd = alloc([128, TP], F32, "rstd")
    sqs = [alloc([128, TP], BF16, "sq%d" % i) for i in range(2)]
    tiny = alloc([128, 64], F32, "tiny")
    ffr = alloc([128, 4, 8], F32, "ffr")
    junk = alloc([128, 128], F32, "junk")
    sqg = [alloc([128, TP], BF16, "sqg%d" % i) for i in range(2)]

    DMA("sp", cst[:], cst_d, [], [cst], guard=False)
    CP(cbf[:, 0:384], cst[:, 0:384], [cst], [cbf])
    ident_f = cst.t[:, C_ID:C_ID + 128]
    tri_f = cst.t[:, C_TRI:C_TRI + 128]
    ones_f = cst.t[:, C_ONE:C_ONE + 128]
    mls_f = cst.t[:, C_MLS:C_MLS + 128]
    ident_b = cbf.t[:, 0:128]
    tri_b = cbf.t[:, 128:256]
    ones_b = cbf.t[:, 256:384]
    for l in range(DEPTH):
        DMA("sp", pk[l][:], pk_d[l], [], [pk[l]], guard=False)
        DMA("pool", lw[l][0][:], lwa[l].rearrange("n c d -> c n d"), [], [lw[l][0]], guard=False)
        DMA("pool", lw[l][1][:], lwx[l].rearrange("n c d -> c n d"), [], [lw[l][1]], guard=False)
        import os as _os
        if _os.environ.get("SKIP_ACT"):
            continue
        o_ = 16 * l
        ACT(tiny[:, o_:o_ + 4], pk[l][:, PK_LAM:PK_LAM + 4], AF.Exp, [pk[l]], [tiny], scale=-1.0)
        ACT(tiny[:, o_ + 4:o_ + 8], tiny[:, o_:o_ + 4], AF.Ln, [tiny], [tiny], bias=1.0)
        ACT(c1[l][:], tiny[:, o_ + 4:o_ + 8], AF.Copy, [tiny], [c1[l]], scale=-8.0)
        ACT(tiny[:, o_ + 8:o_ + 12], pk[l][:, PK_ALOG:PK_ALOG + 4], AF.Exp, [pk[l]], [tiny])
        ACT(nalog[l][:], tiny[:, o_ + 8:o_ + 12], AF.Copy, [tiny], [nalog[l]], scale=-1.0)

    class Seq:
        pass

    def mkseq(name, is_prompt, sidx):
        s = Seq()
        s.name = name
        s.prompt = is_prompt
        s.sidx = sidx
        s.S = [alloc([128, 4, 128], F32, name + "S") for _ in range(DEPTH)]
        s.gch = [alloc([128, 12, 3], F32, name + "gch") for _ in range(DEPTH)]
        s.lh = [alloc([128, 4], F32, name + "lh") for _ in range(DEPTH)]
        s.lch = [alloc([128, 4, 3], F32, name + "lch") for _ in range(DEPTH)]
        s.fch = [alloc([128, 44, 2], F32, name + "fch") for _ in range(DEPTH)]
        s.Fb = [alloc([128, 8], F32, name + "Fb") for _ in range(DEPTH)]
        if is_prompt:
            s.Fk = [alloc([128, SEQ // 128, 8], F32, name + "Fk") for _ in range(DEPTH)]
            s.Fref = [alloc([128, SEQ // 128, 8], F32, name + "Fref") for _ in range(DEPTH)]
        return s

    seq_p = mkseq("p", True, 0)
    seq_s = [mkseq("s%d" % i, False, i) for i in range(2)]
    for l in range(DEPTH):
        for t in (seq_p.S[l], seq_p.gch[l], seq_p.lh[l], seq_p.lch[l], seq_p.fch[l], seq_p.Fb[l]):
            MEMSET(t[:], 0.0, [t], eng="pool")
        for i, s in enumerate(seq_s):
            DMA("sp", s.S[l][:], sg[l, i].rearrange("h k v -> k h v"), [], [s.S[l]], guard=False)
            DMA("sp", s.gch[l][:], sgc[l, i].rearrange("p (c j) -> p c j", j=3), [], [s.gch[l]], guard=False)
            DMA("sp", s.lh[l][:], sl[l, i], [], [s.lh[l]], guard=False)
            DMA("sp", s.lch[l][:], slc[l, i].rearrange("p (c j) -> p c j", j=3), [], [s.lch[l]], guard=False)
            DMA("sp", s.fch[l][:], sfc[l, i].rearrange("p (c j) -> p c j", j=2), [], [s.fch[l]], guard=False)
            MEMSET(s.Fb[l][:], 0.0, [s.Fb[l]], eng="pool")

    region0 = cur[0]
    regbuf = Buf("region")
    P.guard = regbuf
    kvK = [[Buf("kvK%d_%d" % (l, i)) for i in range(NT)] for l in range(DEPTH)]
    kvV = [[Buf("kvV%d_%d" % (l, i)) for i in range(NT)] for l in range(DEPTH)]

    def fence(keep=None):
        cur[0] = region0 if keep is None else keep
        P.nfence += 1
        import os as _os
        if P.nfence > int(_os.environ.get("STOPF", "100000")):
            P.stopped = True
            return
        P.add("dve", lambda e: e.memset(tiny[:, 60:61], 0.0), [], [regbuf], guard=False)

    wrr = [0]

    NSLOT = 50 * DEPTH
    wsc = nc.dram_tensor("wsc", [NSLOT, 128, 16 * 512], BF16).ap()
    wsmsc = nc.dram_tensor("wsmsc", [DEPTH, 128, 256], BF16).ap()
    wslot = {}
    wsbuf = {}

    def wload(key, src_rows_ap, nk, ncols):
        assert ncols == 512
        w = wb[wrr[0] % NW]
        wrr[0] += 1
        if key not in wslot:
            slot = len(wslot)
            assert slot < NSLOT
            wslot[key] = slot
            wsbuf[key] = Buf("wsc%d" % slot)
            DMA("pool", w[:, 0:nk, 0:ncols], src_rows_ap.rearrange("(k p) n -> p k n", p=128), [], [w], guard=False)
            DMA("pool", wsc[slot][:, 0:nk * 512].rearrange("p (k n) -> p k n", n=512), w[:, 0:nk, 0:ncols], [w], [wsbuf[key]], guard=False)
        else:
            slot = wslot[key]
            DMA("sp", w[:, 0:nk, 0:ncols], wsc[slot][:, 0:nk * 512].rearrange("p (k n) -> p k n", n=512), [wsbuf[key]], [w], guard=False)
        return w

    psrr = [0]

    def bank(group=(0, 1, 2, 3)):
        b = ps[group[psrr[0] % len(group)]]
        psrr[0] += 1
        return b

    def rmsnorm_to_hT(T, gcol, l):
        pb = bank()
        for k in range(16):
            sq = sqs[k % 2]
            ACT(sq[:, :T], xT[k][:, :T], AF.Square, [xT[k]], [sq])
            MM(pb[:, :T], ones_b, sq[:, :T], k == 0, k == 15, [sq, cbf], [pb])
        ACT(rstd[:, :T], pb[:, :T], AF.Sqrt, [pb], [rstd], bias=EPS, scale=1.0 / D)
        RCP(rstd[:, :T], rstd[:, :T], [rstd], [rstd])
        for k in range(16):
            STT(hT[k][:, :T], xT[k][:, :T], pk[l][:, gcol + k:gcol + k + 1], rstd[:, :T], ALU.mult, ALU.mult,
                [xT[k], rstd, pk[l]], [hT[k]])

    def fm_cols(w, T, ncc, evac, src=None):
        src = src or hT
        for cc in range(ncc):
            pb = bank()
            for k in range(16):
                MM(pb[:, :T], w[:, k, cc * 128:(cc + 1) * 128], src[k][:, :T], k == 0, k == 15, [w, src[k]], [pb])
            evac(cc, pb)

    def tm_cols(w, cols, ncols, c, evac):
        pb = bank()
        for k in range(16):
            MM(pb[:c, :ncols], hT[k][:, cols], w[:, k, 0:ncols], k == 0, k == 15, [w, hT[k]], [pb])
        evac(pb)

    def run_tile(segs, T, c, l, src_loader, last_layer, y_writer, is_last_tile):
        nseg = len(segs)
        L = segs[0]["L"]
        nsub = L // c
        win = w_in[l]
        pkl = pk[l]
        if src_loader is not None:
            fence()
            src_loader()
        fence()
        rmsnorm_to_hT(T, PK_GMIX, l)
        fence()
        qkvT = [alloc([128, 4, T], BF16, "gq"), alloc([128, 4, T], BF16, "gk"), alloc([128, 4, T], BF16, "gv")]
        zg = alloc([128, nseg * nsub, 512], BF16, "zg")
        bt = alloc([128, nseg * nsub, 4], F32, "beta")
        gg = alloc([128, nseg * nsub, 4], F32, "gg")
        tg = alloc([128, 16], F32, "tg")
        gdn_mark = cur[0]
        raws = [alloc([128, nseg, 3 + L], F32, "raw%d" % i) for i in range(2)]
        cvs = [alloc([128, nseg, L], F32, "cv%d" % i) for i in range(2)]
        sil = [alloc([128, T], F32, "sil%d" % i) for i in range(2)]
        rq = alloc([128, T], F32, "rq")
        for grp in range(3):
            w = wload((l, "qkv", grp), win[:, grp * 512:(grp + 1) * 512], 16, 512)

            def ev(cc, pb, grp=grp):
                ch = grp * 4 + cc
                raw = raws[ch % 2]
                cv_ = cvs[ch % 2]
                sl_ = sil[ch % 2]
                for si, sgm in enumerate(segs):
                    gch = sgm["seq"].gch[l]
                    CP(raw[:, si, 0:3], gch[:, ch, :], [gch], [raw], eng="pool")
                P.add("act", lambda e: e.activation(out=raw[:, :, 3:3 + L], in_=pb[:, :T].rearrange("p (s l) -> p s l", s=nseg), func=AF.Copy),
                      bl([pb]) + [regbuf], bl([raw]), guard=False)
                for si, sgm in enumerate(segs):
                    gch = sgm["seq"].gch[l]
                    CP(gch[:, ch, :], raw[:, si, L:L + 3], [raw], [gch], eng="pool")
                cw = PK_GCW + ch * 4
                TS(cv_[:], raw[:, :, 0:L], pkl[:, cw:cw + 1], None, ALU.mult, None, [raw, pkl], [cv_])
                for j in range(1, 4):
                    STT(cv_[:], raw[:, :, j:j + L], pkl[:, cw + j:cw + j + 1], cv_[:], ALU.mult, ALU.add, [raw, pkl, cv_], [cv_])
                dst = qkvT[grp]
                if grp == 2:
                    ACT(dst[:, cc, :], cv_[:].rearrange("p s l -> p (s l)"), AF.Silu, [cv_], [dst])
                else:
                    ACT(sl_[:, :T], cv_[:].rearrange("p s l -> p (s l)"), AF.Silu, [cv_], [sl_])
                    sq = sqg[ch % 2]
                    ACT(sq[:, :T], sl_[:, :T], AF.Square, [sl_], [sq])
                    pb2 = bank((4, 5))
                    MM(pb2[:, :T], ones_b, sq[:, :T], True, True, [sq, cbf], [pb2])
                    ACT(rq[:, :T], pb2[:, :T], AF.Sqrt, [pb2], [rq], bias=EPS, scale=1.0)
                    RCP(rq[:, :T], rq[:, :T], [rq], [rq])
                    STT(dst[:, cc, :], sl_[:, :T], (128.0 ** -0.5) if grp == 0 else 1.0, rq[:, :T], ALU.mult, ALU.mult, [sl_, rq], [dst])
            fm_cols(w, T, 4, ev)
        w = wload((l, "z"), win[:, 1536:2048], 16, 512)
        if ("wsm", l) not in wsbuf:
            wsbuf[("wsm", l)] = Buf("wsmsc%d" % l)
            DMA("pool", wsm[:, :, 0:8], win[:, 2048:2056].rearrange("(k p) n -> p k n", p=128), [], [wsm], guard=False)
            DMA("pool", wsm[:, :, 8:16], win[:, 5128:5136].rearrange("(k p) n -> p k n", p=128), [], [wsm], guard=False)
            DMA("pool", wsmsc[l].rearrange("p (k n) -> p k n", n=16), wsm[:, :, :], [wsm], [wsbuf[("wsm", l)]], guard=False)
        else:
            DMA("sp", wsm[:, :, :], wsmsc[l].rearrange("p (k n) -> p k n", n=16), [wsbuf[("wsm", l)]], [wsm], guard=False)
        for si, sgm in enumerate(segs):
            for s in range(nsub):
                cols = slice(sgm["col0"] + s * c, sgm["col0"] + (s + 1) * c)
                bi = si * nsub + s

                def evz(pb, bi=bi):
                    ACT(zg[:c, bi, :], pb[:c, :], AF.Silu, [pb], [zg])
                    TT(zg[:c, bi, :], zg[:c, bi, :], pkl[:c, PK_GNG4:PK_GNG4 + 512], ALU.mult, [zg, pkl], [zg])
                tm_cols(w, cols, 512, c, evz)
                pbs = bank()
                for k in range(16):
                    MM(pbs[:c, :16], hT[k][:, cols], wsm[:, k, :], k == 0, k == 15, [wsm, hT[k]], [pbs])
                ACT(bt[:c, bi, :], pbs[:c, 0:4], AF.Sigmoid, [pbs], [bt])
                TT(tg[:c, 0:4], pbs[:c, 4:8], pkl[:c, PK_DTB:PK_DTB + 4], ALU.add, [pbs, pkl], [tg])
                ACT(tg[:c, 4:8], tg[:c, 0:4], AF.Exp, [tg], [tg])
                ACT(tg[:c, 8:12], tg[:c, 4:8], AF.Ln, [tg], [tg], bias=1.0)
                TT(gg[:c, bi, :], tg[:c, 8:12], nalog[l][:c, :], ALU.mult, [tg, nalog[l]], [gg])
                CP(ffr[:c, bi, :], pbs[:c, 8:16], [pbs], [ffr])
        fence(keep=gdn_mark)
        c4 = 4 * c
        gmat = alloc([128, c4], F32, "gmat")
        Gcol = alloc([128, 4], F32, "Gcol")
        Gb = alloc([128, c4], F32, "Gb")
        EGb = alloc([128, c4], F32, "EGb")
        tmpm = [alloc([128, c], F32, "tmpm%d" % i) for i in range(2)]
        DLs = alloc([128, c4], F32, "DLs")
        DUm = alloc([128, c4], F32, "DUm")
        Lm = [alloc([128, c4], F32, "Lm%d" % i) for i in range(2)]
        Um = [alloc([128, c4], F32, "Um%d" % i) for i in range(2)]
        Pm = alloc([128, c4], F32, "Pm")
        MTm = alloc([128, c4], F32, "MTm")
        kbeg = alloc([128, 4, 128], F32, "kbeg")
        kd = alloc([128, 4, 128], F32, "kd")
        vb = alloc([128, 4, 128], F32, "vb")
        nwT = alloc([128, c4], F32, "nwT")
        delta = alloc([128, 4, 128], F32, "delta")
        qgT = alloc([128, c4], F32, "qgT")
        sm = alloc([128, 32], F32, "sm")
        ytok = alloc([128, 512], BF16, "ytok")
        nstep = int(np.log2(c)) - 1
        for si, sgm in enumerate(segs):
            S = sgm["seq"].S[l]
            for s in range(nsub):
                t0 = sgm["col0"] + s * c
                cols = slice(t0, t0 + c)
                bi = si * nsub + s
                qT_, kT_, vT_ = qkvT
                pk_tok = ps[4]
                pv_tok = ps[5]
                for h in range(4):
                    MM(pk_tok[:c, h * 128:(h + 1) * 128], kT_[:, h, cols], ident_b, True, True, [kT_, cbf], [pk_tok])
                    MM(pv_tok[:c, h * 128:(h + 1) * 128], vT_[:, h, cols], ident_b, True, True, [vT_, cbf], [pv_tok])
                pg = bank()
                MM(pg[:c, 0:4], tri_f[:c, :c], gg[:c, bi, :], True, True, [cst, gg], [pg])
                CP(Gcol[:c, :], pg[:c, 0:4], [pg], [Gcol])
                for h in range(4):
                    TS(gmat[:c, h * c:(h + 1) * c], tri_f[:c, :c], gg[:c, bi, h:h + 1], None, ALU.mult, None, [cst, gg], [gmat])
                pgb = bank()
                MM(pgb[:, :c4], ones_f[:c, :], gmat[:c, :c4], True, True, [cst, gmat], [pgb])
                CP(Gb[:, :c4], pgb[:, :c4], [pgb], [Gb])
                ACT(EGb[:, :c4], pgb[:, :c4], AF.Exp, [pgb], [EGb])
                ACT(sm[:c, 0:4], Gcol[:c, :], AF.Exp, [Gcol], [sm])
                TT(sm[:c, 4:8], sm[:c, 0:4], bt[:c, bi, :], ALU.mult, [sm, bt], [sm])
                for h in range(4):
                    ACT(sm[:c, 8 + h:9 + h], Gcol[:c, h:h + 1], AF.Exp, [Gcol, Gb], [sm], scale=-1.0, bias=Gb[:c, h * c + c - 1:h * c + c])
                for h in range(4):
                    TS(kbeg[:c, h, :], pk_tok[:c, h * 128:(h + 1) * 128], sm[:c, 4 + h:5 + h], None, ALU.mult, None, [pk_tok, sm], [kbeg])
                    TS(kd[:c, h, :], pk_tok[:c, h * 128:(h + 1) * 128], sm[:c, 8 + h:9 + h], None, ALU.mult, None, [pk_tok, sm], [kd])
                    TS(vb[:c, h, :], pv_tok[:c, h * 128:(h + 1) * 128], bt[:c, bi, h:h + 1], None, ALU.mult, None, [pv_tok, bt], [vb])
                for h in range(4):
                    hc = slice(h * c, (h + 1) * c)
                    tm = tmpm[0]
                    TS(tm[:c, :], Gb[:c, hc], Gcol[:c, h:h + 1], 0.0, ALU.subtract, ALU.max, [Gb, Gcol], [tm])
                    ACT(tm[:c, :], tm[:c, :], AF.Exp, [tm], [tm], scale=-1.0)
                    STT(DLs[:c, hc], tm[:c, :], bt[:c, bi, h:h + 1], mls_f[:c, :c], ALU.mult, ALU.mult, [tm, bt, cst], [DLs])
                    tm2 = tmpm[1]
                    TS(tm2[:c, :], Gb[:c, hc], Gcol[:c, h:h + 1], 0.0, ALU.subtract, ALU.min, [Gb, Gcol], [tm2])
                    ACT(tm2[:c, :], tm2[:c, :], AF.Exp, [tm2], [tm2])
                    TT(DUm[:c, hc], tm2[:c, :], tri_f[:c, :c], ALU.mult, [tm2, cst], [DUm])
                pkk = bank()
                pqk = bank()
                for h in range(4):
                    hc = slice(h * c, (h + 1) * c)
                    MM(pkk[:c, hc], kT_[:, h, cols], kT_[:, h, cols], True, True, [kT_], [pkk])
                    MM(pqk[:c, hc], kT_[:, h, cols], qT_[:, h, cols], True, True, [kT_, qT_], [pqk])
                L0 = Lm[0]
                TT(L0[:c, :c4], pkk[:c, :c4], DLs[:c, :c4], ALU.mult, [pkk, DLs], [L0])
                TT(MTm[:c, :c4], pqk[:c, :c4], DUm[:c, :c4], ALU.mult, [pqk, DUm], [MTm])
                pB = bank()
                for h in range(4):
                    hc = slice(h * c, (h + 1) * c)
                    MM(pB[:c, hc], L0[:c, hc], ident_f[:c, :c], True, True, [L0, cst], [pB])
                U0 = Um[0]
                CP(U0[:c, :c4], pB[:c, :c4], [pB], [U0], eng="act")
                for h in range(4):
                    hc = slice(h * c, (h + 1) * c)
                    TT(Pm[:c, hc], ident_f[:c, :c], pB[:c, hc], ALU.subtract, [cst, pB], [Pm])
                Lc, Uc = L0, U0
                for it in range(1, nstep + 1):
                    Ln_, Un_ = Lm[it % 2], Um[it % 2]
                    pU = bank()
                    pL = bank()
                    for h in range(4):
                        hc = slice(h * c, (h + 1) * c)
                        MM(pL[:c, hc], Uc[:c, hc], Lc[:c, hc], True, True, [Uc, Lc], [pL])
                        if it < nstep:
                            MM(pU[:c, hc], Lc[:c, hc], Uc[:c, hc], True, True, [Uc, Lc], [pU])
                    CP(Ln_[:c, :c4], pL[:c, :c4], [pL], [Ln_])
                    if it < nstep:
                        CP(Un_[:c, :c4], pU[:c, :c4], [pU], [Un_], eng="act")
                    pP = bank()
                    for h in range(4):
                        hc = slice(h * c, (h + 1) * c)
                        MM(pP[:c, hc], Ln_[:c, hc], Pm[:c, hc], True, True, [Ln_, Pm], [pP])
                    TT(Pm[:c, :c4], Pm[:c, :c4], pP[:c, :c4], ALU.add, [Pm, pP], [Pm])
                    Lc, Uc = Ln_, Un_
                pW = bank()
                for h in range(4):
                    hc = slice(h * c, (h + 1) * c)
                    MM(pW[:, hc], kbeg[:c, h, :], Pm[:c, hc], True, True, [kbeg, Pm], [pW])
                ACT(nwT[:, :c4], pW[:, :c4], AF.Copy, [pW], [nwT], scale=-1.0)
                pD = bank()
                for h in range(4):
                    hc = slice(h * c, (h + 1) * c)
                    MM(pD[:c, h * 128:(h + 1) * 128], Pm[:c, hc], vb[:c, h, :], True, False, [Pm, vb], [pD])
                    MM(pD[:c, h * 128:(h + 1) * 128], nwT[:, hc], S[:, h, :], False, True, [nwT, S], [pD])
                CP(delta[:c, :, :], pD[:c, :].rearrange("p (h v) -> p h v", h=4), [pD], [delta])
                for h in range(4):
                    hc = slice(h * c, (h + 1) * c)
                    TT(qgT[:, hc], qT_[:, h, cols], EGb[:, hc], ALU.mult, [qT_, EGb], [qgT])
                pO = bank()
                for h in range(4):
                    hc = slice(h * c, (h + 1) * c)
                    MM(pO[:c, h * 128:(h + 1) * 128], qgT[:, hc], S[:, h, :], True, False, [qgT, S], [pO])
                    MM(pO[:c, h * 128:(h + 1) * 128], MTm[:c, hc], delta[:c, h, :], False, True, [MTm, delta], [pO])
                pS = bank()
                for h in range(4):
                    MM(pS[:, h * 128:(h + 1) * 128], kd[:c, h, :], delta[:c, h, :], True, True, [kd, delta], [pS])
                for h in range(4):
                    STT(S[:, h, :], S[:, h, :], EGb[:, h * c + c - 1:h * c + c], pS[:, h * 128:(h + 1) * 128], ALU.mult, ALU.add, [S, EGb, pS], [S])
                for h in range(4):
                    ACT(junk[:c, :], pO[:c, h * 128:(h + 1) * 128], AF.Square, [pO], [junk, sm], accum_out=sm[:c, 12 + h:13 + h])
                ACT(sm[:c, 16:20], sm[:c, 12:16], AF.Sqrt, [sm], [sm], bias=EPS, scale=1.0 / 128)
                RCP(sm[:c, 16:20], sm[:c, 16:20], [sm], [sm])
                for h in range(4):
                    STT(ytok[:c, h * 128:(h + 1) * 128], pO[:c, h * 128:(h + 1) * 128], sm[:c, 16 + h:17 + h], zg[:c, bi, h * 128:(h + 1) * 128],
                        ALU.mult, ALU.mult, [pO, sm, zg], [ytok])
                pT = bank()
                for h in range(4):
                    MM(pT[:, h * c:(h + 1) * c], ytok[:c, h * 128:(h + 1) * 128], ident_b[:c, :c], True, True, [ytok, cbf], [pT])
                for h in range(4):
                    CP(oT[h][:, cols], pT[:, h * c:(h + 1) * c], [pT], [oT[h]], eng="act" if h % 2 else "dve")
        if is_last_tile:
            for si, sgm in enumerate(segs):
                sq_ = sgm["seq"]
                DMA("pool", sgm["gs_out"][l].rearrange("h k v -> k h v"), sq_.S[l][:], [sq_.S[l]], [])
                DMA("pool", sgm["gc_out"][l].rearrange("p (c j) -> p c j", j=3), sq_.gch[l][:], [sq_.gch[l]], [])

        fence()
        QT = alloc([128, 8, T], BF16, "QT")
        KT = alloc([128, 8, T], BF16, "KT")
        Va = alloc([128, nseg * nsub, 8, 129], BF16, "Va")
        stg = [alloc([128, 512], F32, "stg%d" % i) for i in range(3)]
        strr = [0]
        MEMSET(Va[:, :, :, 128:129], 1.0, [Va], eng="pool")
        for half in range(2):
            w = wload((l, "fq", half), win[:, 2056 + half * 512:2056 + (half + 1) * 512], 16, 512)
            fm_cols(w, T, 4, lambda cc, pb, half=half: ACT(QT[:, half * 4 + cc, :], pb[:, :T], AF.Copy, [pb], [QT], scale=128.0 ** -0.5))
        for half in range(2):
            w = wload((l, "fk", half), win[:, 3080 + half * 512:3080 + (half + 1) * 512], 16, 512)
            fm_cols(w, T, 4, lambda cc, pb, half=half: CP(KT[:, half * 4 + cc, :], pb[:, :T], [pb], [KT]))
            for si, sgm in enumerate(segs):
                for s in range(nsub):
                    cols = slice(sgm["col0"] + s * c, sgm["col0"] + (s + 1) * c)

                    def evk(pb, sgm=sgm, s=s, half=half):
                        st = stg[strr[0] % 3]
                        strr[0] += 1
                        ACT(st[:c, :], pb[:c, :], AF.Copy, [pb], [st])
                        DMA("pool", sgm["kout"][l][s * c:(s + 1) * c, half * 512:(half + 1) * 512], st[:c, :], [st], [sgm["kbuf"][l]])
                    tm_cols(w, cols, 512, c, evk)
        for half in range(2):
            w = wload((l, "fv", half), win[:, 4104 + half * 512:4104 + (half + 1) * 512], 16, 512)
            for si, sgm in enumerate(segs):
                for s in range(nsub):
                    cols = slice(sgm["col0"] + s * c, sgm["col0"] + (s + 1) * c)
                    bi = si * nsub + s

                    def evv(pb, sgm=sgm, s=s, half=half, bi=bi):
                        st = stg[strr[0] % 3]
                        strr[0] += 1
                        ACT(st[:c, :], pb[:c, :], AF.Copy, [pb], [st])
                        CP(Va[:c, bi, half * 4:(half + 1) * 4, 0:128], pb[:c, :].rearrange("p (h d) -> p h d", h=4), [pb], [Va])
                        DMA("pool", sgm["vout"][l][s * c:(s + 1) * c, half * 512:(half + 1) * 512], st[:c, :], [st], [sgm["vbuf"][l]])
                    tm_cols(w, cols, 512, c, evv)
        lf = alloc([128, nseg * nsub, 8], F32, "lf")
        t8 = alloc([128, 16], F32, "t8")
        for si, sgm in enumerate(segs):
            sq_ = sgm["seq"]
            Fb = sq_.Fb[l]
            if not sq_.prompt:
                nblk = PB + 1
                sq_.Fk_l = alloc([128, nblk, 8], F32, "sFk")
                sq_.Fref_l = alloc([128, nblk, 8], F32, "sFref")
                clg = alloc([128, PB, 8], F32, "clg")
                for q4 in range(0, PB, 8):
                    n = min(8, PB - q4)
                    DMA("pool", clg[:, q4:q4 + n, :], cl[l, sq_.sidx, q4 * 128:(q4 + n) * 128, :].rearrange("(b p) h -> p b h", p=128), [], [clg])
                for kb in range(PB):
                    pf = bank()
                    MM(pf[:, 0:8], tri_f, clg[:, kb, :], True, True, [cst, clg], [pf])
                    MM(pf[:, 8:16], ones_f, clg[:, kb, :], True, True, [cst, clg], [pf])
                    TT(sq_.Fk_l[:, kb, :], pf[:, 0:8], Fb[:, :], ALU.add, [pf, Fb], [sq_.Fk_l])
                    TT(Fb[:, :], Fb[:, :], pf[:, 8:16], ALU.add, [Fb, pf], [Fb])
                Fk, Fref = sq_.Fk_l, sq_.Fref_l
            else:
                Fk, Fref = sq_.Fk[l], sq_.Fref[l]
            for s in range(nsub):
                bi = si * nsub + s
                blk = sgm["blk0"] + s
                TT(t8[:c, 0:8], ffr[:c, bi, :], pkl[:c, PK_FB:PK_FB + 8], ALU.add, [ffr, pkl], [t8])
                ACT(t8[:c, 8:16], t8[:c, 0:8], AF.Exp, [t8], [t8], scale=-1.0)
                ACT(t8[:c, 0:8], t8[:c, 8:16], AF.Ln, [t8], [t8], bias=1.0)
                TS(lf[:c, bi, :], t8[:c, 0:8], -1.0, None, ALU.mult, None, [t8], [lf])
                DMA("pool", sgm["lout"][l][s * c:(s + 1) * c, :], lf[:c, bi, :], [lf], [])
                pf = bank()
                MM(pf[:c, 0:8], tri_f[:c, :c], lf[:c, bi, :], True, True, [cst, lf], [pf])
                MM(pf[:, 8:16], ones_f[:c, :], lf[:c, bi, :], True, True, [cst, lf], [pf])
                CP(Fref[:, blk, :], Fb[:, :], [Fb], [Fref])
                TT(Fk[:c, blk, :], pf[:c, 0:8], Fb[:c, :], ALU.add, [pf, Fb], [Fk])
                TT(Fb[:, :], Fb[:, :], pf[:, 8:16], ALU.add, [Fb, pf], [Fb])
        Kh = [alloc([128, 4, 128], BF16, "Kh%d" % i) for i in range(2)]
        Vh = [alloc([128, 4, 129], BF16, "Vh%d" % i) for i in range(2)]
        for v_ in Vh:
            MEMSET(v_[:, :, 128:129], 1.0, [v_], eng="pool")
        KTh = [alloc([128, 512], BF16, "KTh%d" % i) for i in range(2)]
        PT = [alloc([128, 512], BF16, "PT%d" % i) for i in range(3)]
        bias4 = [alloc([128, 4], F32, "bias4_%d" % i) for i in range(4)]
        yf = alloc([128, nseg * nsub, 1024], BF16, "yf")
        fsm = alloc([128, 16], F32, "fsm")
        hrr = [0]
        prr = [0]
        brr = [0]
        for si, sgm in enumerate(segs):
            sq_ = sgm["seq"]
            if sq_.prompt:
                Fk, Fref = sq_.Fk[l], sq_.Fref[l]
                nhist = sgm["blk0"] // 4
            else:
                Fk, Fref = sq_.Fk_l, sq_.Fref_l
                nhist = PB // 4
            q0 = sgm["col0"]
            qb0 = sgm["blk0"]
            for h in range(8):
                started = [False] * nsub
                Oacc = [ps[4 + qs] for qs in range(nsub)]
                for j in range(nhist):
                    kh = Kh[hrr[0] % 2]
                    vh = Vh[hrr[0] % 2]
                    kth = KTh[hrr[0] % 2]
                    hrr[0] += 1
                    if sq_.prompt:
                        ksrc = fk_p[l, j * 512:(j + 1) * 512, h * 128:(h + 1) * 128]
                        vsrc = fv_p[l, j * 512:(j + 1) * 512, h * 128:(h + 1) * 128]
                        kdep, vdep = [kvK[l][j]], [kvV[l][j]]
                    else:
                        ksrc = ck[l, sq_.sidx, j * 512:(j + 1) * 512, h * 128:(h + 1) * 128]
                        vsrc = cv[l, sq_.sidx, j * 512:(j + 1) * 512, h * 128:(h + 1) * 128]
                        kdep, vdep = [], []
                    DMA("pool", kh[:, :, :], ksrc.rearrange("(b p) d -> p b d", p=128), kdep, [kh])
                    DMA("pool", vh[:, :, 0:128], vsrc.rearrange("(b p) d -> p b d", p=128), vdep, [vh])
                    ptr = ps[3]
                    for kb in range(4):
                        MM(ptr[:, kb * 128:(kb + 1) * 128], kh[:, kb, :], ident_b, True, True, [kh, cbf], [ptr])
                    CP(kth[:, :], ptr[:, :], [ptr], [kth])
                    for kb in range(4):
                        kblk = j * 4 + kb
                        pst = bank((0, 1, 2))
                        MM(pst[:, :L], kth[:, kb * 128:(kb + 1) * 128], QT[:, h, q0:q0 + L], True, True, [kth, QT], [pst])
                        b4 = bias4[brr[0] % 4]
                        brr[0] += 1
                        TS(b4[:, 0:nsub], Fref[:, qb0:qb0 + nsub, h], Fk[:, kblk, h:h + 1], None, ALU.subtract, None, [Fref, Fk], [b4])
                        pt = PT[prr[0] % 3]
                        prr[0] += 1
                        for qs in range(nsub):
                            ACT(pt[:, qs * c:(qs + 1) * c], pst[:, qs * c:(qs + 1) * c], AF.Exp, [pst, b4], [pt], bias=b4[:, qs:qs + 1])
                        for qs in range(nsub):
                            MM(Oacc[qs][:c, 0:129], pt[:, qs * c:(qs + 1) * c], vh[:, kb, :], not started[qs], False, [pt, vh], [Oacc[qs]])
                            started[qs] = True
                for kb in range(nsub):
                    bi = si * nsub + kb
                    kblk = qb0 + kb
                    nq = nsub - kb
                    pst = bank((0, 1, 2))
                    MM(pst[:c, :nq * c], KT[:, h, q0 + kb * c:q0 + (kb + 1) * c], QT[:, h, q0 + kb * c:q0 + L], True, True, [KT, QT], [pst])
                    b4 = bias4[brr[0] % 4]
                    brr[0] += 1
                    TS(b4[:c, 0:nq], Fref[:c, qb0 + kb:qb0 + nsub, h], Fk[:c, kblk, h:h + 1], None, ALU.subtract, None, [Fref, Fk], [b4])
                    pt = PT[prr[0] % 3]
                    prr[0] += 1
                    for qi in range(nq):
                        ACT(pt[:c, qi * c:(qi + 1) * c], pst[:c, qi * c:(qi + 1) * c], AF.Exp, [pst, b4], [pt], bias=b4[:c, qi:qi + 1])
                    TT(pt[:c, 0:c], pt[:c, 0:c], tri_b[:c, :c], ALU.mult, [pt, cbf], [pt])
                    for qi in range(nq):
                        qs = kb + qi
                        MM(Oacc[qs][:c, 0:129], pt[:c, qi * c:(qi + 1) * c], Va[:c, bi, h, :], not started[qs], qs == kb, [pt, Va], [Oacc[qs]])
                        started[qs] = True
                for qs in range(nsub):
                    bi = si * nsub + qs
                    O = Oacc[qs]
                    RCP(fsm[:c, 0:1], O[:c, 128:129], [O], [fsm])
                    ACT(junk[:c, :], O[:c, 0:128], AF.Square, [O, fsm], [junk, fsm], scale=fsm[:c, 0:1], accum_out=fsm[:c, 1:2])
                    ACT(fsm[:c, 2:3], fsm[:c, 1:2], AF.Sqrt, [fsm], [fsm], bias=EPS, scale=1.0 / 128)
                    RCP(fsm[:c, 2:3], fsm[:c, 2:3], [fsm], [fsm])
                    TT(fsm[:c, 3:4], fsm[:c, 2:3], fsm[:c, 0:1], ALU.mult, [fsm], [fsm])
                    STT(yf[:c, bi, h * 128:(h + 1) * 128], O[:c, 0:128], fsm[:c, 3:4], pkl[:c, PK_FNG + h * 128:PK_FNG + (h + 1) * 128],
                        ALU.mult, ALU.mult, [O, fsm, pkl], [yf])
            for qs in range(nsub):
                bi = si * nsub + qs
                cols = slice(q0 + qs * c, q0 + (qs + 1) * c)
                for hg in range(2):
                    pT = bank((0, 1, 2, 3))
                    for hh in range(4):
                        h = hg * 4 + hh
                        MM(pT[:, hh * c:(hh + 1) * c], yf[:c, bi, h * 128:(h + 1) * 128], ident_b[:c, :c], True, True, [yf, cbf], [pT])
                    for hh in range(4):
                        h = hg * 4 + hh
                        CP(oT[4 + h][:, cols], pT[:, hh * c:(hh + 1) * c], [pT], [oT[4 + h]], eng="act" if hh % 2 else "dve")

        fence()
        lxr = alloc([128, 4, nseg, 3 + L], F32, "lxr")
        gel = alloc([128, 4, T], BF16, "gel")
        xc = alloc([128, 4, nseg, L], F32, "xc")
        xcb = alloc([128, 4, T], BF16, "xcb")
        lt = [alloc([128, T], F32, "lt%d" % i) for i in range(6)]
        hs = alloc([128, nseg, L], F32, "hs")
        w = wload((l, "lx"), win[:, 5136:5648], 16, 512)

        def evlx(cc, pb):
            for si, sgm in enumerate(segs):
                lch = sgm["seq"].lch[l]
                CP(lxr[:, cc, si, 0:3], lch[:, cc, :], [lch], [lxr], eng="pool")
            P.add("act", lambda e: e.activation(out=lxr[:, cc, :, 3:3 + L], in_=pb[:, :T].rearrange("p (s l) -> p s l", s=nseg), func=AF.Copy),
                  bl([pb]) + [regbuf], bl([lxr]), guard=False)
            for si, sgm in enumerate(segs):
                lch = sgm["seq"].lch[l]
                CP(lch[:, cc, :], lxr[:, cc, si, L:L + 3], [lxr], [lch], eng="pool")
        fm_cols(w, T, 4, evlx)
        w = wload((l, "lg"), win[:, 5648:6160], 16, 512)

        def evg(cc, pb):
            g1 = lt[cc % 2]
            ACT(g1[:, :T], pb[:, :T], AF.Square, [pb], [g1])
            TS(g1[:, :T], g1[:, :T], 0.044715, 1.0, ALU.mult, ALU.add, [g1], [g1])
            TT(g1[:, :T], g1[:, :T], pb[:, :T], ALU.mult, [g1, pb], [g1])
            ACT(g1[:, :T], g1[:, :T], AF.Sigmoid, [g1], [g1], scale=1.5957691216057308)
            TT(gel[:, cc, :], g1[:, :T], pb[:, :T], ALU.mult, [g1, pb], [gel])
        fm_cols(w, T, 4, evg)
        for n in range(4):
            cw = PK_LCW + n * 4
            TS(xc[:, n, :, :], lxr[:, n, :, 0:L], pkl[:, cw:cw + 1], pkl[:, PK_LCB + n:PK_LCB + n + 1], ALU.mult, ALU.add, [lxr, pkl], [xc])
            for j in range(1, 4):
                STT(xc[:, n, :, :], lxr[:, n, :, j:j + L], pkl[:, cw + j:cw + j + 1], xc[:, n, :, :], ALU.mult, ALU.add, [lxr, pkl, xc], [xc])
            xcn = xc[:, n, :, :].rearrange("p s l -> p (s l)")
            ACT(xcb[:, n, :], xcn, AF.Copy, [xc], [xcb])
            pr = bank()
            MM(pr[:, :T], lw[l][0][:, n, :], xcb[:, n, :], True, True, [lw[l][0], xcb], [pr])
            pi = bank()
            MM(pi[:, :T], lw[l][1][:, n, :], xcb[:, n, :], True, True, [lw[l][1], xcb], [pi])
            r_, i_, a_, a2_, u_, t_ = lt
            ACT(r_[:, :T], pr[:, :T], AF.Sigmoid, [pr, pkl], [r_], bias=pkl[:, PK_LBA + n:PK_LBA + n + 1])
            ACT(i_[:, :T], pi[:, :T], AF.Sigmoid, [pi, pkl], [i_], bias=pkl[:, PK_LBX + n:PK_LBX + n + 1])
            ACT(a_[:, :T], r_[:, :T], AF.Exp, [r_, c1[l]], [a_], scale=c1[l][:, n:n + 1])
            TT(a2_[:, :T], a_[:, :T], a_[:, :T], ALU.mult, [a_], [a2_])
            ACT(a2_[:, :T], a2_[:, :T], AF.Sqrt, [a2_], [a2_], scale=-1.0, bias=1.0)
            TT(u_[:, :T], i_[:, :T], xcn, ALU.mult, [i_, xc], [u_])
            TT(u_[:, :T], u_[:, :T], a2_[:, :T], ALU.mult, [u_, a2_], [u_])
            for si, sgm in enumerate(segs):
                lh = sgm["seq"].lh[l]
                c0 = sgm["col0"]
                P.add("dve", lambda e, si=si, c0=c0, lh=lh, n=n: e.tensor_tensor_scan(out=hs[:, si, :], data0=a_[:, c0:c0 + L], data1=u_[:, c0:c0 + L],
                                                                                initial=lh[:, n:n + 1], op0=ALU.mult, op1=ALU.add),
                      bl([a_, u_, lh]), bl([hs]))
                CP(lh[:, n:n + 1], hs[:, si, L - 1:L], [hs], [lh])
            hsn = hs[:].rearrange("p s l -> p (s l)")
            sq = sqg[0]
            ACT(sq[:, :T], hsn, AF.Square, [hs], [sq])
            pn = bank()
            MM(pn[:, :T], ones_b, sq[:, :T], True, True, [sq, cbf], [pn])
            ACT(t_[:, :T], pn[:, :T], AF.Sqrt, [pn], [t_], bias=EPS, scale=1.0 / 128)
            RCP(t_[:, :T], t_[:, :T], [t_], [t_])
            STT(t_[:, :T], hsn, pkl[:, PK_LNG + n:PK_LNG + n + 1], t_[:, :T], ALU.mult, ALU.mult, [hs, pkl, t_], [t_])
            TT(oT[12 + n][:, :T], t_[:, :T], gel[:, n, :], ALU.mult, [t_, gel], [oT[12 + n]])
        if is_last_tile:
            for si, sgm in enumerate(segs):
                sq_ = sgm["seq"]
                DMA("pool", sgm["lh_out"][l], sq_.lh[l][:], [sq_.lh[l]], [])
                DMA("pool", sgm["lc_out"][l].rearrange("p (c j) -> p c j", j=3), sq_.lch[l][:], [sq_.lch[l]], [])

        fence()
        for g in range(4):
            w = wload((l, "wo", g), w_out[l][:, g * 512:(g + 1) * 512], 16, 512)
            fm_cols(w, T, 4, lambda cc, pb, g=g: TT(xT[g * 4 + cc][:, :T], xT[g * 4 + cc][:, :T], pb[:, :T], ALU.add, [xT[g * 4 + cc], pb], [xT[g * 4 + cc]]),
                    src=oT)

        fence()
        rmsnorm_to_hT(T, PK_GFFN, l)
        actT = alloc([128, 44, T], BF16, "actT")
        gpr = [alloc([128, nseg, 2 + L], F32, "gpr%d" % i) for i in range(2)]
        gcv = [alloc([128, nseg, L], F32, "gcv%d" % i) for i in range(2)]
        sg_ = [alloc([128, T], F32, "sg%d" % i) for i in range(2)]
        for j in range(11):
            wg = wload((l, "ug", j), w_up[l][:, j * 512:(j + 1) * 512], 16, 512)
            wv = wload((l, "uv", j), w_up[l][:, DFF + j * 512:DFF + (j + 1) * 512], 16, 512)
            for cc in range(4):
                m = j * 4 + cc
                pg = bank()
                pv = bank()
                for k in range(16):
                    MM(pg[:, :T], wg[:, k, cc * 128:(cc + 1) * 128], hT[k][:, :T], k == 0, k == 15, [wg, hT[k]], [pg])
                for k in range(16):
                    MM(pv[:, :T], wv[:, k, cc * 128:(cc + 1) * 128], hT[k][:, :T], k == 0, k == 15, [wv, hT[k]], [pv])
                gp = gpr[m % 2]
                gc_ = gcv[m % 2]
                s_ = sg_[m % 2]
                for si, sgm in enumerate(segs):
                    fch = sgm["seq"].fch[l]
                    CP(gp[:, si, 0:2], fch[:, m, :], [fch], [gp], eng="pool")
                P.add("act", lambda e, gp=gp, pg=pg: e.activation(out=gp[:, :, 2:2 + L], in_=pg[:, :T].rearrange("p (s l) -> p s l", s=nseg), func=AF.Copy),
                      bl([pg]) + [regbuf], bl([gp]), guard=False)
                for si, sgm in enumerate(segs):
                    fch = sgm["seq"].fch[l]
                    CP(fch[:, m, :], gp[:, si, L:L + 2], [gp], [fch], eng="pool")
                cw = PK_FCW + m * 3
                TS(gc_[:], gp[:, :, 0:L], pkl[:, cw:cw + 1], None, ALU.mult, None, [gp, pkl], [gc_])
                for jj in range(1, 3):
                    STT(gc_[:], gp[:, :, jj:jj + L], pkl[:, cw + jj:cw + jj + 1], gc_[:], ALU.mult, ALU.add, [gp, pkl, gc_], [gc_])
                ACT(s_[:, :T], gc_[:].rearrange("p s l -> p (s l)"), AF.Silu, [gc_], [s_])
                TT(actT[:, m, :], s_[:, :T], pv[:, :T], ALU.mult, [s_, pv], [actT])
        if is_last_tile:
            for si, sgm in enumerate(segs):
                sq_ = sgm["seq"]
                DMA("pool", sgm["fc_out"][l].rearrange("p (c j) -> p c j", j=2), sq_.fch[l][:], [sq_.fch[l]], [])
        kgroups = [(0, 16), (16, 16), (32, 12)]
        for g in range(4):
            acc = [ps[4 + cc] for cc in range(4)]
            for gi, (k0, nk) in enumerate(kgroups):
                w = wload((l, "dn", g, gi), w_dn[l][k0 * 128:(k0 + nk) * 128, g * 512:(g + 1) * 512], nk, 512)
                for cc in range(4):
                    for k in range(nk):
                        MM(acc[cc][:, :T], w[:, k, cc * 128:(cc + 1) * 128], actT[:, k0 + k, :], (gi == 0 and k == 0), (gi == 2 and k == nk - 1),
                           [w, actT], [acc[cc]])
            for cc in range(4):
                xk = xT[g * 4 + cc]
                TT(xk[:, :T], xk[:, :T], acc[cc][:, :T], ALU.add, [xk, acc[cc]], [xk])

        if last_layer:
            fence()
            pb = bank()
            for k in range(16):
                sq = sqs[k % 2]
                ACT(sq[:, :T], xT[k][:, :T], AF.Square, [xT[k]], [sq])
                MM(pb[:, :T], ones_b, sq[:, :T], k == 0, k == 15, [sq, cbf], [pb])
            ACT(rstd[:, :T], pb[:, :T], AF.Sqrt, [pb], [rstd], bias=EPS, scale=1.0 / D)
            RCP(rstd[:, :T], rstd[:, :T], [rstd], [rstd])
            yT = [alloc([128, T], F32, "yT%d" % i) for i in range(4)]
            ytk = [alloc([128, 512], F32, "ytk%d" % i) for i in range(3)]
            yrr = [0]
            for g in range(4):
                for cc in range(4):
                    k = g * 4 + cc
                    STT(yT[cc][:, :T], xT[k][:, :T], pkl[:, PK_GFIN + k:PK_GFIN + k + 1], rstd[:, :T], ALU.mult, ALU.mult, [xT[k], pkl, rstd], [yT[cc]])
                for si, sgm in enumerate(segs):
                    for s in range(nsub):
                        t0 = sgm["col0"] + s * c
                        pT = bank()
                        for cc in range(4):
                            MM(pT[:c, cc * 128:(cc + 1) * 128], yT[cc][:, t0:t0 + c], ident_f, True, True, [yT[cc], cst], [pT])
                        yt = ytk[yrr[0] % 3]
                        yrr[0] += 1
                        CP(yt[:c, :], pT[:c, :], [pT], [yt], eng="act" if yrr[0] % 2 else "dve")
                        DMA("pool", sgm["yout"][s * c:(s + 1) * c, g * 512:(g + 1) * 512], yt[:c, :], [yt], [])

    def load_x(segs, c, nsub):
        xs = [alloc([128, D], F32, "xs%d" % i) for i in range(2)]
        rr = 0
        for si, sgm in enumerate(segs):
            for s in range(nsub):
                t0 = sgm["col0"] + s * c
                st = xs[rr % 2]
                rr += 1
                DMA("pool", st[:c, :], sgm["xin"][s * c:(s + 1) * c, :], [], [st])
                for g in range(4):
                    pT = bank()
                    for cc in range(4):
                        k = g * 4 + cc
                        MM(pT[:, cc * c:(cc + 1) * c], st[:c, k * 128:(k + 1) * 128], ident_f[:c, :c], True, True, [st, cst], [pT])
                    for cc in range(4):
                        k = g * 4 + cc
                        CP(xT[k][:, t0:t0 + c], pT[:, cc * c:(cc + 1) * c], [pT], [xT[k]], eng="act" if cc % 2 else "dve")

    for i in range(NT):
        sgm = dict(seq=seq_p, col0=0, L=TP, blk0=i * (TP // 128),
                   xin=x_p[i * TP:(i + 1) * TP, :], yout=y_p[i * TP:(i + 1) * TP, :],
                   kout=[fk_p[l, i * TP:(i + 1) * TP, :] for l in range(DEPTH)],
                   vout=[fv_p[l, i * TP:(i + 1) * TP, :] for l in range(DEPTH)],
                   lout=[fl_p[l, i * TP:(i + 1) * TP, :] for l in range(DEPTH)],
                   kbuf=[kvK[l][i] for l in range(DEPTH)], vbuf=[kvV[l][i] for l in range(DEPTH)],
                   gs_out=[gs_p[l] for l in range(DEPTH)], gc_out=[gc_p[l] for l in range(DEPTH)],
                   lh_out=[lh_p[l] for l in range(DEPTH)], lc_out=[lc_p[l] for l in range(DEPTH)], fc_out=[fc_p[l] for l in range(DEPTH)])
        for l in range(DEPTH):
            run_tile([sgm], TP, 128, l, (lambda sgm=sgm: load_x([sgm], 128, TP // 128)) if l == 0 else None,
                     l == DEPTH - 1, None, i == NT - 1)
    dummyK = [Buf("dk") for _ in range(DEPTH)]
    dummyV = [Buf("dv") for _ in range(DEPTH)]
    ssegs = []
    for si in range(2):
        ssegs.append(dict(seq=seq_s[si], col0=si * LS, L=LS, blk0=PB,
                          xin=x_s[si], yout=y_s[si],
                          kout=[fk_s[l, si] for l in range(DEPTH)], vout=[fv_s[l, si] for l in range(DEPTH)],
                          lout=[fl_s[l, si] for l in range(DEPTH)], kbuf=dummyK, vbuf=dummyV,
                          gs_out=[gs_s[l, si] for l in range(DEPTH)], gc_out=[gc_s[l, si] for l in range(DEPTH)],
                          lh_out=[lh_s[l, si] for l in range(DEPTH)], lc_out=[lc_s[l, si] for l in range(DEPTH)],
                          fc_out=[fc_s[l, si] for l in range(DEPTH)]))
    for l in range(DEPTH):
        run_tile(ssegs, 2 * LS, LS, l, (lambda: load_x(ssegs, LS, 1)) if l == 0 else None, l == DEPTH - 1, None, True)
    P.emit()
    return nc


def _pack_small(inp, l):
    pk = np.zeros((128, NPK), np.float32)

    def fm(v, n):
        return np.ascontiguousarray(v.reshape(n, 128).T)
    pk[:, PK_GMIX:PK_GMIX + 16] = fm(inp["norm_mix_g"][l], 16)
    pk[:, PK_GFFN:PK_GFFN + 16] = fm(inp["norm_ffn_g"][l], 16)
    pk[:, PK_GFIN:PK_GFIN + 16] = fm(inp["final_norm_g"], 16)
    pk[:, PK_GCW:PK_GCW + 48] = inp["gdn_conv_w"][l].reshape(4, 12, 128).transpose(2, 1, 0).reshape(128, 48)
    pk[:, PK_LCW:PK_LCW + 16] = inp["lru_conv_w"][l].reshape(4, 4, 128).transpose(2, 1, 0).reshape(128, 16)
    pk[:, PK_LCB:PK_LCB + 4] = fm(inp["lru_conv_b"][l], 4)
    pk[:, PK_LBA:PK_LBA + 4] = fm(inp["lru_b_a"][l], 4)
    pk[:, PK_LBX:PK_LBX + 4] = fm(inp["lru_b_x"][l], 4)
    pk[:, PK_LAM:PK_LAM + 4] = fm(inp["lru_lambda"][l], 4)
    pk[:, PK_LNG:PK_LNG + 4] = fm(inp["lru_norm_g"][l], 4)
    pk[:, PK_FCW:PK_FCW + 132] = inp["ffn_conv_w"][l].reshape(3, 44, 128).transpose(2, 1, 0).reshape(128, 132)
    pk[:, PK_GNG4:PK_GNG4 + 512] = np.tile(inp["gdn_norm_g"][l], 4)[None, :]
    pk[:, PK_FNG:PK_FNG + 1024] = inp["fox_norm_g"][l].reshape(1024)[None, :]
    pk[:, PK_ALOG:PK_ALOG + 4] = inp["gdn_a_log"][l][None, :]
    pk[:, PK_DTB:PK_DTB + 4] = inp["gdn_dt_bias"][l][None, :]
    pk[:, PK_FB:PK_FB + 8] = inp["fox_f_bias"][l][None, :]
    return pk


def _consts():
    c = np.zeros((128, NCST), np.float32)
    c[:, C_ID:C_ID + 128] = np.eye(128)
    c[:, C_TRI:C_TRI + 128] = np.triu(np.ones((128, 128)))
    c[:, C_ONE:C_ONE + 128] = 1.0
    c[:, C_MLS:C_MLS + 128] = np.tril(np.ones((128, 128)), -1)
    return c


_NC_CACHE = {}


def run(inp, n_cores=8, SEQ=4096, PAST=4096):
    inp = {k: np.asarray(v) for k, v in inp.items()}
    DEPTH = 2
    key = (SEQ, PAST)
    if key not in _NC_CACHE:
        _NC_CACHE[key] = build(SEQ=SEQ, PAST=PAST)
    nc = _NC_CACHE[key]
    BP = inp["x_prompt"].shape[0]
    BS = inp["x_sample"].shape[0]
    pk = np.stack([_pack_small(inp, l) for l in range(DEPTH)])
    cst = _consts()
    c32 = np.ascontiguousarray

    def fmT(a, n, j):
        sh = a.shape[:-2]
        return c32(a.reshape(sh + (j, n, 128)).transpose(tuple(range(len(sh))) + (len(sh) + 2, len(sh) + 1, len(sh))).reshape(sh + (128, n * j)))

    in_maps = []
    for core in range(n_cores):
        b = core % BP
        ss = [(2 * (core % (BS // 2))), (2 * (core % (BS // 2)) + 1)]
        m = {
            "x_p": c32(inp["x_prompt"][b]),
            "x_s": c32(inp["x_sample"][ss]),
            "ck": c32(inp["cache_fox_k"][:, ss].reshape(DEPTH, 2, PAST, 1024)),
            "cv": c32(inp["cache_fox_v"][:, ss].reshape(DEPTH, 2, PAST, 1024)),
            "cl": c32(inp["cache_fox_logf"][:, ss]),
            "sg": c32(inp["state_gdn"][:, ss]),
            "sgc": fmT(inp["state_gdn_conv"][:, ss], 12, 3),
            "sl": fmT(inp["state_lru"][:, ss][:, :, None, :], 4, 1),
            "slc": fmT(inp["state_lru_conv"][:, ss], 4, 3),
            "sfc": fmT(inp["state_ffn_conv"][:, ss], 44, 2),
            "w_in": inp["w_in"], "w_out": inp["w_out"], "w_up": inp["ffn_w_up"], "w_dn": inp["ffn_w_down"],
            "lwa": inp["lru_w_a"], "lwx": inp["lru_w_x"], "pk": pk, "cst": cst,
        }
        in_maps.append(m)
    res = run_bass_kernel_spmd(nc, in_maps, core_ids=list(range(n_cores)))
    R = res.results

    def unfm(a, n, j):
        sh = a.shape[:-2]
        return c32(a.reshape(sh + (128, n, j)).transpose(tuple(range(len(sh))) + (len(sh) + 2, len(sh) + 1, len(sh))).reshape(sh + (j, n * 128)))

    pc = list(range(BP))
    sc = list(range(BS // 2))
    y_p = np.stack([R[c]["y_p"] for c in pc])
    y_s = np.concatenate([R[c]["y_s"] for c in sc])
    fk_p = np.stack([R[c]["fk_p"] for c in pc], 1).reshape(DEPTH, BP, SEQ, 8, 128)
    fv_p = np.stack([R[c]["fv_p"] for c in pc], 1).reshape(DEPTH, BP, SEQ, 8, 128)
    fl_p = np.stack([R[c]["fl_p"] for c in pc], 1)
    gs_p = np.stack([R[c]["gs_p"] for c in pc], 1)
    gc_p = unfm(np.stack([R[c]["gc_p"] for c in pc], 1), 12, 3)
    lh_p = unfm(np.stack([R[c]["lh_p"] for c in pc], 1), 4, 1)[:, :, 0, :]
    lc_p = unfm(np.stack([R[c]["lc_p"] for c in pc], 1), 4, 3)
    fc_p = unfm(np.stack([R[c]["fc_p"] for c in pc], 1), 44, 2)
    fk_s = np.concatenate([R[c]["fk_s"] for c in sc], 1).reshape(DEPTH, BS, 16, 8, 128)
    fv_s = np.concatenate([R[c]["fv_s"] for c in sc], 1).reshape(DEPTH, BS, 16, 8, 128)
    fl_s = np.concatenate([R[c]["fl_s"] for c in sc], 1)
    gs_s = np.concatenate([R[c]["gs_s"] for c in sc], 1)
    gc_s = unfm(np.concatenate([R[c]["gc_s"] for c in sc], 1), 12, 3)
    lh_s = unfm(np.concatenate([R[c]["lh_s"] for c in sc], 1), 4, 1)[:, :, 0, :]
    lc_s = unfm(np.concatenate([R[c]["lc_s"] for c in sc], 1), 4, 3)
    fc_s = unfm(np.concatenate([R[c]["fc_s"] for c in sc], 1), 44, 2)
    outs = (y_p, y_s, fk_p, fv_p, fl_p, gs_p, gc_p, lh_p, lc_p, fc_p, fk_s, fv_s, fl_s, gs_s, gc_s, lh_s, lc_s, fc_s)
    return tuple(np.ascontiguousarray(o, dtype=np.float32) for o in outs)


def kernel(**inputs):
    return run(inputs, n_cores=8, SEQ=4096, PAST=4096)
```

```python
import numpy as np
from contextlib import ExitStack
import concourse.bass as bass
import concourse.mybir as mybir
from concourse.bass_utils import run_bass_kernel_spmd

F32 = mybir.dt.float32
BF16 = mybir.dt.bfloat16
AF = mybir.ActivationFunctionType
ALU = mybir.AluOpType

ENGS = ("pe", "act", "dve", "pool", "sp")
D = 2048
DFF = 5632
INW = 6160
EPS = 1e-6
NPK = 1816
PK_GMIX, PK_GFFN, PK_GFIN, PK_GCW, PK_LCW, PK_LCB, PK_LBA, PK_LBX, PK_LAM, PK_LNG, PK_FCW = 0, 16, 32, 48, 96, 112, 116, 120, 124, 128, 132
PK_GNG4, PK_FNG, PK_ALOG, PK_DTB, PK_FB = 264, 776, 1800, 1804, 1808
C_ID, C_TRI, C_ONE, C_MLS = 0, 128, 256, 384
NCST = 512


class Buf:
    __slots__ = ("name", "w", "r", "excl")

    def __init__(self, name="", excl=False):
        self.name = name
        self.w = None
        self.r = {}
        self.excl = excl


class Op:
    __slots__ = ("eng", "fn", "deps", "dma", "tok", "need_sig", "idx")

    def __init__(self, eng, fn, dma):
        self.eng = eng
        self.fn = fn
        self.dma = dma
        self.deps = set()
        self.tok = None
        self.need_sig = False


class Prog:
    def __init__(self, nc, n_dma_sems=12):
        self.nc = nc
        self.ops = []
        self.n_dma_sems = n_dma_sems
        self.guard = None
        self.stopped = False
        self.nfence = 0

    def add(self, eng, fn, reads=(), writes=(), dma=False, guard=True):
        import os as _os
        if self.stopped or len(self.ops) >= int(_os.environ.get("STOPOPS", "100000000")):
            return None
        op = Op(eng, fn, dma)
        idx = len(self.ops)
        op.idx = idx
        reads = list(reads)
        if guard and self.guard is not None:
            reads.append(self.guard)
        for b in reads:
            if b.w is not None:
                op.deps.add(b.w)
            if b.excl:
                for key, ridx in b.r.items():
                    if key != eng:
                        op.deps.add(ridx)
        for b in writes:
            if b.w is not None:
                op.deps.add(b.w)
            for ridx in b.r.values():
                op.deps.add(ridx)
        op.deps.discard(idx)
        if eng == "pe" and not dma:
            op.deps = {d for d in op.deps if not (self.ops[d].eng == "pe" and not self.ops[d].dma)}
        for d in op.deps:
            self.ops[d].need_sig = True
        self.ops.append(op)
        for b in reads:
            key = ("d", idx) if dma else eng
            b.r[key] = idx
        for b in writes:
            b.w = idx
            b.r = {}
        return op

    def emit(self):
        nc = self.nc
        ops = self.ops
        nds = self.n_dma_sems
        dslot = {}
        dcount = {q: [0] * nds for q in ("sp", "pool")}
        drr = {q: 0 for q in ("sp", "pool")}
        last_on_slot = {}
        prewait = {}
        for op in ops:
            if op.dma:
                i = drr[op.eng] % nds
                drr[op.eng] += 1
                if (op.eng, i) in last_on_slot:
                    prewait[op.idx] = last_on_slot[(op.eng, i)]
                last_on_slot[(op.eng, i)] = op.idx
                dcount[op.eng][i] += 1
                dslot[op.idx] = (op.eng, i, dcount[op.eng][i] * 16)
        ordn = {}
        ecnt = {e: 0 for e in ENGS}
        prev_on = {e: None for e in ENGS}
        vc = [None] * len(ops)

        def merge(a, b):
            for kk, vv in b.items():
                if a.get(kk, 0) < vv:
                    a[kk] = vv
        for op in ops:
            v = {}
            for d in op.deps:
                merge(v, vc[d])
            if op.dma:
                q, i, val = dslot[op.idx]
                v[("d", q, i)] = max(v.get(("d", q, i), 0), val)
            else:
                ecnt[op.eng] += 1
                ordn[op.idx] = ecnt[op.eng]
                if prev_on[op.eng] is not None:
                    merge(v, vc[prev_on[op.eng]])
                v[("e", op.eng)] = ecnt[op.eng]
                prev_on[op.eng] = op.idx
            vc[op.idx] = v
        per_eng = {e: [op for op in ops if op.eng == e] for e in ENGS}

        def implied(known, d):
            if ops[d].dma:
                q, i, val = dslot[d]
                return known.get(("d", q, i), 0) >= val
            return known.get(("e", ops[d].eng), 0) >= ordn[d]

        def plan(e):
            known = {}
            out = []
            for op in per_eng[e]:
                ws = []
                cand = sorted(op.deps, reverse=True)
                if op.idx in prewait:
                    cand.append(prewait[op.idx])
                for d in cand:
                    if not implied(known, d):
                        ws.append(d)
                        merge(known, vc[d])
                out.append((op, ws))
            return out
        plans = {e: plan(e) for e in ENGS}
        for op in ops:
            op.need_sig = op.dma
        for e in ENGS:
            for op, ws in plans[e]:
                for d in ws:
                    ops[d].need_sig = True
        with ExitStack() as es:
            esem = {e: es.enter_context(nc.semaphore("s_" + e)) for e in ENGS}
            dsem = {q: [es.enter_context(nc.semaphore("d_%s%d" % (q, i))) for i in range(nds)] for q in ("sp", "pool")}
            ecount = {e: 0 for e in ENGS}
            for op in ops:
                if op.dma:
                    q, i, val = dslot[op.idx]
                    op.tok = (dsem[q][i], val)
                elif op.need_sig:
                    ecount[op.eng] += 1
                    op.tok = (esem[op.eng], ecount[op.eng])
            block = es.enter_context(nc.Block())

            def run(e, eng):
                for op, ws in plans[e]:
                    for d in ws:
                        sem, val = ops[d].tok
                        eng.wait_ge(sem, val)
                    ins = op.fn(eng)
                    if op.need_sig:
                        ins.then_inc(op.tok[0], 16 if op.dma else 1)
                if e == "sp":
                    for q in dsem:
                        for i in range(nds):
                            if dcount[q][i] > 0:
                                eng.wait_ge(dsem[q][i], dcount[q][i] * 16)

            @block.tensor
            def _(eng):
                run("pe", eng)

            @block.scalar
            def _(eng):
                run("act", eng)

            @block.vector
            def _(eng):
                run("dve", eng)

            @block.gpsimd
            def _(eng):
                run("pool", eng)

            @block.sync
            def _(eng):
                run("sp", eng)


class T_:
    def __init__(self, t, name, excl=False):
        self.t = t
        self.b = Buf(name, excl)

    def __getitem__(self, k):
        return self.t[k]


def build(SEQ=4096, PAST=4096, TP=512, DEPTH=2):
    nc = bass.Bass("TRN2", target_bir_lowering=False)
    NT = SEQ // TP
    PB = PAST // 128
    LS = 16

    def din(name, shape):
        return nc.dram_tensor(name, list(shape), F32, kind="ExternalInput").ap()

    def dout(name, shape):
        return nc.dram_tensor(name, list(shape), F32, kind="ExternalOutput").ap()

    x_p = din("x_p", [SEQ, D]); x_s = din("x_s", [2, LS, D])
    ck = din("ck", [DEPTH, 2, PAST, 1024]); cv = din("cv", [DEPTH, 2, PAST, 1024]); cl = din("cl", [DEPTH, 2, PAST, 8])
    sg = din("sg", [DEPTH, 2, 4, 128, 128]); sgc = din("sgc", [DEPTH, 2, 128, 36])
    sl = din("sl", [DEPTH, 2, 128, 4]); slc = din("slc", [DEPTH, 2, 128, 12]); sfc = din("sfc", [DEPTH, 2, 128, 88])
    w_in = din("w_in", [DEPTH, D, INW]); w_out = din("w_out", [DEPTH, D, D])
    w_up = din("w_up", [DEPTH, D, 2 * DFF]); w_dn = din("w_dn", [DEPTH, DFF, D])
    lwa = din("lwa", [DEPTH, 4, 128, 128]); lwx = din("lwx", [DEPTH, 4, 128, 128])
    pk_d = din("pk", [DEPTH, 128, NPK]); cst_d = din("cst", [128, NCST])

    y_p = dout("y_p", [SEQ, D]); y_s = dout("y_s", [2, LS, D])
    fk_p = dout("fk_p", [DEPTH, SEQ, 1024]); fv_p = dout("fv_p", [DEPTH, SEQ, 1024]); fl_p = dout("fl_p", [DEPTH, SEQ, 8])
    gs_p = dout("gs_p", [DEPTH, 4, 128, 128]); gc_p = dout("gc_p", [DEPTH, 128, 36]); lh_p = dout("lh_p", [DEPTH, 128, 4])
    lc_p = dout("lc_p", [DEPTH, 128, 12]); fc_p = dout("fc_p", [DEPTH, 128, 88])
    fk_s = dout("fk_s", [DEPTH, 2, LS, 1024]); fv_s = dout("fv_s", [DEPTH, 2, LS, 1024]); fl_s = dout("fl_s", [DEPTH, 2, LS, 8])
    gs_s = dout("gs_s", [DEPTH, 2, 4, 128, 128]); gc_s = dout("gc_s", [DEPTH, 2, 128, 36]); lh_s = dout("lh_s", [DEPTH, 2, 128, 4])
    lc_s = dout("lc_s", [DEPTH, 2, 128, 12]); fc_s = dout("fc_s", [DEPTH, 2, 128, 88])

    P = Prog(nc)
    sb_base = (nc.sbuf_base + 63) // 64 * 64
    sb_top = nc.sbuf_top
    cur = [sb_base]
    cnt = [0]

    def alloc(shape, dt, name="t"):
        nbytes = int(np.prod(shape[1:])) * (4 if dt == F32 else 2)
        nbytes = (nbytes + 63) // 64 * 64
        off = cur[0]
        cur[0] += nbytes
        assert cur[0] <= sb_top, ("SBUF overflow", name, cur[0], sb_top)
        cnt[0] += 1
        t = nc.alloc_sbuf_tensor_at("%s_%d" % (name, cnt[0]), list(shape), dt, offset=off)
        return T_(t, name)

    ps = []
    for i in range(8):
        ps.append(T_(nc.alloc_psum_tensor("ps%d" % i, [128, 512], F32), "ps%d" % i, excl=True))

    def bl(xs):
        return [x.b if isinstance(x, T_) else x for x in xs]

    def ACT(out, in_, func, R, W, **kw):
        P.add("act", lambda e: e.activation(out=out, in_=in_, func=func, **kw), bl(R), bl(W))

    def TT(out, in0, in1, op, R, W, eng="dve"):
        P.add(eng, lambda e: e.tensor_tensor(out=out, in0=in0, in1=in1, op=op), bl(R), bl(W))

    def TS(out, in0, s1, s2, op0, op1, R, W, eng="dve"):
        if s2 is None:
            P.add(eng, lambda e: e.tensor_scalar(out=out, in0=in0, scalar1=s1, scalar2=None, op0=op0), bl(R), bl(W))
        else:
            P.add(eng, lambda e: e.tensor_scalar(out=out, in0=in0, scalar1=s1, scalar2=s2, op0=op0, op1=op1), bl(R), bl(W))

    def STT(out, in0, sc, in1, op0, op1, R, W):
        P.add("dve", lambda e: e.scalar_tensor_tensor(out=out, in0=in0, scalar=sc, in1=in1, op0=op0, op1=op1), bl(R), bl(W))

    def CP(out, in_, R, W, eng="dve"):
        if eng == "act":
            P.add(eng, lambda e: e.activation(out=out, in_=in_, func=AF.Copy), bl(R), bl(W))
        else:
            P.add(eng, lambda e: e.tensor_copy(out=out, in_=in_), bl(R), bl(W))

    def RCP(out, in_, R, W):
        P.add("dve", lambda e: e.reciprocal(out=out, in_=in_), bl(R), bl(W))

    def MM(out, lhsT, rhs, start, stop, R, W):
        P.add("pe", lambda e: e.matmul(out, lhsT=lhsT, rhs=rhs, start=start, stop=stop), bl(R), bl(W))

    def DMA(q, out, in_, R, W, guard=True, slow=False):
        if slow:
            P.add(q, lambda e: e.dma_start(out=out, in_=in_, allow_slow_non_contiguous=True), bl(R), bl(W), dma=True, guard=guard)
        else:
            P.add(q, lambda e: e.dma_start(out=out, in_=in_), bl(R), bl(W), dma=True, guard=guard)

    def MEMSET(ap, val, W, eng="dve"):
        import os as _os
        if _os.environ.get("SKIP_POOLMS") and eng == "pool":
            eng = "dve"
        P.add(eng, lambda e: e.memset(ap, val), [], bl(W))

    cst = alloc([128, NCST], F32, "cst")
    cbf = alloc([128, 384], BF16, "cbf")
    pk = [alloc([128, NPK], F32, "pk%d" % l) for l in range(DEPTH)]
    lw = [[alloc([128, 4, 128], BF16, "lwa%d" % l), alloc([128, 4, 128], BF16, "lwx%d" % l)] for l in range(DEPTH)]
    c1 = [alloc([128, 4], F32, "c1_%d" % l) for l in range(DEPTH)]
    nalog = [alloc([128, 4], F32, "nalog%d" % l) for l in range(DEPTH)]
    xT = [alloc([128, TP], F32, "xT%d" % k) for k in range(16)]
    hT = [alloc([128, TP], BF16, "hT%d" % k) for k in range(16)]
    oT = [alloc([128, TP], BF16, "oT%d" % k) for k in range(16)]
    NW = 2
    wb = [alloc([128, 16, 512], BF16, "wb%d" % i) for i in range(NW)]
    wsm = alloc([128, 16, 16], BF16, "wsm")
    rstd = alloc([128, TP], F32, "rstd")
    sqs = [alloc([128, TP], BF16, "sq%d" % i) for i in range(2)]
    tiny = alloc([128, 64], F32, "tiny")
    ffr = alloc([128, 4, 8], F32, "ffr")
    junk = alloc([128, 128], F32, "junk")
    sqg = [alloc([128, TP], BF16, "sqg%d" % i) for i in range(2)]

    DMA("sp", cst[:], cst_d, [], [cst], guard=False)
    CP(cbf[:, 0:384], cst[:, 0:384], [cst], [cbf])
    ident_f = cst.t[:, C_ID:C_ID + 128]
    tri_f = cst.t[:, C_TRI:C_TRI + 128]
    ones_f = cst.t[:, C_ONE:C_ONE + 128]
    mls_f = cst.t[:, C_MLS:C_MLS + 128]
    ident_b = cbf.t[:, 0:128]
    tri_b = cbf.t[:, 128:256]
    ones_b = cbf.t[:, 256:384]
    for l in range(DEPTH):
        DMA("sp", pk[l][:], pk_d[l], [], [pk[l]], guard=False)
        DMA("pool", lw[l][0][:], lwa[l].rearrange("n c d -> c n d"), [], [lw[l][0]], guard=False)
        DMA("pool", lw[l][1][:], lwx[l].rearrange("n c d -> c n d"), [], [lw[l][1]], guard=False)
        import os as _os
        if _os.environ.get("SKIP_ACT"):
            continue
        o_ = 16 * l
        ACT(tiny[:, o_:o_ + 4], pk[l][:, PK_LAM:PK_LAM + 4], AF.Exp, [pk[l]], [tiny], scale=-1.0)
        ACT(tiny[:, o_ + 4:o_ + 8], tiny[:, o_:o_ + 4], AF.Ln, [tiny], [tiny], bias=1.0)
        ACT(c1[l][:], tiny[:, o_ + 4:o_ + 8], AF.Copy, [tiny], [c1[l]], scale=-8.0)
        ACT(tiny[:, o_ + 8:o_ + 12], pk[l][:, PK_ALOG:PK_ALOG + 4], AF.Exp, [pk[l]], [tiny])
        ACT(nalog[l][:], tiny[:, o_ + 8:o_ + 12], AF.Copy, [tiny], [nalog[l]], scale=-1.0)

    class Seq:
        pass

    def mkseq(name, is_prompt, sidx):
        s = Seq()
        s.name = name
        s.prompt = is_prompt
        s.sidx = sidx
        s.S = [alloc([128, 4, 128], F32, name + "S") for _ in range(DEPTH)]
        s.gch = [alloc([128, 12, 3], F32, name + "gch") for _ in range(DEPTH)]
        s.lh = [alloc([128, 4], F32, name + "lh") for _ in range(DEPTH)]
        s.lch = [alloc([128, 4, 3], F32, name + "lch") for _ in range(DEPTH)]
        s.fch = [alloc([128, 44, 2], F32, name + "fch") for _ in range(DEPTH)]
        s.Fb = [alloc([128, 8], F32, name + "Fb") for _ in range(DEPTH)]
        if is_prompt:
            s.Fk = [alloc([128, SEQ // 128, 8], F32, name + "Fk") for _ in range(DEPTH)]
            s.Fref = [alloc([128, SEQ // 128, 8], F32, name + "Fref") for _ in range(DEPTH)]
        return s

    seq_p = mkseq("p", True, 0)
    seq_s = [mkseq("s%d" % i, False, i) for i in range(2)]
    for l in range(DEPTH):
        for t in (seq_p.S[l], seq_p.gch[l], seq_p.lh[l], seq_p.lch[l], seq_p.fch[l], seq_p.Fb[l]):
            MEMSET(t[:], 0.0, [t], eng="pool")
        for i, s in enumerate(seq_s):
            DMA("sp", s.S[l][:], sg[l, i].rearrange("h k v -> k h v"), [], [s.S[l]], guard=False)
            DMA("sp", s.gch[l][:], sgc[l, i].rearrange("p (c j) -> p c j", j=3), [], [s.gch[l]], guard=False)
            DMA("sp", s.lh[l][:], sl[l, i], [], [s.lh[l]], guard=False)
            DMA("sp", s.lch[l][:], slc[l, i].rearrange("p (c j) -> p c j", j=3), [], [s.lch[l]], guard=False)
            DMA("sp", s.fch[l][:], sfc[l, i].rearrange("p (c j) -> p c j", j=2), [], [s.fch[l]], guard=False)
            MEMSET(s.Fb[l][:], 0.0, [s.Fb[l]], eng="pool")

    region0 = cur[0]
    regbuf = Buf("region")
    P.guard = regbuf
    kvK = [[Buf("kvK%d_%d" % (l, i)) for i in range(NT)] for l in range(DEPTH)]
    kvV = [[Buf("kvV%d_%d" % (l, i)) for i in range(NT)] for l in range(DEPTH)]

    def fence(keep=None):
        cur[0] = region0 if keep is None else keep
        P.nfence += 1
        import os as _os
        if P.nfence > int(_os.environ.get("STOPF", "100000")):
            P.stopped = True
            return
        P.add("dve", lambda e: e.memset(tiny[:, 60:61], 0.0), [], [regbuf], guard=False)

    wrr = [0]

    NSLOT = 50 * DEPTH
    wsc = nc.dram_tensor("wsc", [NSLOT, 128, 16 * 512], BF16).ap()
    wsmsc = nc.dram_tensor("wsmsc", [DEPTH, 128, 256], BF16).ap()
    wslot = {}
    wsbuf = {}

    def wload(key, src_rows_ap, nk, ncols):
        assert ncols == 512
        w = wb[wrr[0] % NW]
        wrr[0] += 1
        if key not in wslot:
            slot = len(wslot)
            assert slot < NSLOT
            wslot[key] = slot
            wsbuf[key] = Buf("wsc%d" % slot)
            DMA("pool", w[:, 0:nk, 0:ncols], src_rows_ap.rearrange("(k p) n -> p k n", p=128), [], [w], guard=False)
            DMA("pool", wsc[slot][:, 0:nk * 512].rearrange("p (k n) -> p k n", n=512), w[:, 0:nk, 0:ncols], [w], [wsbuf[key]], guard=False)
        else:
            slot = wslot[key]
            DMA("sp", w[:, 0:nk, 0:ncols], wsc[slot][:, 0:nk * 512].rearrange("p (k n) -> p k n", n=512), [wsbuf[key]], [w], guard=False)
        return w

    psrr = [0]

    def bank(group=(0, 1, 2, 3)):
        b = ps[group[psrr[0] % len(group)]]
        psrr[0] += 1
        return b

    def rmsnorm_to_hT(T, gcol, l):
        pb = bank()
        for k in range(16):
            sq = sqs[k % 2]
            ACT(sq[:, :T], xT[k][:, :T], AF.Square, [xT[k]], [sq])
            MM(pb[:, :T], ones_b, sq[:, :T], k == 0, k == 15, [sq, cbf], [pb])
        ACT(rstd[:, :T], pb[:, :T], AF.Sqrt, [pb], [rstd], bias=EPS, scale=1.0 / D)
        RCP(rstd[:, :T], rstd[:, :T], [rstd], [rstd])
        for k in range(16):
            STT(hT[k][:, :T], xT[k][:, :T], pk[l][:, gcol + k:gcol + k + 1], rstd[:, :T], ALU.mult, ALU.mult,
                [xT[k], rstd, pk[l]], [hT[k]])

    def fm_cols(w, T, ncc, evac, src=None):
        src = src or hT
        for cc in range(ncc):
            pb = bank()
            for k in range(16):
                MM(pb[:, :T], w[:, k, cc * 128:(cc + 1) * 128], src[k][:, :T], k == 0, k == 15, [w, src[k]], [pb])
            evac(cc, pb)

    def tm_cols(w, cols, ncols, c, evac):
        pb = bank()
        for k in range(16):
            MM(pb[:c, :ncols], hT[k][:, cols], w[:, k, 0:ncols], k == 0, k == 15, [w, hT[k]], [pb])
        evac(pb)

    def run_tile(segs, T, c, l, src_loader, last_layer, y_writer, is_last_tile):
        nseg = len(segs)
        L = segs[0]["L"]
        nsub = L // c
        win = w_in[l]
        pkl = pk[l]
        if src_loader is not None:
            fence()
            src_loader()
        fence()
        rmsnorm_to_hT(T, PK_GMIX, l)
        fence()
        qkvT = [alloc([128, 4, T], BF16, "gq"), alloc([128, 4, T], BF16, "gk"), alloc([128, 4, T], BF16, "gv")]
        zg = alloc([128, nseg * nsub, 512], BF16, "zg")
        bt = alloc([128, nseg * nsub, 4], F32, "beta")
        gg = alloc([128, nseg * nsub, 4], F32, "gg")
        tg = alloc([128, 16], F32, "tg")
        gdn_mark = cur[0]
        raws = [alloc([128, nseg, 3 + L], F32, "raw%d" % i) for i in range(2)]
        cvs = [alloc([128, nseg, L], F32, "cv%d" % i) for i in range(2)]
        sil = [alloc([128, T], F32, "sil%d" % i) for i in range(2)]
        rq = alloc([128, T], F32, "rq")
        for grp in range(3):
            w = wload((l, "qkv", grp), win[:, grp * 512:(grp + 1) * 512], 16, 512)

            def ev(cc, pb, grp=grp):
                ch = grp * 4 + cc
                raw = raws[ch % 2]
                cv_ = cvs[ch % 2]
                sl_ = sil[ch % 2]
                for si, sgm in enumerate(segs):
                    gch = sgm["seq"].gch[l]
                    CP(raw[:, si, 0:3], gch[:, ch, :], [gch], [raw], eng="pool")
                P.add("act", lambda e: e.activation(out=raw[:, :, 3:3 + L], in_=pb[:, :T].rearrange("p (s l) -> p s l", s=nseg), func=AF.Copy),
                      bl([pb]) + [regbuf], bl([raw]), guard=False)
                for si, sgm in enumerate(segs):
                    gch = sgm["seq"].gch[l]
                    CP(gch[:, ch, :], raw[:, si, L:L + 3], [raw], [gch], eng="pool")
                cw = PK_GCW + ch * 4
                TS(cv_[:], raw[:, :, 0:L], pkl[:, cw:cw + 1], None, ALU.mult, None, [raw, pkl], [cv_])
                for j in range(1, 4):
                    STT(cv_[:], raw[:, :, j:j + L], pkl[:, cw + j:cw + j + 1], cv_[:], ALU.mult, ALU.add, [raw, pkl, cv_], [cv_])
                dst = qkvT[grp]
                if grp == 2:
                    ACT(dst[:, cc, :], cv_[:].rearrange("p s l -> p (s l)"), AF.Silu, [cv_], [dst])
                else:
                    ACT(sl_[:, :T], cv_[:].rearrange("p s l -> p (s l)"), AF.Silu, [cv_], [sl_])
                    sq = sqg[ch % 2]
                    ACT(sq[:, :T], sl_[:, :T], AF.Square, [sl_], [sq])
                    pb2 = bank((4, 5))
                    MM(pb2[:, :T], ones_b, sq[:, :T], True, True, [sq, cbf], [pb2])
                    ACT(rq[:, :T], pb2[:, :T], AF.Sqrt, [pb2], [rq], bias=EPS, scale=1.0)
                    RCP(rq[:, :T], rq[:, :T], [rq], [rq])
                    STT(dst[:, cc, :], sl_[:, :T], (128.0 ** -0.5) if grp == 0 else 1.0, rq[:, :T], ALU.mult, ALU.mult, [sl_, rq], [dst])
            fm_cols(w, T, 4, ev)
        w = wload((l, "z"), win[:, 1536:2048], 16, 512)
        if ("wsm", l) not in wsbuf:
            wsbuf[("wsm", l)] = Buf("wsmsc%d" % l)
            DMA("pool", wsm[:, :, 0:8], win[:, 2048:2056].rearrange("(k p) n -> p k n", p=128), [], [wsm], guard=False)
            DMA("pool", wsm[:, :, 8:16], win[:, 5128:5136].rearrange("(k p) n -> p k n", p=128), [], [wsm], guard=False)
            DMA("pool", wsmsc[l].rearrange("p (k n) -> p k n", n=16), wsm[:, :, :], [wsm], [wsbuf[("wsm", l)]], guard=False)
        else:
            DMA("sp", wsm[:, :, :], wsmsc[l].rearrange("p (k n) -> p k n", n=16), [wsbuf[("wsm", l)]], [wsm], guard=False)
        for si, sgm in enumerate(segs):
            for s in range(nsub):
                cols = slice(sgm["col0"] + s * c, sgm["col0"] + (s + 1) * c)
                bi = si * nsub + s

                def evz(pb, bi=bi):
                    ACT(zg[:c, bi, :], pb[:c, :], AF.Silu, [pb], [zg])
                    TT(zg[:c, bi, :], zg[:c, bi, :], pkl[:c, PK_GNG4:PK_GNG4 + 512], ALU.mult, [zg, pkl], [zg])
                tm_cols(w, cols, 512, c, evz)
                pbs = bank()
                for k in range(16):
                    MM(pbs[:c, :16], hT[k][:, cols], wsm[:, k, :], k == 0, k == 15, [wsm, hT[k]], [pbs])
                ACT(bt[:c, bi, :], pbs[:c, 0:4], AF.Sigmoid, [pbs], [bt])
                TT(tg[:c, 0:4], pbs[:c, 4:8], pkl[:c, PK_DTB:PK_DTB + 4], ALU.add, [pbs, pkl], [tg])
                ACT(tg[:c, 4:8], tg[:c, 0:4], AF.Exp, [tg], [tg])
                ACT(tg[:c, 8:12], tg[:c, 4:8], AF.Ln, [tg], [tg], bias=1.0)
                TT(gg[:c, bi, :], tg[:c, 8:12], nalog[l][:c, :], ALU.mult, [tg, nalog[l]], [gg])
                CP(ffr[:c, bi, :], pbs[:c, 8:16], [pbs], [ffr])
        fence(keep=gdn_mark)
        c4 = 4 * c
        gmat = alloc([128, c4], F32, "gmat")
        Gcol = alloc([128, 4], F32, "Gcol")
        Gb = alloc([128, c4], F32, "Gb")
        EGb = alloc([128, c4], F32, "EGb")
        tmpm = [alloc([128, c], F32, "tmpm%d" % i) for i in range(2)]
        DLs = alloc([128, c4], F32, "DLs")
        DUm = alloc([128, c4], F32, "DUm")
        Lm = [alloc([128, c4], F32, "Lm%d" % i) for i in range(2)]
        Um = [alloc([128, c4], F32, "Um%d" % i) for i in range(2)]
        Pm = alloc([128, c4], F32, "Pm")
        MTm = alloc([128, c4], F32, "MTm")
        kbeg = alloc([128, 4, 128], F32, "kbeg")
        kd = alloc([128, 4, 128], F32, "kd")
        vb = alloc([128, 4, 128], F32, "vb")
        nwT = alloc([128, c4], F32, "nwT")
        delta = alloc([128, 4, 128], F32, "delta")
        qgT = alloc([128, c4], F32, "qgT")
        sm = alloc([128, 32], F32, "sm")
        ytok = alloc([128, 512], BF16, "ytok")
        nstep = int(np.log2(c)) - 1
        for si, sgm in enumerate(segs):
            S = sgm["seq"].S[l]
            for s in range(nsub):
                t0 = sgm["col0"] + s * c
                cols = slice(t0, t0 + c)
                bi = si * nsub + s
                qT_, kT_, vT_ = qkvT
                pk_tok = ps[4]
                pv_tok = ps[5]
                for h in range(4):
                    MM(pk_tok[:c, h * 128:(h + 1) * 128], kT_[:, h, cols], ident_b, True, True, [kT_, cbf], [pk_tok])
                    MM(pv_tok[:c, h * 128:(h + 1) * 128], vT_[:, h, cols], ident_b, True, True, [vT_, cbf], [pv_tok])
                pg = bank()
                MM(pg[:c, 0:4], tri_f[:c, :c], gg[:c, bi, :], True, True, [cst, gg], [pg])
                CP(Gcol[:c, :], pg[:c, 0:4], [pg], [Gcol])
                for h in range(4):
                    TS(gmat[:c, h * c:(h + 1) * c], tri_f[:c, :c], gg[:c, bi, h:h + 1], None, ALU.mult, None, [cst, gg], [gmat])
                pgb = bank()
                MM(pgb[:, :c4], ones_f[:c, :], gmat[:c, :c4], True, True, [cst, gmat], [pgb])
                CP(Gb[:, :c4], pgb[:, :c4], [pgb], [Gb])
                ACT(EGb[:, :c4], pgb[:, :c4], AF.Exp, [pgb], [EGb])
                ACT(sm[:c, 0:4], Gcol[:c, :], AF.Exp, [Gcol], [sm])
                TT(sm[:c, 4:8], sm[:c, 0:4], bt[:c, bi, :], ALU.mult, [sm, bt], [sm])
                for h in range(4):
                    ACT(sm[:c, 8 + h:9 + h], Gcol[:c, h:h + 1], AF.Exp, [Gcol, Gb], [sm], scale=-1.0, bias=Gb[:c, h * c + c - 1:h * c + c])
                for h in range(4):
                    TS(kbeg[:c, h, :], pk_tok[:c, h * 128:(h + 1) * 128], sm[:c, 4 + h:5 + h], None, ALU.mult, None, [pk_tok, sm], [kbeg])
                    TS(kd[:c, h, :], pk_tok[:c, h * 128:(h + 1) * 128], sm[:c, 8 + h:9 + h], None, ALU.mult, None, [pk_tok, sm], [kd])
                    TS(vb[:c, h, :], pv_tok[:c, h * 128:(h + 1) * 128], bt[:c, bi, h:h + 1], None, ALU.mult, None, [pv_tok, bt], [vb])
                for h in range(4):
                    hc = slice(h * c, (h + 1) * c)
                    tm = tmpm[0]
                    TS(tm[:c, :], Gb[:c, hc], Gcol[:c, h:h + 1], 0.0, ALU.subtract, ALU.max, [Gb, Gcol], [tm])
                    ACT(tm[:c, :], tm[:c, :], AF.Exp, [tm], [tm], scale=-1.0)
                    STT(DLs[:c, hc], tm[:c, :], bt[:c, bi, h:h + 1], mls_f[:c, :c], ALU.mult, ALU.mult, [tm, bt, cst], [DLs])
                    tm2 = tmpm[1]
                    TS(tm2[:c, :], Gb[:c, hc], Gcol[:c, h:h + 1], 0.0, ALU.subtract, ALU.min, [Gb, Gcol], [tm2])
                    ACT(tm2[:c, :], tm2[:c, :], AF.Exp, [tm2], [tm2])
                    TT(DUm[:c, hc], tm2[:c, :], tri_f[:c, :c], ALU.mult, [tm2, cst], [DUm])
                pkk = bank()
                pqk = bank()
                for h in range(4):
                    hc = slice(h * c, (h + 1) * c)
                    MM(pkk[:c, hc], kT_[:, h, cols], kT_[:, h, cols], True, True, [kT_], [pkk])
                    MM(pqk[:c, hc], kT_[:, h, cols], qT_[:, h, cols], True, True, [kT_, qT_], [pqk])
                L0 = Lm[0]
                TT(L0[:c, :c4], pkk[:c, :c4], DLs[:c, :c4], ALU.mult, [pkk, DLs], [L0])
                TT(MTm[:c, :c4], pqk[:c, :c4], DUm[:c, :c4], ALU.mult, [pqk, DUm], [MTm])
                pB = bank()
                for h in range(4):
                    hc = slice(h * c, (h + 1) * c)
                    MM(pB[:c, hc], L0[:c, hc], ident_f[:c, :c], True, True, [L0, cst], [pB])
                U0 = Um[0]
                CP(U0[:c, :c4], pB[:c, :c4], [pB], [U0], eng="act")
                for h in range(4):
                    hc = slice(h * c, (h + 1) * c)
                    TT(Pm[:c, hc], ident_f[:c, :c], pB[:c, hc], ALU.subtract, [cst, pB], [Pm])
                Lc, Uc = L0, U0
                for it in range(1, nstep + 1):
                    Ln_, Un_ = Lm[it % 2], Um[it % 2]
                    pU = bank()
                    pL = bank()
                    for h in range(4):
                        hc = slice(h * c, (h + 1) * c)
                        MM(pL[:c, hc], Uc[:c, hc], Lc[:c, hc], True, True, [Uc, Lc], [pL])
                        if it < nstep:
                            MM(pU[:c, hc], Lc[:c, hc], Uc[:c, hc], True, True, [Uc, Lc], [pU])
                    CP(Ln_[:c, :c4], pL[:c, :c4], [pL], [Ln_])
                    if it < nstep:
                        CP(Un_[:c, :c4], pU[:c, :c4], [pU], [Un_], eng="act")
                    pP = bank()
                    for h in range(4):
                        hc = slice(h * c, (h + 1) * c)
                        MM(pP[:c, hc], Ln_[:c, hc], Pm[:c, hc], True, True, [Ln_, Pm], [pP])
                    TT(Pm[:c, :c4], Pm[:c, :c4], pP[:c, :c4], ALU.add, [Pm, pP], [Pm])
                    Lc, Uc = Ln_, Un_
                pW = bank()
                for h in range(4):
                    hc = slice(h * c, (h + 1) * c)
                    MM(pW[:, hc], kbeg[:c, h, :], Pm[:c, hc], True, True, [kbeg, Pm], [pW])
                ACT(nwT[:, :c4], pW[:, :c4], AF.Copy, [pW], [nwT], scale=-1.0)
                pD = bank()
                for h in range(4):
                    hc = slice(h * c, (h + 1) * c)
                    MM(pD[:c, h * 128:(h + 1) * 128], Pm[:c, hc], vb[:c, h, :], True, False, [Pm, vb], [pD])
                    MM(pD[:c, h * 128:(h + 1) * 128], nwT[:, hc], S[:, h, :], False, True, [nwT, S], [pD])
                CP(delta[:c, :, :], pD[:c, :].rearrange("p (h v) -> p h v", h=4), [pD], [delta])
                for h in range(4):
                    hc = slice(h * c, (h + 1) * c)
                    TT(qgT[:, hc], qT_[:, h, cols], EGb[:, hc], ALU.mult, [qT_, EGb], [qgT])
                pO = bank()
                for h in range(4):
                    hc = slice(h * c, (h + 1) * c)
                    MM(pO[:c, h * 128:(h + 1) * 128], qgT[:, hc], S[:, h, :], True, False, [qgT, S], [pO])
                    MM(pO[:c, h * 128:(h + 1) * 128], MTm[:c, hc], delta[:c, h, :], False, True, [MTm, delta], [pO])
                pS = bank()
                for h in range(4):
                    MM(pS[:, h * 128:(h + 1) * 128], kd[:c, h, :], delta[:c, h, :], True, True, [kd, delta], [pS])
                for h in range(4):
                    STT(S[:, h, :], S[:, h, :], EGb[:, h * c + c - 1:h * c + c], pS[:, h * 128:(h + 1) * 128], ALU.mult, ALU.add, [S, EGb, pS], [S])
                for h in range(4):
                    ACT(junk[:c, :], pO[:c, h * 128:(h + 1) * 128], AF.Square, [pO], [junk, sm], accum_out=sm[:c, 12 + h:13 + h])
                ACT(sm[:c, 16:20], sm[:c, 12:16], AF.Sqrt, [sm], [sm], bias=EPS, scale=1.0 / 128)
                RCP(sm[:c, 16:20], sm[:c, 16:20], [sm], [sm])
                for h in range(4):
                    STT(ytok[:c, h * 128:(h + 1) * 128], pO[:c, h * 128:(h + 1) * 128], sm[:c, 16 + h:17 + h], zg[:c, bi, h * 128:(h + 1) * 128],
                        ALU.mult, ALU.mult, [pO, sm, zg], [ytok])
                pT = bank()
                for h in range(4):
                    MM(pT[:, h * c:(h + 1) * c], ytok[:c, h * 128:(h + 1) * 128], ident_b[:c, :c], True, True, [ytok, cbf], [pT])
                for h in range(4):
                    CP(oT[h][:, cols], pT[:, h * c:(h + 1) * c], [pT], [oT[h]], eng="act" if h % 2 else "dve")
        if is_last_tile:
            for si, sgm in enumerate(segs):
                sq_ = sgm["seq"]
                DMA("pool", sgm["gs_out"][l].rearrange("h k v -> k h v"), sq_.S[l][:], [sq_.S[l]], [])
                DMA("pool", sgm["gc_out"][l].rearrange("p (c j) -> p c j", j=3), sq_.gch[l][:], [sq_.gch[l]], [])

        fence()
        QT = alloc([128, 8, T], BF16, "QT")
        KT = alloc([128, 8, T], BF16, "KT")
        Va = alloc([128, nseg * nsub, 8, 129], BF16, "Va")
        stg = [alloc([128, 512], F32, "stg%d" % i) for i in range(3)]
        strr = [0]
        MEMSET(Va[:, :, :, 128:129], 1.0, [Va], eng="pool")
        for half in range(2):
            w = wload((l, "fq", half), win[:, 2056 + half * 512:2056 + (half + 1) * 512], 16, 512)
            fm_cols(w, T, 4, lambda cc, pb, half=half: ACT(QT[:, half * 4 + cc, :], pb[:, :T], AF.Copy, [pb], [QT], scale=128.0 ** -0.5))
        for half in range(2):
            w = wload((l, "fk", half), win[:, 3080 + half * 512:3080 + (half + 1) * 512], 16, 512)
            fm_cols(w, T, 4, lambda cc, pb, half=half: CP(KT[:, half * 4 + cc, :], pb[:, :T], [pb], [KT]))
            for si, sgm in enumerate(segs):
                for s in range(nsub):
                    cols = slice(sgm["col0"] + s * c, sgm["col0"] + (s + 1) * c)

                    def evk(pb, sgm=sgm, s=s, half=half):
                        st = stg[strr[0] % 3]
                        strr[0] += 1
                        ACT(st[:c, :], pb[:c, :], AF.Copy, [pb], [st])
                        DMA("pool", sgm["kout"][l][s * c:(s + 1) * c, half * 512:(half + 1) * 512], st[:c, :], [st], [sgm["kbuf"][l]])
                    tm_cols(w, cols, 512, c, evk)
        for half in range(2):
            w = wload((l, "fv", half), win[:, 4104 + half * 512:4104 + (half + 1) * 512], 16, 512)
            for si, sgm in enumerate(segs):
                for s in range(nsub):
                    cols = slice(sgm["col0"] + s * c, sgm["col0"] + (s + 1) * c)
                    bi = si * nsub + s

                    def evv(pb, sgm=sgm, s=s, half=half, bi=bi):
                        st = stg[strr[0] % 3]
                        strr[0] += 1
                        ACT(st[:c, :], pb[:c, :], AF.Copy, [pb], [st])
                        CP(Va[:c, bi, half * 4:(half + 1) * 4, 0:128], pb[:c, :].rearrange("p (h d) -> p h d", h=4), [pb], [Va])
                        DMA("pool", sgm["vout"][l][s * c:(s + 1) * c, half * 512:(half + 1) * 512], st[:c, :], [st], [sgm["vbuf"][l]])
                    tm_cols(w, cols, 512, c, evv)
        lf = alloc([128, nseg * nsub, 8], F32, "lf")
        t8 = alloc([128, 16], F32, "t8")
        for si, sgm in enumerate(segs):
            sq_ = sgm["seq"]
            Fb = sq_.Fb[l]
            if not sq_.prompt:
                nblk = PB + 1
                sq_.Fk_l = alloc([128, nblk, 8], F32, "sFk")
                sq_.Fref_l = alloc([128, nblk, 8], F32, "sFref")
                clg = alloc([128, PB, 8], F32, "clg")
                for q4 in range(0, PB, 8):
                    n = min(8, PB - q4)
                    DMA("pool", clg[:, q4:q4 + n, :], cl[l, sq_.sidx, q4 * 128:(q4 + n) * 128, :].rearrange("(b p) h -> p b h", p=128), [], [clg])
                for kb in range(PB):
                    pf = bank()
                    MM(pf[:, 0:8], tri_f, clg[:, kb, :], True, True, [cst, clg], [pf])
                    MM(pf[:, 8:16], ones_f, clg[:, kb, :], True, True, [cst, clg], [pf])
                    TT(sq_.Fk_l[:, kb, :], pf[:, 0:8], Fb[:, :], ALU.add, [pf, Fb], [sq_.Fk_l])
                    TT(Fb[:, :], Fb[:, :], pf[:, 8:16], ALU.add, [Fb, pf], [Fb])
                Fk, Fref = sq_.Fk_l, sq_.Fref_l
            else:
                Fk, Fref = sq_.Fk[l], sq_.Fref[l]
            for s in range(nsub):
                bi = si * nsub + s
                blk = sgm["blk0"] + s
                TT(t8[:c, 0:8], ffr[:c, bi, :], pkl[:c, PK_FB:PK_FB + 8], ALU.add, [ffr, pkl], [t8])
                ACT(t8[:c, 8:16], t8[:c, 0:8], AF.Exp, [t8], [t8], scale=-1.0)
                ACT(t8[:c, 0:8], t8[:c, 8:16], AF.Ln, [t8], [t8], bias=1.0)
                TS(lf[:c, bi, :], t8[:c, 0:8], -1.0, None, ALU.mult, None, [t8], [lf])
                DMA("pool", sgm["lout"][l][s * c:(s + 1) * c, :], lf[:c, bi, :], [lf], [])
                pf = bank()
                MM(pf[:c, 0:8], tri_f[:c, :c], lf[:c, bi, :], True, True, [cst, lf], [pf])
                MM(pf[:, 8:16], ones_f[:c, :], lf[:c, bi, :], True, True, [cst, lf], [pf])
                CP(Fref[:, blk, :], Fb[:, :], [Fb], [Fref])
                TT(Fk[:c, blk, :], pf[:c, 0:8], Fb[:c, :], ALU.add, [pf, Fb], [Fk])
                TT(Fb[:, :], Fb[:, :], pf[:, 8:16], ALU.add, [Fb, pf], [Fb])
        Kh = [alloc([128, 4, 128], BF16, "Kh%d" % i) for i in range(2)]
        Vh = [alloc([128, 4, 129], BF16, "Vh%d" % i) for i in range(2)]
        for v_ in Vh:
            MEMSET(v_[:, :, 128:129], 1.0, [v_], eng="pool")
        KTh = [alloc([128, 512], BF16, "KTh%d" % i) for i in range(2)]
        PT = [alloc([128, 512], BF16, "PT%d" % i) for i in range(3)]
        bias4 = [alloc([128, 4], F32, "bias4_%d" % i) for i in range(4)]
        yf = alloc([128, nseg * nsub, 1024], BF16, "yf")
        fsm = alloc([128, 16], F32, "fsm")
        hrr = [0]
        prr = [0]
        brr = [0]
        for si, sgm in enumerate(segs):
            sq_ = sgm["seq"]
            if sq_.prompt:
                Fk, Fref = sq_.Fk[l], sq_.Fref[l]
                nhist = sgm["blk0"] // 4
            else:
                Fk, Fref = sq_.Fk_l, sq_.Fref_l
                nhist = PB // 4
            q0 = sgm["col0"]
            qb0 = sgm["blk0"]
            for h in range(8):
                started = [False] * nsub
                Oacc = [ps[4 + qs] for qs in range(nsub)]
                items = []
                hist_ctx = {}

                def mk_hist(j, kb):
                    it = {}

                    def score():
                        if kb == 0:
                            kh = Kh[hrr[0] % 2]
                            vh = Vh[hrr[0] % 2]
                            kth = KTh[hrr[0] % 2]
                            hrr[0] += 1
                            if sq_.prompt:
                                ksrc = fk_p[l, j * 512:(j + 1) * 512, h * 128:(h + 1) * 128]
                                vsrc = fv_p[l, j * 512:(j + 1) * 512, h * 128:(h + 1) * 128]
                                kdep, vdep = [kvK[l][j]], [kvV[l][j]]
                            else:
                                ksrc = ck[l, sq_.sidx, j * 512:(j + 1) * 512, h * 128:(h + 1) * 128]
                                vsrc = cv[l, sq_.sidx, j * 512:(j + 1) * 512, h * 128:(h + 1) * 128]
                                kdep, vdep = [], []
                            DMA("pool", kh[:, :, :], ksrc.rearrange("(b p) d -> p b d", p=128), kdep, [kh])
                            DMA("pool", vh[:, :, 0:128], vsrc.rearrange("(b p) d -> p b d", p=128), vdep, [vh])
                            ptr = ps[3]
                            for kb2 in range(4):
                                MM(ptr[:, kb2 * 128:(kb2 + 1) * 128], kh[:, kb2, :], ident_b, True, True, [kh, cbf], [ptr])
                            CP(kth[:, :], ptr[:, :], [ptr], [kth])
                            hist_ctx[j] = (vh, kth)
                        vh, kth = hist_ctx[j]
                        kblk = j * 4 + kb
                        pst = bank((0, 1, 2))
                        MM(pst[:, :L], kth[:, kb * 128:(kb + 1) * 128], QT[:, h, q0:q0 + L], True, True, [kth, QT], [pst])
                        b4 = bias4[brr[0] % 4]
                        brr[0] += 1
                        TS(b4[:, 0:nsub], Fref[:, qb0:qb0 + nsub, h], Fk[:, kblk, h:h + 1], None, ALU.subtract, None, [Fref, Fk], [b4])
                        it["pst"], it["b4"], it["vh"] = pst, b4, vh

                    def rest():
                        pst, b4, vh = it["pst"], it["b4"], it["vh"]
                        pt = PT[prr[0] % 3]
                        prr[0] += 1
                        for qs in range(nsub):
                            ACT(pt[:, qs * c:(qs + 1) * c], pst[:, qs * c:(qs + 1) * c], AF.Exp, [pst, b4], [pt], bias=b4[:, qs:qs + 1])
                        for qs in range(nsub):
                            MM(Oacc[qs][:c, 0:129], pt[:, qs * c:(qs + 1) * c], vh[:, kb, :], not started[qs], False, [pt, vh], [Oacc[qs]])
                            started[qs] = True
                    return score, rest

                def mk_cur(kb):
                    it = {}
                    bi = si * nsub + kb
                    kblk = qb0 + kb
                    nq = nsub - kb

                    def score():
                        pst = bank((0, 1, 2))
                        MM(pst[:c, :nq * c], KT[:, h, q0 + kb * c:q0 + (kb + 1) * c], QT[:, h, q0 + kb * c:q0 + L], True, True, [KT, QT], [pst])
                        b4 = bias4[brr[0] % 4]
                        brr[0] += 1
                        TS(b4[:c, 0:nq], Fref[:c, qb0 + kb:qb0 + nsub, h], Fk[:c, kblk, h:h + 1], None, ALU.subtract, None, [Fref, Fk], [b4])
                        it["pst"], it["b4"] = pst, b4

                    def rest():
                        pst, b4 = it["pst"], it["b4"]
                        pt = PT[prr[0] % 3]
                        prr[0] += 1
                        for qi in range(nq):
                            ACT(pt[:c, qi * c:(qi + 1) * c], pst[:c, qi * c:(qi + 1) * c], AF.Exp, [pst, b4], [pt], bias=b4[:c, qi:qi + 1])
                        TT(pt[:c, 0:c], pt[:c, 0:c], tri_b[:c, :c], ALU.mult, [pt, cbf], [pt])
                        for qi in range(nq):
                            qs = kb + qi
                            MM(Oacc[qs][:c, 0:129], pt[:c, qi * c:(qi + 1) * c], Va[:c, bi, h, :], not started[qs], qs == kb, [pt, Va], [Oacc[qs]])
                            started[qs] = True
                    return score, rest

                for j in range(nhist):
                    for kb in range(4):
                        items.append(mk_hist(j, kb))
                for kb in range(nsub):
                    items.append(mk_cur(kb))
                for ii in range(len(items) + 1):
                    if ii < len(items):
                        items[ii][0]()
                    if ii >= 1:
                        items[ii - 1][1]()
                for qs in range(nsub):
                    bi = si * nsub + qs
                    O = Oacc[qs]
                    RCP(fsm[:c, 0:1], O[:c, 128:129], [O], [fsm])
                    ACT(junk[:c, :], O[:c, 0:128], AF.Square, [O, fsm], [junk, fsm], scale=fsm[:c, 0:1], accum_out=fsm[:c, 1:2])
                    ACT(fsm[:c, 2:3], fsm[:c, 1:2], AF.Sqrt, [fsm], [fsm], bias=EPS, scale=1.0 / 128)
                    RCP(fsm[:c, 2:3], fsm[:c, 2:3], [fsm], [fsm])
                    TT(fsm[:c, 3:4], fsm[:c, 2:3], fsm[:c, 0:1], ALU.mult, [fsm], [fsm])
                    STT(yf[:c, bi, h * 128:(h + 1) * 128], O[:c, 0:128], fsm[:c, 3:4], pkl[:c, PK_FNG + h * 128:PK_FNG + (h + 1) * 128],
                        ALU.mult, ALU.mult, [O, fsm, pkl], [yf])
            for qs in range(nsub):
                bi = si * nsub + qs
                cols = slice(q0 + qs * c, q0 + (qs + 1) * c)
                for hg in range(2):
                    pT = bank((0, 1, 2, 3))
                    for hh in range(4):
                        h = hg * 4 + hh
                        MM(pT[:, hh * c:(hh + 1) * c], yf[:c, bi, h * 128:(h + 1) * 128], ident_b[:c, :c], True, True, [yf, cbf], [pT])
                    for hh in range(4):
                        h = hg * 4 + hh
                        CP(oT[4 + h][:, cols], pT[:, hh * c:(hh + 1) * c], [pT], [oT[4 + h]], eng="act" if hh % 2 else "dve")

        fence()
        lxr = alloc([128, 4, nseg, 3 + L], F32, "lxr")
        gel = alloc([128, 4, T], BF16, "gel")
        xc = alloc([128, 4, nseg, L], F32, "xc")
        xcb = alloc([128, 4, T], BF16, "xcb")
        lt = [alloc([128, T], F32, "lt%d" % i) for i in range(6)]
        hs = alloc([128, nseg, L], F32, "hs")
        w = wload((l, "lx"), win[:, 5136:5648], 16, 512)

        def evlx(cc, pb):
            for si, sgm in enumerate(segs):
                lch = sgm["seq"].lch[l]
                CP(lxr[:, cc, si, 0:3], lch[:, cc, :], [lch], [lxr], eng="pool")
            P.add("act", lambda e: e.activation(out=lxr[:, cc, :, 3:3 + L], in_=pb[:, :T].rearrange("p (s l) -> p s l", s=nseg), func=AF.Copy),
                  bl([pb]) + [regbuf], bl([lxr]), guard=False)
            for si, sgm in enumerate(segs):
                lch = sgm["seq"].lch[l]
                CP(lch[:, cc, :], lxr[:, cc, si, L:L + 3], [lxr], [lch], eng="pool")
        fm_cols(w, T, 4, evlx)
        w = wload((l, "lg"), win[:, 5648:6160], 16, 512)

        def evg(cc, pb):
            g1 = lt[cc % 2]
            ACT(g1[:, :T], pb[:, :T], AF.Square, [pb], [g1])
            TS(g1[:, :T], g1[:, :T], 0.044715, 1.0, ALU.mult, ALU.add, [g1], [g1])
            TT(g1[:, :T], g1[:, :T], pb[:, :T], ALU.mult, [g1, pb], [g1])
            ACT(g1[:, :T], g1[:, :T], AF.Sigmoid, [g1], [g1], scale=1.5957691216057308)
            TT(gel[:, cc, :], g1[:, :T], pb[:, :T], ALU.mult, [g1, pb], [gel])
        fm_cols(w, T, 4, evg)
        for n in range(4):
            cw = PK_LCW + n * 4
            TS(xc[:, n, :, :], lxr[:, n, :, 0:L], pkl[:, cw:cw + 1], pkl[:, PK_LCB + n:PK_LCB + n + 1], ALU.mult, ALU.add, [lxr, pkl], [xc])
            for j in range(1, 4):
                STT(xc[:, n, :, :], lxr[:, n, :, j:j + L], pkl[:, cw + j:cw + j + 1], xc[:, n, :, :], ALU.mult, ALU.add, [lxr, pkl, xc], [xc])
            xcn = xc[:, n, :, :].rearrange("p s l -> p (s l)")
            ACT(xcb[:, n, :], xcn, AF.Copy, [xc], [xcb])
            pr = bank()
            MM(pr[:, :T], lw[l][0][:, n, :], xcb[:, n, :], True, True, [lw[l][0], xcb], [pr])
            pi = bank()
            MM(pi[:, :T], lw[l][1][:, n, :], xcb[:, n, :], True, True, [lw[l][1], xcb], [pi])
            r_, i_, a_, a2_, u_, t_ = lt
            ACT(r_[:, :T], pr[:, :T], AF.Sigmoid, [pr, pkl], [r_], bias=pkl[:, PK_LBA + n:PK_LBA + n + 1])
            ACT(i_[:, :T], pi[:, :T], AF.Sigmoid, [pi, pkl], [i_], bias=pkl[:, PK_LBX + n:PK_LBX + n + 1])
            ACT(a_[:, :T], r_[:, :T], AF.Exp, [r_, c1[l]], [a_], scale=c1[l][:, n:n + 1])
            TT(a2_[:, :T], a_[:, :T], a_[:, :T], ALU.mult, [a_], [a2_])
            ACT(a2_[:, :T], a2_[:, :T], AF.Sqrt, [a2_], [a2_], scale=-1.0, bias=1.0)
            TT(u_[:, :T], i_[:, :T], xcn, ALU.mult, [i_, xc], [u_])
            TT(u_[:, :T], u_[:, :T], a2_[:, :T], ALU.mult, [u_, a2_], [u_])
            for si, sgm in enumerate(segs):
                lh = sgm["seq"].lh[l]
                c0 = sgm["col0"]
                P.add("dve", lambda e, si=si, c0=c0, lh=lh, n=n: e.tensor_tensor_scan(out=hs[:, si, :], data0=a_[:, c0:c0 + L], data1=u_[:, c0:c0 + L],
                                                                                initial=lh[:, n:n + 1], op0=ALU.mult, op1=ALU.add),
                      bl([a_, u_, lh]), bl([hs]))
                CP(lh[:, n:n + 1], hs[:, si, L - 1:L], [hs], [lh])
            hsn = hs[:].rearrange("p s l -> p (s l)")
            sq = sqg[0]
            ACT(sq[:, :T], hsn, AF.Square, [hs], [sq])
            pn = bank()
            MM(pn[:, :T], ones_b, sq[:, :T], True, True, [sq, cbf], [pn])
            ACT(t_[:, :T], pn[:, :T], AF.Sqrt, [pn], [t_], bias=EPS, scale=1.0 / 128)
            RCP(t_[:, :T], t_[:, :T], [t_], [t_])
            STT(t_[:, :T], hsn, pkl[:, PK_LNG + n:PK_LNG + n + 1], t_[:, :T], ALU.mult, ALU.mult, [hs, pkl, t_], [t_])
            TT(oT[12 + n][:, :T], t_[:, :T], gel[:, n, :], ALU.mult, [t_, gel], [oT[12 + n]])
        if is_last_tile:
            for si, sgm in enumerate(segs):
                sq_ = sgm["seq"]
                DMA("pool", sgm["lh_out"][l], sq_.lh[l][:], [sq_.lh[l]], [])
                DMA("pool", sgm["lc_out"][l].rearrange("p (c j) -> p c j", j=3), sq_.lch[l][:], [sq_.lch[l]], [])

        fence()
        for g in range(4):
            w = wload((l, "wo", g), w_out[l][:, g * 512:(g + 1) * 512], 16, 512)
            fm_cols(w, T, 4, lambda cc, pb, g=g: TT(xT[g * 4 + cc][:, :T], xT[g * 4 + cc][:, :T], pb[:, :T], ALU.add, [xT[g * 4 + cc], pb], [xT[g * 4 + cc]]),
                    src=oT)

        fence()
        rmsnorm_to_hT(T, PK_GFFN, l)
        actT = alloc([128, 44, T], BF16, "actT")
        gpr = [alloc([128, nseg, 2 + L], F32, "gpr%d" % i) for i in range(2)]
        gcv = [alloc([128, nseg, L], F32, "gcv%d" % i) for i in range(2)]
        sg_ = [alloc([128, T], F32, "sg%d" % i) for i in range(2)]
        for j in range(11):
            wg = wload((l, "ug", j), w_up[l][:, j * 512:(j + 1) * 512], 16, 512)
            wv = wload((l, "uv", j), w_up[l][:, DFF + j * 512:DFF + (j + 1) * 512], 16, 512)
            for cc in range(4):
                m = j * 4 + cc
                pg = bank()
                pv = bank()
                for k in range(16):
                    MM(pg[:, :T], wg[:, k, cc * 128:(cc + 1) * 128], hT[k][:, :T], k == 0, k == 15, [wg, hT[k]], [pg])
                for k in range(16):
                    MM(pv[:, :T], wv[:, k, cc * 128:(cc + 1) * 128], hT[k][:, :T], k == 0, k == 15, [wv, hT[k]], [pv])
                gp = gpr[m % 2]
                gc_ = gcv[m % 2]
                s_ = sg_[m % 2]
                for si, sgm in enumerate(segs):
                    fch = sgm["seq"].fch[l]
                    CP(gp[:, si, 0:2], fch[:, m, :], [fch], [gp], eng="pool")
                P.add("act", lambda e, gp=gp, pg=pg: e.activation(out=gp[:, :, 2:2 + L], in_=pg[:, :T].rearrange("p (s l) -> p s l", s=nseg), func=AF.Copy),
                      bl([pg]) + [regbuf], bl([gp]), guard=False)
                for si, sgm in enumerate(segs):
                    fch = sgm["seq"].fch[l]
                    CP(fch[:, m, :], gp[:, si, L:L + 2], [gp], [fch], eng="pool")
                cw = PK_FCW + m * 3
                TS(gc_[:], gp[:, :, 0:L], pkl[:, cw:cw + 1], None, ALU.mult, None, [gp, pkl], [gc_])
                for jj in range(1, 3):
                    STT(gc_[:], gp[:, :, jj:jj + L], pkl[:, cw + jj:cw + jj + 1], gc_[:], ALU.mult, ALU.add, [gp, pkl, gc_], [gc_])
                ACT(s_[:, :T], gc_[:].rearrange("p s l -> p (s l)"), AF.Silu, [gc_], [s_])
                TT(actT[:, m, :], s_[:, :T], pv[:, :T], ALU.mult, [s_, pv], [actT])
        if is_last_tile:
            for si, sgm in enumerate(segs):
                sq_ = sgm["seq"]
                DMA("pool", sgm["fc_out"][l].rearrange("p (c j) -> p c j", j=2), sq_.fch[l][:], [sq_.fch[l]], [])
        kgroups = [(0, 16), (16, 16), (32, 12)]
        for g in range(4):
            acc = [ps[4 + cc] for cc in range(4)]
            for gi, (k0, nk) in enumerate(kgroups):
                w = wload((l, "dn", g, gi), w_dn[l][k0 * 128:(k0 + nk) * 128, g * 512:(g + 1) * 512], nk, 512)
                for cc in range(4):
                    for k in range(nk):
                        MM(acc[cc][:, :T], w[:, k, cc * 128:(cc + 1) * 128], actT[:, k0 + k, :], (gi == 0 and k == 0), (gi == 2 and k == nk - 1),
                           [w, actT], [acc[cc]])
            for cc in range(4):
                xk = xT[g * 4 + cc]
                TT(xk[:, :T], xk[:, :T], acc[cc][:, :T], ALU.add, [xk, acc[cc]], [xk])

        if last_layer:
            fence()
            pb = bank()
            for k in range(16):
                sq = sqs[k % 2]
                ACT(sq[:, :T], xT[k][:, :T], AF.Square, [xT[k]], [sq])
                MM(pb[:, :T], ones_b, sq[:, :T], k == 0, k == 15, [sq, cbf], [pb])
            ACT(rstd[:, :T], pb[:, :T], AF.Sqrt, [pb], [rstd], bias=EPS, scale=1.0 / D)
            RCP(rstd[:, :T], rstd[:, :T], [rstd], [rstd])
            yT = [alloc([128, T], F32, "yT%d" % i) for i in range(4)]
            ytk = [alloc([128, 512], F32, "ytk%d" % i) for i in range(3)]
            yrr = [0]
            for g in range(4):
                for cc in range(4):
                    k = g * 4 + cc
                    STT(yT[cc][:, :T], xT[k][:, :T], pkl[:, PK_GFIN + k:PK_GFIN + k + 1], rstd[:, :T], ALU.mult, ALU.mult, [xT[k], pkl, rstd], [yT[cc]])
                for si, sgm in enumerate(segs):
                    for s in range(nsub):
                        t0 = sgm["col0"] + s * c
                        pT = bank()
                        for cc in range(4):
                            MM(pT[:c, cc * 128:(cc + 1) * 128], yT[cc][:, t0:t0 + c], ident_f, True, True, [yT[cc], cst], [pT])
                        yt = ytk[yrr[0] % 3]
                        yrr[0] += 1
                        CP(yt[:c, :], pT[:c, :], [pT], [yt], eng="act" if yrr[0] % 2 else "dve")
                        DMA("pool", sgm["yout"][s * c:(s + 1) * c, g * 512:(g + 1) * 512], yt[:c, :], [yt], [])

    def load_x(segs, c, nsub):
        xs = [alloc([128, D], F32, "xs%d" % i) for i in range(2)]
        rr = 0
        for si, sgm in enumerate(segs):
            for s in range(nsub):
                t0 = sgm["col0"] + s * c
                st = xs[rr % 2]
                rr += 1
                DMA("pool", st[:c, :], sgm["xin"][s * c:(s + 1) * c, :], [], [st])
                for g in range(4):
                    pT = bank()
                    for cc in range(4):
                        k = g * 4 + cc
                        MM(pT[:, cc * c:(cc + 1) * c], st[:c, k * 128:(k + 1) * 128], ident_f[:c, :c], True, True, [st, cst], [pT])
                    for cc in range(4):
                        k = g * 4 + cc
                        CP(xT[k][:, t0:t0 + c], pT[:, cc * c:(cc + 1) * c], [pT], [xT[k]], eng="act" if cc % 2 else "dve")

    for i in range(NT):
        sgm = dict(seq=seq_p, col0=0, L=TP, blk0=i * (TP // 128),
                   xin=x_p[i * TP:(i + 1) * TP, :], yout=y_p[i * TP:(i + 1) * TP, :],
                   kout=[fk_p[l, i * TP:(i + 1) * TP, :] for l in range(DEPTH)],
                   vout=[fv_p[l, i * TP:(i + 1) * TP, :] for l in range(DEPTH)],
                   lout=[fl_p[l, i * TP:(i + 1) * TP, :] for l in range(DEPTH)],
                   kbuf=[kvK[l][i] for l in range(DEPTH)], vbuf=[kvV[l][i] for l in range(DEPTH)],
                   gs_out=[gs_p[l] for l in range(DEPTH)], gc_out=[gc_p[l] for l in range(DEPTH)],
                   lh_out=[lh_p[l] for l in range(DEPTH)], lc_out=[lc_p[l] for l in range(DEPTH)], fc_out=[fc_p[l] for l in range(DEPTH)])
        for l in range(DEPTH):
            run_tile([sgm], TP, 128, l, (lambda sgm=sgm: load_x([sgm], 128, TP // 128)) if l == 0 else None,
                     l == DEPTH - 1, None, i == NT - 1)
    dummyK = [Buf("dk") for _ in range(DEPTH)]
    dummyV = [Buf("dv") for _ in range(DEPTH)]
    ssegs = []
    for si in range(2):
        ssegs.append(dict(seq=seq_s[si], col0=si * LS, L=LS, blk0=PB,
                          xin=x_s[si], yout=y_s[si],
                          kout=[fk_s[l, si] for l in range(DEPTH)], vout=[fv_s[l, si] for l in range(DEPTH)],
                          lout=[fl_s[l, si] for l in range(DEPTH)], kbuf=dummyK, vbuf=dummyV,
                          gs_out=[gs_s[l, si] for l in range(DEPTH)], gc_out=[gc_s[l, si] for l in range(DEPTH)],
                          lh_out=[lh_s[l, si] for l in range(DEPTH)], lc_out=[lc_s[l, si] for l in range(DEPTH)],
                          fc_out=[fc_s[l, si] for l in range(DEPTH)]))
    for l in range(DEPTH):
        run_tile(ssegs, 2 * LS, LS, l, (lambda: load_x(ssegs, LS, 1)) if l == 0 else None, l == DEPTH - 1, None, True)
    P.emit()
    return nc


def _pack_small(inp, l):
    pk = np.zeros((128, NPK), np.float32)

    def fm(v, n):
        return np.ascontiguousarray(v.reshape(n, 128).T)
    pk[:, PK_GMIX:PK_GMIX + 16] = fm(inp["norm_mix_g"][l], 16)
    pk[:, PK_GFFN:PK_GFFN + 16] = fm(inp["norm_ffn_g"][l], 16)
    pk[:, PK_GFIN:PK_GFIN + 16] = fm(inp["final_norm_g"], 16)
    pk[:, PK_GCW:PK_GCW + 48] = inp["gdn_conv_w"][l].reshape(4, 12, 128).transpose(2, 1, 0).reshape(128, 48)
    pk[:, PK_LCW:PK_LCW + 16] = inp["lru_conv_w"][l].reshape(4, 4, 128).transpose(2, 1, 0).reshape(128, 16)
    pk[:, PK_LCB:PK_LCB + 4] = fm(inp["lru_conv_b"][l], 4)
    pk[:, PK_LBA:PK_LBA + 4] = fm(inp["lru_b_a"][l], 4)
    pk[:, PK_LBX:PK_LBX + 4] = fm(inp["lru_b_x"][l], 4)
    pk[:, PK_LAM:PK_LAM + 4] = fm(inp["lru_lambda"][l], 4)
    pk[:, PK_LNG:PK_LNG + 4] = fm(inp["lru_norm_g"][l], 4)
    pk[:, PK_FCW:PK_FCW + 132] = inp["ffn_conv_w"][l].reshape(3, 44, 128).transpose(2, 1, 0).reshape(128, 132)
    pk[:, PK_GNG4:PK_GNG4 + 512] = np.tile(inp["gdn_norm_g"][l], 4)[None, :]
    pk[:, PK_FNG:PK_FNG + 1024] = inp["fox_norm_g"][l].reshape(1024)[None, :]
    pk[:, PK_ALOG:PK_ALOG + 4] = inp["gdn_a_log"][l][None, :]
    pk[:, PK_DTB:PK_DTB + 4] = inp["gdn_dt_bias"][l][None, :]
    pk[:, PK_FB:PK_FB + 8] = inp["fox_f_bias"][l][None, :]
    return pk


def _consts():
    c = np.zeros((128, NCST), np.float32)
    c[:, C_ID:C_ID + 128] = np.eye(128)
    c[:, C_TRI:C_TRI + 128] = np.triu(np.ones((128, 128)))
    c[:, C_ONE:C_ONE + 128] = 1.0
    c[:, C_MLS:C_MLS + 128] = np.tril(np.ones((128, 128)), -1)
    return c


_NC_CACHE = {}


def run(inp, n_cores=8, SEQ=4096, PAST=4096):
    inp = {k: np.asarray(v) for k, v in inp.items()}
    DEPTH = 2
    key = (SEQ, PAST)
    if key not in _NC_CACHE:
        _NC_CACHE[key] = build(SEQ=SEQ, PAST=PAST)
    nc = _NC_CACHE[key]
    BP = inp["x_prompt"].shape[0]
    BS = inp["x_sample"].shape[0]
    pk = np.stack([_pack_small(inp, l) for l in range(DEPTH)])
    cst = _consts()
    c32 = np.ascontiguousarray

    def fmT(a, n, j):
        sh = a.shape[:-2]
        return c32(a.reshape(sh + (j, n, 128)).transpose(tuple(range(len(sh))) + (len(sh) + 2, len(sh) + 1, len(sh))).reshape(sh + (128, n * j)))

    in_maps = []
    for core in range(n_cores):
        b = core % BP
        ss = [(2 * (core % (BS // 2))), (2 * (core % (BS // 2)) + 1)]
        m = {
            "x_p": c32(inp["x_prompt"][b]),
            "x_s": c32(inp["x_sample"][ss]),
            "ck": c32(inp["cache_fox_k"][:, ss].reshape(DEPTH, 2, PAST, 1024)),
            "cv": c32(inp["cache_fox_v"][:, ss].reshape(DEPTH, 2, PAST, 1024)),
            "cl": c32(inp["cache_fox_logf"][:, ss]),
            "sg": c32(inp["state_gdn"][:, ss]),
            "sgc": fmT(inp["state_gdn_conv"][:, ss], 12, 3),
            "sl": fmT(inp["state_lru"][:, ss][:, :, None, :], 4, 1),
            "slc": fmT(inp["state_lru_conv"][:, ss], 4, 3),
            "sfc": fmT(inp["state_ffn_conv"][:, ss], 44, 2),
            "w_in": inp["w_in"], "w_out": inp["w_out"], "w_up": inp["ffn_w_up"], "w_dn": inp["ffn_w_down"],
            "lwa": inp["lru_w_a"], "lwx": inp["lru_w_x"], "pk": pk, "cst": cst,
        }
        in_maps.append(m)
    res = run_bass_kernel_spmd(nc, in_maps, core_ids=list(range(n_cores)))
    R = res.results

    def unfm(a, n, j):
        sh = a.shape[:-2]
        return c32(a.reshape(sh + (128, n, j)).transpose(tuple(range(len(sh))) + (len(sh) + 2, len(sh) + 1, len(sh))).reshape(sh + (j, n * 128)))

    pc = list(range(BP))
    sc = list(range(BS // 2))
    y_p = np.stack([R[c]["y_p"] for c in pc])
    y_s = np.concatenate([R[c]["y_s"] for c in sc])
    fk_p = np.stack([R[c]["fk_p"] for c in pc], 1).reshape(DEPTH, BP, SEQ, 8, 128)
    fv_p = np.stack([R[c]["fv_p"] for c in pc], 1).reshape(DEPTH, BP, SEQ, 8, 128)
    fl_p = np.stack([R[c]["fl_p"] for c in pc], 1)
    gs_p = np.stack([R[c]["gs_p"] for c in pc], 1)
    gc_p = unfm(np.stack([R[c]["gc_p"] for c in pc], 1), 12, 3)
    lh_p = unfm(np.stack([R[c]["lh_p"] for c in pc], 1), 4, 1)[:, :, 0, :]
    lc_p = unfm(np.stack([R[c]["lc_p"] for c in pc], 1), 4, 3)
    fc_p = unfm(np.stack([R[c]["fc_p"] for c in pc], 1), 44, 2)
    fk_s = np.concatenate([R[c]["fk_s"] for c in sc], 1).reshape(DEPTH, BS, 16, 8, 128)
    fv_s = np.concatenate([R[c]["fv_s"] for c in sc], 1).reshape(DEPTH, BS, 16, 8, 128)
    fl_s = np.concatenate([R[c]["fl_s"] for c in sc], 1)
    gs_s = np.concatenate([R[c]["gs_s"] for c in sc], 1)
    gc_s = unfm(np.concatenate([R[c]["gc_s"] for c in sc], 1), 12, 3)
    lh_s = unfm(np.concatenate([R[c]["lh_s"] for c in sc], 1), 4, 1)[:, :, 0, :]
    lc_s = unfm(np.concatenate([R[c]["lc_s"] for c in sc], 1), 4, 3)
    fc_s = unfm(np.concatenate([R[c]["fc_s"] for c in sc], 1), 44, 2)
    outs = (y_p, y_s, fk_p, fv_p, fl_p, gs_p, gc_p, lh_p, lc_p, fc_p, fk_s, fv_s, fl_s, gs_s, gc_s, lh_s, lc_s, fc_s)
    return tuple(np.ascontiguousarray(o, dtype=np.float32) for o in outs)


def kernel(**inputs):
    return run(inputs, n_cores=8, SEQ=4096, PAST=4096)
```

```python
import numpy as np
from contextlib import ExitStack
import concourse.bass as bass
import concourse.mybir as mybir
from concourse.bass_utils import run_bass_kernel_spmd

F32 = mybir.dt.float32
BF16 = mybir.dt.bfloat16
AF = mybir.ActivationFunctionType
ALU = mybir.AluOpType

ENGS = ("pe", "act", "dve", "pool", "sp")
D = 2048
DFF = 5632
INW = 6160
EPS = 1e-6
NPK = 1816
PK_GMIX, PK_GFFN, PK_GFIN, PK_GCW, PK_LCW, PK_LCB, PK_LBA, PK_LBX, PK_LAM, PK_LNG, PK_FCW = 0, 16, 32, 48, 96, 112, 116, 120, 124, 128, 132
PK_GNG4, PK_FNG, PK_ALOG, PK_DTB, PK_FB = 264, 776, 1800, 1804, 1808
C_ID, C_TRI, C_ONE, C_MLS = 0, 128, 256, 384
NCST = 512


class Buf:
    __slots__ = ("name", "w", "r", "excl")

    def __init__(self, name="", excl=False):
        self.name = name
        self.w = None
        self.r = {}
        self.excl = excl


class Op:
    __slots__ = ("eng", "fn", "deps", "dma", "tok", "need_sig", "idx")

    def __init__(self, eng, fn, dma):
        self.eng = eng
        self.fn = fn
        self.dma = dma
        self.deps = set()
        self.tok = None
        self.need_sig = False


class Prog:
    def __init__(self, nc, n_dma_sems=12):
        self.nc = nc
        self.ops = []
        self.n_dma_sems = n_dma_sems
        self.guard = None
        self.stopped = False
        self.nfence = 0

    def add(self, eng, fn, reads=(), writes=(), dma=False, guard=True):
        import os as _os
        if self.stopped or len(self.ops) >= int(_os.environ.get("STOPOPS", "100000000")):
            return None
        op = Op(eng, fn, dma)
        idx = len(self.ops)
        op.idx = idx
        reads = list(reads)
        if guard and self.guard is not None:
            reads.append(self.guard)
        for b in reads:
            if b.w is not None:
                op.deps.add(b.w)
            if b.excl:
                for key, ridx in b.r.items():
                    if key != eng:
                        op.deps.add(ridx)
        for b in writes:
            if b.w is not None:
                op.deps.add(b.w)
            for ridx in b.r.values():
                op.deps.add(ridx)
        op.deps.discard(idx)
        if eng == "pe" and not dma:
            op.deps = {d for d in op.deps if not (self.ops[d].eng == "pe" and not self.ops[d].dma)}
        for d in op.deps:
            self.ops[d].need_sig = True
        self.ops.append(op)
        for b in reads:
            key = ("d", idx) if dma else eng
            b.r[key] = idx
        for b in writes:
            b.w = idx
            b.r = {}
        return op

    def emit(self):
        nc = self.nc
        ops = self.ops
        nds = self.n_dma_sems
        dslot = {}
        dcount = {q: [0] * nds for q in ("sp", "pool")}
        drr = {q: 0 for q in ("sp", "pool")}
        last_on_slot = {}
        prewait = {}
        for op in ops:
            if op.dma:
                i = drr[op.eng] % nds
                drr[op.eng] += 1
                if (op.eng, i) in last_on_slot:
                    prewait[op.idx] = last_on_slot[(op.eng, i)]
                last_on_slot[(op.eng, i)] = op.idx
                dcount[op.eng][i] += 1
                dslot[op.idx] = (op.eng, i, dcount[op.eng][i] * 16)
        ordn = {}
        ecnt = {e: 0 for e in ENGS}
        prev_on = {e: None for e in ENGS}
        vc = [None] * len(ops)

        def merge(a, b):
            for kk, vv in b.items():
                if a.get(kk, 0) < vv:
                    a[kk] = vv
        for op in ops:
            v = {}
            for d in op.deps:
                merge(v, vc[d])
            if op.dma:
                q, i, val = dslot[op.idx]
                v[("d", q, i)] = max(v.get(("d", q, i), 0), val)
            else:
                ecnt[op.eng] += 1
                ordn[op.idx] = ecnt[op.eng]
                if prev_on[op.eng] is not None:
                    merge(v, vc[prev_on[op.eng]])
                v[("e", op.eng)] = ecnt[op.eng]
                prev_on[op.eng] = op.idx
            vc[op.idx] = v
        per_eng = {e: [op for op in ops if op.eng == e] for e in ENGS}

        def implied(known, d):
            if ops[d].dma:
                q, i, val = dslot[d]
                return known.get(("d", q, i), 0) >= val
            return known.get(("e", ops[d].eng), 0) >= ordn[d]

        def plan(e):
            known = {}
            out = []
            for op in per_eng[e]:
                ws = []
                cand = sorted(op.deps, reverse=True)
                if op.idx in prewait:
                    cand.append(prewait[op.idx])
                for d in cand:
                    if not implied(known, d):
                        ws.append(d)
                        merge(known, vc[d])
                out.append((op, ws))
            return out
        plans = {e: plan(e) for e in ENGS}
        for op in ops:
            op.need_sig = op.dma
        for e in ENGS:
            for op, ws in plans[e]:
                for d in ws:
                    ops[d].need_sig = True
        with ExitStack() as es:
            esem = {e: es.enter_context(nc.semaphore("s_" + e)) for e in ENGS}
            dsem = {q: [es.enter_context(nc.semaphore("d_%s%d" % (q, i))) for i in range(nds)] for q in ("sp", "pool")}
            ecount = {e: 0 for e in ENGS}
            for op in ops:
                if op.dma:
                    q, i, val = dslot[op.idx]
                    op.tok = (dsem[q][i], val)
                elif op.need_sig:
                    ecount[op.eng] += 1
                    op.tok = (esem[op.eng], ecount[op.eng])
            block = es.enter_context(nc.Block())

            def run(e, eng):
                for op, ws in plans[e]:
                    for d in ws:
                        sem, val = ops[d].tok
                        eng.wait_ge(sem, val)
                    ins = op.fn(eng)
                    if op.need_sig:
                        ins.then_inc(op.tok[0], 16 if op.dma else 1)
                if e == "sp":
                    for q in dsem:
                        for i in range(nds):
                            if dcount[q][i] > 0:
                                eng.wait_ge(dsem[q][i], dcount[q][i] * 16)

            @block.tensor
            def _(eng):
                run("pe", eng)

            @block.scalar
            def _(eng):
                run("act", eng)

            @block.vector
            def _(eng):
                run("dve", eng)

            @block.gpsimd
            def _(eng):
                run("pool", eng)

            @block.sync
            def _(eng):
                run("sp", eng)


class T_:
    def __init__(self, t, name, excl=False):
        self.t = t
        self.b = Buf(name, excl)

    def __getitem__(self, k):
        return self.t[k]


def build(SEQ=4096, PAST=4096, TP=512, DEPTH=2):
    nc = bass.Bass("TRN2", target_bir_lowering=False)
    NT = SEQ // TP
    PB = PAST // 128
    LS = 16

    def din(name, shape):
        return nc.dram_tensor(name, list(shape), F32, kind="ExternalInput").ap()

    def dout(name, shape):
        return nc.dram_tensor(name, list(shape), F32, kind="ExternalOutput").ap()

    x_p = din("x_p", [SEQ, D]); x_s = din("x_s", [2, LS, D])
    ck = din("ck", [DEPTH, 2, PAST, 1024]); cv = din("cv", [DEPTH, 2, PAST, 1024]); cl = din("cl", [DEPTH, 2, PAST, 8])
    sg = din("sg", [DEPTH, 2, 4, 128, 128]); sgc = din("sgc", [DEPTH, 2, 128, 36])
    sl = din("sl", [DEPTH, 2, 128, 4]); slc = din("slc", [DEPTH, 2, 128, 12]); sfc = din("sfc", [DEPTH, 2, 128, 88])
    w_in = din("w_in", [DEPTH, D, INW]); w_out = din("w_out", [DEPTH, D, D])
    w_up = din("w_up", [DEPTH, D, 2 * DFF]); w_dn = din("w_dn", [DEPTH, DFF, D])
    lwa = din("lwa", [DEPTH, 4, 128, 128]); lwx = din("lwx", [DEPTH, 4, 128, 128])
    pk_d = din("pk", [DEPTH, 128, NPK]); cst_d = din("cst", [128, NCST])

    y_p = dout("y_p", [SEQ, D]); y_s = dout("y_s", [2, LS, D])
    fk_p = dout("fk_p", [DEPTH, SEQ, 1024]); fv_p = dout("fv_p", [DEPTH, SEQ, 1024]); fl_p = dout("fl_p", [DEPTH, SEQ, 8])
    gs_p = dout("gs_p", [DEPTH, 4, 128, 128]); gc_p = dout("gc_p", [DEPTH, 128, 36]); lh_p = dout("lh_p", [DEPTH, 128, 4])
    lc_p = dout("lc_p", [DEPTH, 128, 12]); fc_p = dout("fc_p", [DEPTH, 128, 88])
    fk_s = dout("fk_s", [DEPTH, 2, LS, 1024]); fv_s = dout("fv_s", [DEPTH, 2, LS, 1024]); fl_s = dout("fl_s", [DEPTH, 2, LS, 8])
    gs_s = dout("gs_s", [DEPTH, 2, 4, 128, 128]); gc_s = dout("gc_s", [DEPTH, 2, 128, 36]); lh_s = dout("lh_s", [DEPTH, 2, 128, 4])
    lc_s = dout("lc_s", [DEPTH, 2, 128, 12]); fc_s = dout("fc_s", [DEPTH, 2, 128, 88])

    P = Prog(nc)
    sb_base = (nc.sbuf_base + 63) // 64 * 64
    sb_top = nc.sbuf_top
    cur = [sb_base]
    cnt = [0]

    def alloc(shape, dt, name="t"):
        nbytes = int(np.prod(shape[1:])) * (4 if dt == F32 else 2)
        nbytes = (nbytes + 63) // 64 * 64
        off = cur[0]
        cur[0] += nbytes
        assert cur[0] <= sb_top, ("SBUF overflow", name, cur[0], sb_top)
        cnt[0] += 1
        t = nc.alloc_sbuf_tensor_at("%s_%d" % (name, cnt[0]), list(shape), dt, offset=off)
        return T_(t, name)

    ps = []
    for i in range(8):
        ps.append(T_(nc.alloc_psum_tensor("ps%d" % i, [128, 512], F32), "ps%d" % i, excl=True))

    def bl(xs):
        return [x.b if isinstance(x, T_) else x for x in xs]

    def ACT(out, in_, func, R, W, **kw):
        P.add("act", lambda e: e.activation(out=out, in_=in_, func=func, **kw), bl(R), bl(W))

    def TT(out, in0, in1, op, R, W, eng="dve"):
        P.add(eng, lambda e: e.tensor_tensor(out=out, in0=in0, in1=in1, op=op), bl(R), bl(W))

    def TS(out, in0, s1, s2, op0, op1, R, W, eng="dve"):
        if s2 is None:
            P.add(eng, lambda e: e.tensor_scalar(out=out, in0=in0, scalar1=s1, scalar2=None, op0=op0), bl(R), bl(W))
        else:
            P.add(eng, lambda e: e.tensor_scalar(out=out, in0=in0, scalar1=s1, scalar2=s2, op0=op0, op1=op1), bl(R), bl(W))

    def STT(out, in0, sc, in1, op0, op1, R, W):
        P.add("dve", lambda e: e.scalar_tensor_tensor(out=out, in0=in0, scalar=sc, in1=in1, op0=op0, op1=op1), bl(R), bl(W))

    def CP(out, in_, R, W, eng="dve"):
        if eng == "act":
            P.add(eng, lambda e: e.activation(out=out, in_=in_, func=AF.Copy), bl(R), bl(W))
        else:
            P.add(eng, lambda e: e.tensor_copy(out=out, in_=in_), bl(R), bl(W))

    def RCP(out, in_, R, W):
        P.add("dve", lambda e: e.reciprocal(out=out, in_=in_), bl(R), bl(W))

    def MM(out, lhsT, rhs, start, stop, R, W):
        P.add("pe", lambda e: e.matmul(out, lhsT=lhsT, rhs=rhs, start=start, stop=stop), bl(R), bl(W))

    def DMA(q, out, in_, R, W, guard=True, slow=False):
        if slow:
            P.add(q, lambda e: e.dma_start(out=out, in_=in_, allow_slow_non_contiguous=True), bl(R), bl(W), dma=True, guard=guard)
        else:
            P.add(q, lambda e: e.dma_start(out=out, in_=in_), bl(R), bl(W), dma=True, guard=guard)

    def MEMSET(ap, val, W, eng="dve"):
        import os as _os
        if _os.environ.get("SKIP_POOLMS") and eng == "pool":
            eng = "dve"
        P.add(eng, lambda e: e.memset(ap, val), [], bl(W))

    cst = alloc([128, NCST], F32, "cst")
    cbf = alloc([128, 384], BF16, "cbf")
    pk = [alloc([128, NPK], F32, "pk%d" % l) for l in range(DEPTH)]
    lw = [[alloc([128, 4, 128], BF16, "lwa%d" % l), alloc([128, 4, 128], BF16, "lwx%d" % l)] for l in range(DEPTH)]
    c1 = [alloc([128, 4], F32, "c1_%d" % l) for l in range(DEPTH)]
    nalog = [alloc([128, 4], F32, "nalog%d" % l) for l in range(DEPTH)]
    xT = [alloc([128, TP], F32, "xT%d" % k) for k in range(16)]
    hT = [alloc([128, TP], BF16, "hT%d" % k) for k in range(16)]
    oT = [alloc([128, TP], BF16, "oT%d" % k) for k in range(16)]
    NW = 2
    wb = [alloc([128, 16, 512], BF16, "wb%d" % i) for i in range(NW)]
    wsm = alloc([128, 16, 16], BF16, "wsm")
    rstd = alloc([128, TP], F32, "rstd")
    sqs = [alloc([128, TP], BF16, "sq%d" % i) for i in range(2)]
    tiny = alloc([128, 64], F32, "tiny")
    ffr = alloc([128, 4, 8], F32, "ffr")
    junk = alloc([128, 128], F32, "junk")
    sqg = [alloc([128, TP], BF16, "sqg%d" % i) for i in range(2)]

    DMA("sp", cst[:], cst_d, [], [cst], guard=False)
    CP(cbf[:, 0:384], cst[:, 0:384], [cst], [cbf])
    ident_f = cst.t[:, C_ID:C_ID + 128]
    tri_f = cst.t[:, C_TRI:C_TRI + 128]
    ones_f = cst.t[:, C_ONE:C_ONE + 128]
    mls_f = cst.t[:, C_MLS:C_MLS + 128]
    ident_b = cbf.t[:, 0:128]
    tri_b = cbf.t[:, 128:256]
    ones_b = cbf.t[:, 256:384]
    for l in range(DEPTH):
        DMA("sp", pk[l][:], pk_d[l], [], [pk[l]], guard=False)
        DMA("pool", lw[l][0][:], lwa[l].rearrange("n c d -> c n d"), [], [lw[l][0]], guard=False)
        DMA("pool", lw[l][1][:], lwx[l].rearrange("n c d -> c n d"), [], [lw[l][1]], guard=False)
        import os as _os
        if _os.environ.get("SKIP_ACT"):
            continue
        o_ = 16 * l
        ACT(tiny[:, o_:o_ + 4], pk[l][:, PK_LAM:PK_LAM + 4], AF.Exp, [pk[l]], [tiny], scale=-1.0)
        ACT(tiny[:, o_ + 4:o_ + 8], tiny[:, o_:o_ + 4], AF.Ln, [tiny], [tiny], bias=1.0)
        ACT(c1[l][:], tiny[:, o_ + 4:o_ + 8], AF.Copy, [tiny], [c1[l]], scale=-8.0)
        ACT(tiny[:, o_ + 8:o_ + 12], pk[l][:, PK_ALOG:PK_ALOG + 4], AF.Exp, [pk[l]], [tiny])
        ACT(nalog[l][:], tiny[:, o_ + 8:o_ + 12], AF.Copy, [tiny], [nalog[l]], scale=-1.0)

    class Seq:
        pass

    def mkseq(name, is_prompt, sidx):
        s = Seq()
        s.name = name
        s.prompt = is_prompt
        s.sidx = sidx
        s.S = [alloc([128, 4, 128], F32, name + "S") for _ in range(DEPTH)]
        s.gch = [alloc([128, 12, 3], F32, name + "gch") for _ in range(DEPTH)]
        s.lh = [alloc([128, 4], F32, name + "lh") for _ in range(DEPTH)]
        s.lch = [alloc([128, 4, 3], F32, name + "lch") for _ in range(DEPTH)]
        s.fch = [alloc([128, 44, 2], F32, name + "fch") for _ in range(DEPTH)]
        s.Fb = [alloc([128, 8], F32, name + "Fb") for _ in range(DEPTH)]
        if is_prompt:
            s.Fk = [alloc([128, SEQ // 128, 8], F32, name + "Fk") for _ in range(DEPTH)]
            s.Fref = [alloc([128, SEQ // 128, 8], F32, name + "Fref") for _ in range(DEPTH)]
        return s

    seq_p = mkseq("p", True, 0)
    seq_s = [mkseq("s%d" % i, False, i) for i in range(2)]
    for l in range(DEPTH):
        for t in (seq_p.S[l], seq_p.gch[l], seq_p.lh[l], seq_p.lch[l], seq_p.fch[l], seq_p.Fb[l]):
            MEMSET(t[:], 0.0, [t], eng="pool")
        for i, s in enumerate(seq_s):
            DMA("sp", s.S[l][:], sg[l, i].rearrange("h k v -> k h v"), [], [s.S[l]], guard=False)
            DMA("sp", s.gch[l][:], sgc[l, i].rearrange("p (c j) -> p c j", j=3), [], [s.gch[l]], guard=False)
            DMA("sp", s.lh[l][:], sl[l, i], [], [s.lh[l]], guard=False)
            DMA("sp", s.lch[l][:], slc[l, i].rearrange("p (c j) -> p c j", j=3), [], [s.lch[l]], guard=False)
            DMA("sp", s.fch[l][:], sfc[l, i].rearrange("p (c j) -> p c j", j=2), [], [s.fch[l]], guard=False)
            MEMSET(s.Fb[l][:], 0.0, [s.Fb[l]], eng="pool")

    region0 = cur[0]
    regbuf = Buf("region")
    P.guard = regbuf
    kvK = [[Buf("kvK%d_%d" % (l, i)) for i in range(NT)] for l in range(DEPTH)]
    kvV = [[Buf("kvV%d_%d" % (l, i)) for i in range(NT)] for l in range(DEPTH)]

    def fence(keep=None):
        cur[0] = region0 if keep is None else keep
        P.nfence += 1
        import os as _os
        if P.nfence > int(_os.environ.get("STOPF", "100000")):
            P.stopped = True
            return
        P.add("dve", lambda e: e.memset(tiny[:, 60:61], 0.0), [], [regbuf], guard=False)

    wrr = [0]

    NSLOT = 50 * DEPTH
    wsc = nc.dram_tensor("wsc", [NSLOT, 128, 16 * 512], BF16).ap()
    wsmsc = nc.dram_tensor("wsmsc", [DEPTH, 128, 256], BF16).ap()
    wslot = {}
    wsbuf = {}

    def wload(key, src_rows_ap, nk, ncols):
        assert ncols == 512
        w = wb[wrr[0] % NW]
        wrr[0] += 1
        if key not in wslot:
            slot = len(wslot)
            assert slot < NSLOT
            wslot[key] = slot
            wsbuf[key] = Buf("wsc%d" % slot)
            DMA("pool", w[:, 0:nk, 0:ncols], src_rows_ap.rearrange("(k p) n -> p k n", p=128), [], [w], guard=False)
            DMA("pool", wsc[slot][:, 0:nk * 512].rearrange("p (k n) -> p k n", n=512), w[:, 0:nk, 0:ncols], [w], [wsbuf[key]], guard=False)
        else:
            slot = wslot[key]
            DMA("sp", w[:, 0:nk, 0:ncols], wsc[slot][:, 0:nk * 512].rearrange("p (k n) -> p k n", n=512), [wsbuf[key]], [w], guard=False)
        return w

    psrr = [0]

    def bank(group=(0, 1, 2, 3)):
        b = ps[group[psrr[0] % len(group)]]
        psrr[0] += 1
        return b

    def rmsnorm_to_hT(T, gcol, l):
        pb = bank()
        for k in range(16):
            sq = sqs[k % 2]
            ACT(sq[:, :T], xT[k][:, :T], AF.Square, [xT[k]], [sq])
            MM(pb[:, :T], ones_b, sq[:, :T], k == 0, k == 15, [sq, cbf], [pb])
        ACT(rstd[:, :T], pb[:, :T], AF.Sqrt, [pb], [rstd], bias=EPS, scale=1.0 / D)
        RCP(rstd[:, :T], rstd[:, :T], [rstd], [rstd])
        for k in range(16):
            STT(hT[k][:, :T], xT[k][:, :T], pk[l][:, gcol + k:gcol + k + 1], rstd[:, :T], ALU.mult, ALU.mult,
                [xT[k], rstd, pk[l]], [hT[k]])

    def fm_cols(w, T, ncc, evac, src=None):
        src = src or hT
        for cc in range(ncc):
            pb = bank()
            for k in range(16):
                MM(pb[:, :T], w[:, k, cc * 128:(cc + 1) * 128], src[k][:, :T], k == 0, k == 15, [w, src[k]], [pb])
            evac(cc, pb)

    def tm_cols(w, cols, ncols, c, evac):
        pb = bank()
        for k in range(16):
            MM(pb[:c, :ncols], hT[k][:, cols], w[:, k, 0:ncols], k == 0, k == 15, [w, hT[k]], [pb])
        evac(pb)

    def run_tile(segs, T, c, l, src_loader, last_layer, y_writer, is_last_tile):
        nseg = len(segs)
        L = segs[0]["L"]
        nsub = L // c
        win = w_in[l]
        pkl = pk[l]
        if src_loader is not None:
            fence()
            src_loader()
        fence()
        rmsnorm_to_hT(T, PK_GMIX, l)
        fence()
        qkvT = [alloc([128, 4, T], BF16, "gq"), alloc([128, 4, T], BF16, "gk"), alloc([128, 4, T], BF16, "gv")]
        zg = alloc([128, nseg * nsub, 512], BF16, "zg")
        bt = alloc([128, nseg * nsub, 4], F32, "beta")
        gg = alloc([128, nseg * nsub, 4], F32, "gg")
        tg = alloc([128, 16], F32, "tg")
        gdn_mark = cur[0]
        raws = [alloc([128, nseg, 3 + L], F32, "raw%d" % i) for i in range(2)]
        cvs = [alloc([128, nseg, L], F32, "cv%d" % i) for i in range(2)]
        sil = [alloc([128, T], F32, "sil%d" % i) for i in range(2)]
        rq = alloc([128, T], F32, "rq")
        for grp in range(3):
            w = wload((l, "qkv", grp), win[:, grp * 512:(grp + 1) * 512], 16, 512)

            def ev(cc, pb, grp=grp):
                ch = grp * 4 + cc
                raw = raws[ch % 2]
                cv_ = cvs[ch % 2]
                sl_ = sil[ch % 2]
                for si, sgm in enumerate(segs):
                    gch = sgm["seq"].gch[l]
                    CP(raw[:, si, 0:3], gch[:, ch, :], [gch], [raw], eng="pool")
                P.add("act", lambda e: e.activation(out=raw[:, :, 3:3 + L], in_=pb[:, :T].rearrange("p (s l) -> p s l", s=nseg), func=AF.Copy),
                      bl([pb]) + [regbuf], bl([raw]), guard=False)
                for si, sgm in enumerate(segs):
                    gch = sgm["seq"].gch[l]
                    CP(gch[:, ch, :], raw[:, si, L:L + 3], [raw], [gch], eng="pool")
                cw = PK_GCW + ch * 4
                TS(cv_[:], raw[:, :, 0:L], pkl[:, cw:cw + 1], None, ALU.mult, None, [raw, pkl], [cv_])
                for j in range(1, 4):
                    STT(cv_[:], raw[:, :, j:j + L], pkl[:, cw + j:cw + j + 1], cv_[:], ALU.mult, ALU.add, [raw, pkl, cv_], [cv_])
                dst = qkvT[grp]
                if grp == 2:
                    ACT(dst[:, cc, :], cv_[:].rearrange("p s l -> p (s l)"), AF.Silu, [cv_], [dst])
                else:
                    ACT(sl_[:, :T], cv_[:].rearrange("p s l -> p (s l)"), AF.Silu, [cv_], [sl_])
                    sq = sqg[ch % 2]
                    ACT(sq[:, :T], sl_[:, :T], AF.Square, [sl_], [sq])
                    pb2 = bank((4, 5))
                    MM(pb2[:, :T], ones_b, sq[:, :T], True, True, [sq, cbf], [pb2])
                    ACT(rq[:, :T], pb2[:, :T], AF.Sqrt, [pb2], [rq], bias=EPS, scale=1.0)
                    RCP(rq[:, :T], rq[:, :T], [rq], [rq])
                    STT(dst[:, cc, :], sl_[:, :T], (128.0 ** -0.5) if grp == 0 else 1.0, rq[:, :T], ALU.mult, ALU.mult, [sl_, rq], [dst])
            fm_cols(w, T, 4, ev)
        w = wload((l, "z"), win[:, 1536:2048], 16, 512)
        if ("wsm", l) not in wsbuf:
            wsbuf[("wsm", l)] = Buf("wsmsc%d" % l)
            DMA("pool", wsm[:, :, 0:8], win[:, 2048:2056].rearrange("(k p) n -> p k n", p=128), [], [wsm], guard=False)
            DMA("pool", wsm[:, :, 8:16], win[:, 5128:5136].rearrange("(k p) n -> p k n", p=128), [], [wsm], guard=False)
            DMA("pool", wsmsc[l].rearrange("p (k n) -> p k n", n=16), wsm[:, :, :], [wsm], [wsbuf[("wsm", l)]], guard=False)
        else:
            DMA("sp", wsm[:, :, :], wsmsc[l].rearrange("p (k n) -> p k n", n=16), [wsbuf[("wsm", l)]], [wsm], guard=False)
        for si, sgm in enumerate(segs):
            for s in range(nsub):
                cols = slice(sgm["col0"] + s * c, sgm["col0"] + (s + 1) * c)
                bi = si * nsub + s

                def evz(pb, bi=bi):
                    ACT(zg[:c, bi, :], pb[:c, :], AF.Silu, [pb], [zg])
                    TT(zg[:c, bi, :], zg[:c, bi, :], pkl[:c, PK_GNG4:PK_GNG4 + 512], ALU.mult, [zg, pkl], [zg])
                tm_cols(w, cols, 512, c, evz)
                pbs = bank()
                for k in range(16):
                    MM(pbs[:c, :16], hT[k][:, cols], wsm[:, k, :], k == 0, k == 15, [wsm, hT[k]], [pbs])
                ACT(bt[:c, bi, :], pbs[:c, 0:4], AF.Sigmoid, [pbs], [bt])
                TT(tg[:c, 0:4], pbs[:c, 4:8], pkl[:c, PK_DTB:PK_DTB + 4], ALU.add, [pbs, pkl], [tg])
                ACT(tg[:c, 4:8], tg[:c, 0:4], AF.Exp, [tg], [tg])
                ACT(tg[:c, 8:12], tg[:c, 4:8], AF.Ln, [tg], [tg], bias=1.0)
                TT(gg[:c, bi, :], tg[:c, 8:12], nalog[l][:c, :], ALU.mult, [tg, nalog[l]], [gg])
                CP(ffr[:c, bi, :], pbs[:c, 8:16], [pbs], [ffr])
        fence(keep=gdn_mark)
        c4 = 4 * c
        gmat = alloc([128, c4], F32, "gmat")
        Gcol = alloc([128, 4], F32, "Gcol")
        Gb = alloc([128, c4], F32, "Gb")
        EGb = alloc([128, c4], F32, "EGb")
        tmpm = [alloc([128, c], F32, "tmpm%d" % i) for i in range(2)]
        DLs = alloc([128, c4], F32, "DLs")
        DUm = alloc([128, c4], F32, "DUm")
        Lm = [alloc([128, c4], F32, "Lm%d" % i) for i in range(2)]
        Um = [alloc([128, c4], F32, "Um%d" % i) for i in range(2)]
        Pm = alloc([128, c4], F32, "Pm")
        MTm = alloc([128, c4], F32, "MTm")
        kbeg = alloc([128, 4, 128], F32, "kbeg")
        kd = alloc([128, 4, 128], F32, "kd")
        vb = alloc([128, 4, 128], F32, "vb")
        nwT = alloc([128, c4], F32, "nwT")
        delta = alloc([128, 4, 128], F32, "delta")
        qgT = alloc([128, c4], F32, "qgT")
        sm = alloc([128, 32], F32, "sm")
        ytok = alloc([128, 512], BF16, "ytok")
        nstep = int(np.log2(c)) - 1
        for si, sgm in enumerate(segs):
            S = sgm["seq"].S[l]
            for s in range(nsub):
                t0 = sgm["col0"] + s * c
                cols = slice(t0, t0 + c)
                bi = si * nsub + s
                qT_, kT_, vT_ = qkvT
                pk_tok = ps[4]
                pv_tok = ps[5]
                for h in range(4):
                    MM(pk_tok[:c, h * 128:(h + 1) * 128], kT_[:, h, cols], ident_b, True, True, [kT_, cbf], [pk_tok])
                    MM(pv_tok[:c, h * 128:(h + 1) * 128], vT_[:, h, cols], ident_b, True, True, [vT_, cbf], [pv_tok])
                pg = bank()
                MM(pg[:c, 0:4], tri_f[:c, :c], gg[:c, bi, :], True, True, [cst, gg], [pg])
                CP(Gcol[:c, :], pg[:c, 0:4], [pg], [Gcol])
                for h in range(4):
                    TS(gmat[:c, h * c:(h + 1) * c], tri_f[:c, :c], gg[:c, bi, h:h + 1], None, ALU.mult, None, [cst, gg], [gmat])
                pgb = bank()
                MM(pgb[:, :c4], ones_f[:c, :], gmat[:c, :c4], True, True, [cst, gmat], [pgb])
                CP(Gb[:, :c4], pgb[:, :c4], [pgb], [Gb])
                ACT(EGb[:, :c4], pgb[:, :c4], AF.Exp, [pgb], [EGb])
                ACT(sm[:c, 0:4], Gcol[:c, :], AF.Exp, [Gcol], [sm])
                TT(sm[:c, 4:8], sm[:c, 0:4], bt[:c, bi, :], ALU.mult, [sm, bt], [sm])
                for h in range(4):
                    ACT(sm[:c, 8 + h:9 + h], Gcol[:c, h:h + 1], AF.Exp, [Gcol, Gb], [sm], scale=-1.0, bias=Gb[:c, h * c + c - 1:h * c + c])
                for h in range(4):
                    TS(kbeg[:c, h, :], pk_tok[:c, h * 128:(h + 1) * 128], sm[:c, 4 + h:5 + h], None, ALU.mult, None, [pk_tok, sm], [kbeg])
                    TS(kd[:c, h, :], pk_tok[:c, h * 128:(h + 1) * 128], sm[:c, 8 + h:9 + h], None, ALU.mult, None, [pk_tok, sm], [kd])
                    TS(vb[:c, h, :], pv_tok[:c, h * 128:(h + 1) * 128], bt[:c, bi, h:h + 1], None, ALU.mult, None, [pv_tok, bt], [vb])
                for h in range(4):
                    hc = slice(h * c, (h + 1) * c)
                    tm = tmpm[0]
                    TS(tm[:c, :], Gb[:c, hc], Gcol[:c, h:h + 1], 0.0, ALU.subtract, ALU.max, [Gb, Gcol], [tm])
                    ACT(tm[:c, :], tm[:c, :], AF.Exp, [tm], [tm], scale=-1.0)
                    STT(DLs[:c, hc], tm[:c, :], bt[:c, bi, h:h + 1], mls_f[:c, :c], ALU.mult, ALU.mult, [tm, bt, cst], [DLs])
                    tm2 = tmpm[1]
                    TS(tm2[:c, :], Gb[:c, hc], Gcol[:c, h:h + 1], 0.0, ALU.subtract, ALU.min, [Gb, Gcol], [tm2])
                    ACT(tm2[:c, :], tm2[:c, :], AF.Exp, [tm2], [tm2])
                    TT(DUm[:c, hc], tm2[:c, :], tri_f[:c, :c], ALU.mult, [tm2, cst], [DUm])
                pkk = bank()
                pqk = bank()
                for h in range(4):
                    hc = slice(h * c, (h + 1) * c)
                    MM(pkk[:c, hc], kT_[:, h, cols], kT_[:, h, cols], True, True, [kT_], [pkk])
                    MM(pqk[:c, hc], kT_[:, h, cols], qT_[:, h, cols], True, True, [kT_, qT_], [pqk])
                L0 = Lm[0]
                TT(L0[:c, :c4], pkk[:c, :c4], DLs[:c, :c4], ALU.mult, [pkk, DLs], [L0])
                TT(MTm[:c, :c4], pqk[:c, :c4], DUm[:c, :c4], ALU.mult, [pqk, DUm], [MTm])
                pB = bank()
                for h in range(4):
                    hc = slice(h * c, (h + 1) * c)
                    MM(pB[:c, hc], L0[:c, hc], ident_f[:c, :c], True, True, [L0, cst], [pB])
                U0 = Um[0]
                CP(U0[:c, :c4], pB[:c, :c4], [pB], [U0], eng="act")
                for h in range(4):
                    hc = slice(h * c, (h + 1) * c)
                    TT(Pm[:c, hc], ident_f[:c, :c], pB[:c, hc], ALU.subtract, [cst, pB], [Pm])
                Lc, Uc = L0, U0
                for it in range(1, nstep + 1):
                    Ln_, Un_ = Lm[it % 2], Um[it % 2]
                    pU = bank()
                    pL = bank()
                    for h in range(4):
                        hc = slice(h * c, (h + 1) * c)
                        MM(pL[:c, hc], Uc[:c, hc], Lc[:c, hc], True, True, [Uc, Lc], [pL])
                        if it < nstep:
                            MM(pU[:c, hc], Lc[:c, hc], Uc[:c, hc], True, True, [Uc, Lc], [pU])
                    CP(Ln_[:c, :c4], pL[:c, :c4], [pL], [Ln_])
                    if it < nstep:
                        CP(Un_[:c, :c4], pU[:c, :c4], [pU], [Un_], eng="act")
                    pP = bank()
                    for h in range(4):
                        hc = slice(h * c, (h + 1) * c)
                        MM(pP[:c, hc], Ln_[:c, hc], Pm[:c, hc], True, True, [Ln_, Pm], [pP])
                    TT(Pm[:c, :c4], Pm[:c, :c4], pP[:c, :c4], ALU.add, [Pm, pP], [Pm])
                    Lc, Uc = Ln_, Un_
                pW = bank()
                for h in range(4):
                    hc = slice(h * c, (h + 1) * c)
                    MM(pW[:, hc], kbeg[:c, h, :], Pm[:c, hc], True, True, [kbeg, Pm], [pW])
                ACT(nwT[:, :c4], pW[:, :c4], AF.Copy, [pW], [nwT], scale=-1.0)
                pD = bank()
                for h in range(4):
                    hc = slice(h * c, (h + 1) * c)
                    MM(pD[:c, h * 128:(h + 1) * 128], Pm[:c, hc], vb[:c, h, :], True, False, [Pm, vb], [pD])
                    MM(pD[:c, h * 128:(h + 1) * 128], nwT[:, hc], S[:, h, :], False, True, [nwT, S], [pD])
                CP(delta[:c, :, :], pD[:c, :].rearrange("p (h v) -> p h v", h=4), [pD], [delta])
                for h in range(4):
                    hc = slice(h * c, (h + 1) * c)
                    TT(qgT[:, hc], qT_[:, h, cols], EGb[:, hc], ALU.mult, [qT_, EGb], [qgT])
                pO = bank()
                for h in range(4):
                    hc = slice(h * c, (h + 1) * c)
                    MM(pO[:c, h * 128:(h + 1) * 128], qgT[:, hc], S[:, h, :], True, False, [qgT, S], [pO])
                    MM(pO[:c, h * 128:(h + 1) * 128], MTm[:c, hc], delta[:c, h, :], False, True, [MTm, delta], [pO])
                pS = bank()
                for h in range(4):
                    MM(pS[:, h * 128:(h + 1) * 128], kd[:c, h, :], delta[:c, h, :], True, True, [kd, delta], [pS])
                for h in range(4):
                    STT(S[:, h, :], S[:, h, :], EGb[:, h * c + c - 1:h * c + c], pS[:, h * 128:(h + 1) * 128], ALU.mult, ALU.add, [S, EGb, pS], [S])
                for h in range(4):
                    ACT(junk[:c, :], pO[:c, h * 128:(h + 1) * 128], AF.Square, [pO], [junk, sm], accum_out=sm[:c, 12 + h:13 + h])
                ACT(sm[:c, 16:20], sm[:c, 12:16], AF.Sqrt, [sm], [sm], bias=EPS, scale=1.0 / 128)
                RCP(sm[:c, 16:20], sm[:c, 16:20], [sm], [sm])
                for h in range(4):
                    STT(ytok[:c, h * 128:(h + 1) * 128], pO[:c, h * 128:(h + 1) * 128], sm[:c, 16 + h:17 + h], zg[:c, bi, h * 128:(h + 1) * 128],
                        ALU.mult, ALU.mult, [pO, sm, zg], [ytok])
                pT = bank()
                for h in range(4):
                    MM(pT[:, h * c:(h + 1) * c], ytok[:c, h * 128:(h + 1) * 128], ident_b[:c, :c], True, True, [ytok, cbf], [pT])
                for h in range(4):
                    CP(oT[h][:, cols], pT[:, h * c:(h + 1) * c], [pT], [oT[h]], eng="act" if h % 2 else "dve")
        if is_last_tile:
            for si, sgm in enumerate(segs):
                sq_ = sgm["seq"]
                DMA("pool", sgm["gs_out"][l].rearrange("h k v -> k h v"), sq_.S[l][:], [sq_.S[l]], [])
                DMA("pool", sgm["gc_out"][l].rearrange("p (c j) -> p c j", j=3), sq_.gch[l][:], [sq_.gch[l]], [])

        fence()
        QT = alloc([128, 8, T], BF16, "QT")
        KT = alloc([128, 8, T], BF16, "KT")
        Va = alloc([128, nseg * nsub, 8, 129], BF16, "Va")
        stg = [alloc([128, 512], F32, "stg%d" % i) for i in range(3)]
        strr = [0]
        MEMSET(Va[:, :, :, 128:129], 1.0, [Va], eng="pool")
        for half in range(2):
            w = wload((l, "fq", half), win[:, 2056 + half * 512:2056 + (half + 1) * 512], 16, 512)
            fm_cols(w, T, 4, lambda cc, pb, half=half: ACT(QT[:, half * 4 + cc, :], pb[:, :T], AF.Copy, [pb], [QT], scale=128.0 ** -0.5))
        for half in range(2):
            w = wload((l, "fk", half), win[:, 3080 + half * 512:3080 + (half + 1) * 512], 16, 512)
            fm_cols(w, T, 4, lambda cc, pb, half=half: CP(KT[:, half * 4 + cc, :], pb[:, :T], [pb], [KT]))
            for si, sgm in enumerate(segs):
                for s in range(nsub):
                    cols = slice(sgm["col0"] + s * c, sgm["col0"] + (s + 1) * c)

                    def evk(pb, sgm=sgm, s=s, half=half):
                        st = stg[strr[0] % 3]
                        strr[0] += 1
                        ACT(st[:c, :], pb[:c, :], AF.Copy, [pb], [st])
                        DMA("pool", sgm["kout"][l][s * c:(s + 1) * c, half * 512:(half + 1) * 512], st[:c, :], [st], [sgm["kbuf"][l]])
                    tm_cols(w, cols, 512, c, evk)
        for half in range(2):
            w = wload((l, "fv", half), win[:, 4104 + half * 512:4104 + (half + 1) * 512], 16, 512)
            for si, sgm in enumerate(segs):
                for s in range(nsub):
                    cols = slice(sgm["col0"] + s * c, sgm["col0"] + (s + 1) * c)
                    bi = si * nsub + s

                    def evv(pb, sgm=sgm, s=s, half=half, bi=bi):
                        st = stg[strr[0] % 3]
                        strr[0] += 1
                        ACT(st[:c, :], pb[:c, :], AF.Copy, [pb], [st])
                        CP(Va[:c, bi, half * 4:(half + 1) * 4, 0:128], pb[:c, :].rearrange("p (h d) -> p h d", h=4), [pb], [Va])
                        DMA("pool", sgm["vout"][l][s * c:(s + 1) * c, half * 512:(half + 1) * 512], st[:c, :], [st], [sgm["vbuf"][l]])
                    tm_cols(w, cols, 512, c, evv)
        lf = alloc([128, nseg * nsub, 8], F32, "lf")
        t8 = alloc([128, 16], F32, "t8")
        for si, sgm in enumerate(segs):
            sq_ = sgm["seq"]
            Fb = sq_.Fb[l]
            if not sq_.prompt:
                nblk = PB + 1
                sq_.Fk_l = alloc([128, nblk, 8], F32, "sFk")
                sq_.Fref_l = alloc([128, nblk, 8], F32, "sFref")
                clg = alloc([128, PB, 8], F32, "clg")
                for q4 in range(0, PB, 8):
                    n = min(8, PB - q4)
                    DMA("pool", clg[:, q4:q4 + n, :], cl[l, sq_.sidx, q4 * 128:(q4 + n) * 128, :].rearrange("(b p) h -> p b h", p=128), [], [clg])
                for kb in range(PB):
                    pf = bank()
                    MM(pf[:, 0:8], tri_f, clg[:, kb, :], True, True, [cst, clg], [pf])
                    MM(pf[:, 8:16], ones_f, clg[:, kb, :], True, True, [cst, clg], [pf])
                    TT(sq_.Fk_l[:, kb, :], pf[:, 0:8], Fb[:, :], ALU.add, [pf, Fb], [sq_.Fk_l])
                    TT(Fb[:, :], Fb[:, :], pf[:, 8:16], ALU.add, [Fb, pf], [Fb])
                Fk, Fref = sq_.Fk_l, sq_.Fref_l
            else:
                Fk, Fref = sq_.Fk[l], sq_.Fref[l]
            for s in range(nsub):
                bi = si * nsub + s
                blk = sgm["blk0"] + s
                TT(t8[:c, 0:8], ffr[:c, bi, :], pkl[:c, PK_FB:PK_FB + 8], ALU.add, [ffr, pkl], [t8])
                ACT(t8[:c, 8:16], t8[:c, 0:8], AF.Exp, [t8], [t8], scale=-1.0)
                ACT(t8[:c, 0:8], t8[:c, 8:16], AF.Ln, [t8], [t8], bias=1.0)
                TS(lf[:c, bi, :], t8[:c, 0:8], -1.0, None, ALU.mult, None, [t8], [lf])
                DMA("pool", sgm["lout"][l][s * c:(s + 1) * c, :], lf[:c, bi, :], [lf], [])
                pf = bank()
                MM(pf[:c, 0:8], tri_f[:c, :c], lf[:c, bi, :], True, True, [cst, lf], [pf])
                MM(pf[:, 8:16], ones_f[:c, :], lf[:c, bi, :], True, True, [cst, lf], [pf])
                CP(Fref[:, blk, :], Fb[:, :], [Fb], [Fref])
                TT(Fk[:c, blk, :], pf[:c, 0:8], Fb[:c, :], ALU.add, [pf, Fb], [Fk])
                TT(Fb[:, :], Fb[:, :], pf[:, 8:16], ALU.add, [Fb, pf], [Fb])
        Kh = [alloc([128, 4, 128], BF16, "Kh%d" % i) for i in range(2)]
        Vh = [alloc([128, 4, 129], BF16, "Vh%d" % i) for i in range(2)]
        for v_ in Vh:
            MEMSET(v_[:, :, 128:129], 1.0, [v_], eng="pool")
        KTh = [alloc([128, 512], BF16, "KTh%d" % i) for i in range(2)]
        PT = [alloc([128, 512], BF16, "PT%d" % i) for i in range(3)]
        bias4 = [alloc([128, 4], F32, "bias4_%d" % i) for i in range(4)]
        yf = alloc([128, nseg * nsub, 1024], BF16, "yf")
        fsm = alloc([128, 16], F32, "fsm")
        hrr = [0]
        prr = [0]
        brr = [0]
        for si, sgm in enumerate(segs):
            sq_ = sgm["seq"]
            if sq_.prompt:
                Fk, Fref = sq_.Fk[l], sq_.Fref[l]
                nhist = sgm["blk0"] // 4
            else:
                Fk, Fref = sq_.Fk_l, sq_.Fref_l
                nhist = PB // 4
            q0 = sgm["col0"]
            qb0 = sgm["blk0"]
            for h in range(8):
                started = [False] * nsub
                Oacc = [ps[4 + qs] for qs in range(nsub)]
                items = []
                hist_ctx = {}

                def mk_hist(j, kb):
                    it = {}

                    def score():
                        if kb == 0:
                            kh = Kh[hrr[0] % 2]
                            vh = Vh[hrr[0] % 2]
                            kth = KTh[hrr[0] % 2]
                            hrr[0] += 1
                            if sq_.prompt:
                                ksrc = fk_p[l, j * 512:(j + 1) * 512, h * 128:(h + 1) * 128]
                                vsrc = fv_p[l, j * 512:(j + 1) * 512, h * 128:(h + 1) * 128]
                                kdep, vdep = [kvK[l][j]], [kvV[l][j]]
                            else:
                                ksrc = ck[l, sq_.sidx, j * 512:(j + 1) * 512, h * 128:(h + 1) * 128]
                                vsrc = cv[l, sq_.sidx, j * 512:(j + 1) * 512, h * 128:(h + 1) * 128]
                                kdep, vdep = [], []
                            DMA("pool", kh[:, :, :], ksrc.rearrange("(b p) d -> p b d", p=128), kdep, [kh])
                            DMA("pool", vh[:, :, 0:128], vsrc.rearrange("(b p) d -> p b d", p=128), vdep, [vh])
                            ptr = ps[3]
                            for kb2 in range(4):
                                MM(ptr[:, kb2 * 128:(kb2 + 1) * 128], kh[:, kb2, :], ident_b, True, True, [kh, cbf], [ptr])
                            CP(kth[:, :], ptr[:, :], [ptr], [kth])
                            hist_ctx[j] = (vh, kth)
                        vh, kth = hist_ctx[j]
                        kblk = j * 4 + kb
                        pst = bank((0, 1, 2))
                        MM(pst[:, :L], kth[:, kb * 128:(kb + 1) * 128], QT[:, h, q0:q0 + L], True, True, [kth, QT], [pst])
                        b4 = bias4[brr[0] % 4]
                        brr[0] += 1
                        TS(b4[:, 0:nsub], Fref[:, qb0:qb0 + nsub, h], Fk[:, kblk, h:h + 1], None, ALU.subtract, None, [Fref, Fk], [b4])
                        it["pst"], it["b4"], it["vh"] = pst, b4, vh

                    def rest():
                        pst, b4, vh = it["pst"], it["b4"], it["vh"]
                        pt = PT[prr[0] % 3]
                        prr[0] += 1
                        for qs in range(nsub):
                            ACT(pt[:, qs * c:(qs + 1) * c], pst[:, qs * c:(qs + 1) * c], AF.Exp, [pst, b4], [pt], bias=b4[:, qs:qs + 1])
                        for qs in range(nsub):
                            MM(Oacc[qs][:c, 0:129], pt[:, qs * c:(qs + 1) * c], vh[:, kb, :], not started[qs], False, [pt, vh], [Oacc[qs]])
                            started[qs] = True
                    return score, rest

                def mk_cur(kb):
                    it = {}
                    bi = si * nsub + kb
                    kblk = qb0 + kb
                    nq = nsub - kb

                    def score():
                        pst = bank((0, 1, 2))
                        MM(pst[:c, :nq * c], KT[:, h, q0 + kb * c:q0 + (kb + 1) * c], QT[:, h, q0 + kb * c:q0 + L], True, True, [KT, QT], [pst])
                        b4 = bias4[brr[0] % 4]
                        brr[0] += 1
                        TS(b4[:c, 0:nq], Fref[:c, qb0 + kb:qb0 + nsub, h], Fk[:c, kblk, h:h + 1], None, ALU.subtract, None, [Fref, Fk], [b4])
                        it["pst"], it["b4"] = pst, b4

                    def rest():
                        pst, b4 = it["pst"], it["b4"]
                        pt = PT[prr[0] % 3]
                        prr[0] += 1
                        for qi in range(nq):
                            ACT(pt[:c, qi * c:(qi + 1) * c], pst[:c, qi * c:(qi + 1) * c], AF.Exp, [pst, b4], [pt], bias=b4[:c, qi:qi + 1])
                        TT(pt[:c, 0:c], pt[:c, 0:c], tri_b[:c, :c], ALU.mult, [pt, cbf], [pt])
                        for qi in range(nq):
                            qs = kb + qi
                            MM(Oacc[qs][:c, 0:129], pt[:c, qi * c:(qi + 1) * c], Va[:c, bi, h, :], not started[qs], qs == kb, [pt, Va], [Oacc[qs]])
                            started[qs] = True
                    return score, rest

                for j in range(nhist):
                    for kb in range(4):
                        items.append(mk_hist(j, kb))
                for kb in range(nsub):
                    items.append(mk_cur(kb))
                for ii in range(len(items) + 1):
                    if ii < len(items):
                        items[ii][0]()
                    if ii >= 1:
                        items[ii - 1][1]()
                for qs in range(nsub):
                    bi = si * nsub + qs
                    O = Oacc[qs]
                    RCP(fsm[:c, 0:1], O[:c, 128:129], [O], [fsm])
                    ACT(junk[:c, :], O[:c, 0:128], AF.Square, [O, fsm], [junk, fsm], scale=fsm[:c, 0:1], accum_out=fsm[:c, 1:2])
                    ACT(fsm[:c, 2:3], fsm[:c, 1:2], AF.Sqrt, [fsm], [fsm], bias=EPS, scale=1.0 / 128)
                    RCP(fsm[:c, 2:3], fsm[:c, 2:3], [fsm], [fsm])
                    TT(fsm[:c, 3:4], fsm[:c, 2:3], fsm[:c, 0:1], ALU.mult, [fsm], [fsm])
                    STT(yf[:c, bi, h * 128:(h + 1) * 128], O[:c, 0:128], fsm[:c, 3:4], pkl[:c, PK_FNG + h * 128:PK_FNG + (h + 1) * 128],
                        ALU.mult, ALU.mult, [O, fsm, pkl], [yf])
            for qs in range(nsub):
                bi = si * nsub + qs
                cols = slice(q0 + qs * c, q0 + (qs + 1) * c)
                for hg in range(2):
                    pT = bank((0, 1, 2, 3))
                    for hh in range(4):
                        h = hg * 4 + hh
                        MM(pT[:, hh * c:(hh + 1) * c], yf[:c, bi, h * 128:(h + 1) * 128], ident_b[:c, :c], True, True, [yf, cbf], [pT])
                    for hh in range(4):
                        h = hg * 4 + hh
                        CP(oT[4 + h][:, cols], pT[:, hh * c:(hh + 1) * c], [pT], [oT[4 + h]], eng="act" if hh % 2 else "dve")

        fence()
        lxr = alloc([128, 4, nseg, 3 + L], F32, "lxr")
        gel = alloc([128, 4, T], BF16, "gel")
        xc = alloc([128, 4, nseg, L], F32, "xc")
        xcb = alloc([128, 4, T], BF16, "xcb")
        lt = [alloc([128, T], F32, "lt%d" % i) for i in range(6)]
        hs = alloc([128, nseg, L], F32, "hs")
        w = wload((l, "lx"), win[:, 5136:5648], 16, 512)

        def evlx(cc, pb):
            for si, sgm in enumerate(segs):
                lch = sgm["seq"].lch[l]
                CP(lxr[:, cc, si, 0:3], lch[:, cc, :], [lch], [lxr], eng="pool")
            P.add("act", lambda e: e.activation(out=lxr[:, cc, :, 3:3 + L], in_=pb[:, :T].rearrange("p (s l) -> p s l", s=nseg), func=AF.Copy),
                  bl([pb]) + [regbuf], bl([lxr]), guard=False)
            for si, sgm in enumerate(segs):
                lch = sgm["seq"].lch[l]
                CP(lch[:, cc, :], lxr[:, cc, si, L:L + 3], [lxr], [lch], eng="pool")
        fm_cols(w, T, 4, evlx)
        w = wload((l, "lg"), win[:, 5648:6160], 16, 512)

        def evg(cc, pb):
            g1 = lt[cc % 2]
            ACT(g1[:, :T], pb[:, :T], AF.Square, [pb], [g1])
            TS(g1[:, :T], g1[:, :T], 0.044715, 1.0, ALU.mult, ALU.add, [g1], [g1])
            TT(g1[:, :T], g1[:, :T], pb[:, :T], ALU.mult, [g1, pb], [g1])
            ACT(g1[:, :T], g1[:, :T], AF.Sigmoid, [g1], [g1], scale=1.5957691216057308)
            TT(gel[:, cc, :], g1[:, :T], pb[:, :T], ALU.mult, [g1, pb], [gel])
        fm_cols(w, T, 4, evg)
        for n in range(4):
            cw = PK_LCW + n * 4
            TS(xc[:, n, :, :], lxr[:, n, :, 0:L], pkl[:, cw:cw + 1], pkl[:, PK_LCB + n:PK_LCB + n + 1], ALU.mult, ALU.add, [lxr, pkl], [xc])
            for j in range(1, 4):
                STT(xc[:, n, :, :], lxr[:, n, :, j:j + L], pkl[:, cw + j:cw + j + 1], xc[:, n, :, :], ALU.mult, ALU.add, [lxr, pkl, xc], [xc])
            xcn = xc[:, n, :, :].rearrange("p s l -> p (s l)")
            ACT(xcb[:, n, :], xcn, AF.Copy, [xc], [xcb])
            pr = bank()
            MM(pr[:, :T], lw[l][0][:, n, :], xcb[:, n, :], True, True, [lw[l][0], xcb], [pr])
            pi = bank()
            MM(pi[:, :T], lw[l][1][:, n, :], xcb[:, n, :], True, True, [lw[l][1], xcb], [pi])
            r_, i_, a_, a2_, u_, t_ = lt
            ACT(r_[:, :T], pr[:, :T], AF.Sigmoid, [pr, pkl], [r_], bias=pkl[:, PK_LBA + n:PK_LBA + n + 1])
            ACT(i_[:, :T], pi[:, :T], AF.Sigmoid, [pi, pkl], [i_], bias=pkl[:, PK_LBX + n:PK_LBX + n + 1])
            ACT(a_[:, :T], r_[:, :T], AF.Exp, [r_, c1[l]], [a_], scale=c1[l][:, n:n + 1])
            TT(a2_[:, :T], a_[:, :T], a_[:, :T], ALU.mult, [a_], [a2_])
            ACT(a2_[:, :T], a2_[:, :T], AF.Sqrt, [a2_], [a2_], scale=-1.0, bias=1.0)
            TT(u_[:, :T], i_[:, :T], xcn, ALU.mult, [i_, xc], [u_])
            TT(u_[:, :T], u_[:, :T], a2_[:, :T], ALU.mult, [u_, a2_], [u_])
            for si, sgm in enumerate(segs):
                lh = sgm["seq"].lh[l]
                c0 = sgm["col0"]
                P.add("dve", lambda e, si=si, c0=c0, lh=lh, n=n: e.tensor_tensor_scan(out=hs[:, si, :], data0=a_[:, c0:c0 + L], data1=u_[:, c0:c0 + L],
                                                                                initial=lh[:, n:n + 1], op0=ALU.mult, op1=ALU.add),
                      bl([a_, u_, lh]), bl([hs]))
                CP(lh[:, n:n + 1], hs[:, si, L - 1:L], [hs], [lh])
            hsn = hs[:].rearrange("p s l -> p (s l)")
            sq = sqg[0]
            ACT(sq[:, :T], hsn, AF.Square, [hs], [sq])
            pn = bank()
            MM(pn[:, :T], ones_b, sq[:, :T], True, True, [sq, cbf], [pn])
            ACT(t_[:, :T], pn[:, :T], AF.Sqrt, [pn], [t_], bias=EPS, scale=1.0 / 128)
            RCP(t_[:, :T], t_[:, :T], [t_], [t_])
            STT(t_[:, :T], hsn, pkl[:, PK_LNG + n:PK_LNG + n + 1], t_[:, :T], ALU.mult, ALU.mult, [hs, pkl, t_], [t_])
            TT(oT[12 + n][:, :T], t_[:, :T], gel[:, n, :], ALU.mult, [t_, gel], [oT[12 + n]])
        if is_last_tile:
            for si, sgm in enumerate(segs):
                sq_ = sgm["seq"]
                DMA("pool", sgm["lh_out"][l], sq_.lh[l][:], [sq_.lh[l]], [])
                DMA("pool", sgm["lc_out"][l].rearrange("p (c j) -> p c j", j=3), sq_.lch[l][:], [sq_.lch[l]], [])

        fence()
        for g in range(4):
            w = wload((l, "wo", g), w_out[l][:, g * 512:(g + 1) * 512], 16, 512)
            fm_cols(w, T, 4, lambda cc, pb, g=g: TT(xT[g * 4 + cc][:, :T], xT[g * 4 + cc][:, :T], pb[:, :T], ALU.add, [xT[g * 4 + cc], pb], [xT[g * 4 + cc]]),
                    src=oT)

        fence()
        rmsnorm_to_hT(T, PK_GFFN, l)
        actT = alloc([128, 44, T], BF16, "actT")
        gpr = [alloc([128, nseg, 2 + L], F32, "gpr%d" % i) for i in range(2)]
        gcv = [alloc([128, nseg, L], F32, "gcv%d" % i) for i in range(2)]
        sg_ = [alloc([128, T], F32, "sg%d" % i) for i in range(2)]
        for j in range(11):
            wg = wload((l, "ug", j), w_up[l][:, j * 512:(j + 1) * 512], 16, 512)
            wv = wload((l, "uv", j), w_up[l][:, DFF + j * 512:DFF + (j + 1) * 512], 16, 512)
            for cc in range(4):
                m = j * 4 + cc
                pg = bank()
                pv = bank()
                for k in range(16):
                    MM(pg[:, :T], wg[:, k, cc * 128:(cc + 1) * 128], hT[k][:, :T], k == 0, k == 15, [wg, hT[k]], [pg])
                for k in range(16):
                    MM(pv[:, :T], wv[:, k, cc * 128:(cc + 1) * 128], hT[k][:, :T], k == 0, k == 15, [wv, hT[k]], [pv])
                gp = gpr[m % 2]
                gc_ = gcv[m % 2]
                s_ = sg_[m % 2]
                for si, sgm in enumerate(segs):
                    fch = sgm["seq"].fch[l]
                    CP(gp[:, si, 0:2], fch[:, m, :], [fch], [gp], eng="pool")
                P.add("act", lambda e, gp=gp, pg=pg: e.activation(out=gp[:, :, 2:2 + L], in_=pg[:, :T].rearrange("p (s l) -> p s l", s=nseg), func=AF.Copy),
                      bl([pg]) + [regbuf], bl([gp]), guard=False)
                for si, sgm in enumerate(segs):
                    fch = sgm["seq"].fch[l]
                    CP(fch[:, m, :], gp[:, si, L:L + 2], [gp], [fch], eng="pool")
                cw = PK_FCW + m * 3
                TS(gc_[:], gp[:, :, 0:L], pkl[:, cw:cw + 1], None, ALU.mult, None, [gp, pkl], [gc_])
                for jj in range(1, 3):
                    STT(gc_[:], gp[:, :, jj:jj + L], pkl[:, cw + jj:cw + jj + 1], gc_[:], ALU.mult, ALU.add, [gp, pkl, gc_], [gc_])
                ACT(s_[:, :T], gc_[:].rearrange("p s l -> p (s l)"), AF.Silu, [gc_], [s_])
                TT(actT[:, m, :], s_[:, :T], pv[:, :T], ALU.mult, [s_, pv], [actT])
        if is_last_tile:
            for si, sgm in enumerate(segs):
                sq_ = sgm["seq"]
                DMA("pool", sgm["fc_out"][l].rearrange("p (c j) -> p c j", j=2), sq_.fch[l][:], [sq_.fch[l]], [])
        kgroups = [(0, 16), (16, 16), (32, 12)]
        for g in range(4):
            acc = [ps[4 + cc] for cc in range(4)]
            for gi, (k0, nk) in enumerate(kgroups):
                w = wload((l, "dn", g, gi), w_dn[l][k0 * 128:(k0 + nk) * 128, g * 512:(g + 1) * 512], nk, 512)
                for cc in range(4):
                    for k in range(nk):
                        MM(acc[cc][:, :T], w[:, k, cc * 128:(cc + 1) * 128], actT[:, k0 + k, :], (gi == 0 and k == 0), (gi == 2 and k == nk - 1),
                           [w, actT], [acc[cc]])
            for cc in range(4):
                xk = xT[g * 4 + cc]
                TT(xk[:, :T], xk[:, :T], acc[cc][:, :T], ALU.add, [xk, acc[cc]], [xk])

        if last_layer:
            fence()
            pb = bank()
            for k in range(16):
                sq = sqs[k % 2]
                ACT(sq[:, :T], xT[k][:, :T], AF.Square, [xT[k]], [sq])
                MM(pb[:, :T], ones_b, sq[:, :T], k == 0, k == 15, [sq, cbf], [pb])
            ACT(rstd[:, :T], pb[:, :T], AF.Sqrt, [pb], [rstd], bias=EPS, scale=1.0 / D)
            RCP(rstd[:, :T], rstd[:, :T], [rstd], [rstd])
            yT = [alloc([128, T], F32, "yT%d" % i) for i in range(4)]
            ytk = [alloc([128, 512], F32, "ytk%d" % i) for i in range(3)]
            yrr = [0]
            for g in range(4):
                for cc in range(4):
                    k = g * 4 + cc
                    STT(yT[cc][:, :T], xT[k][:, :T], pkl[:, PK_GFIN + k:PK_GFIN + k + 1], rstd[:, :T], ALU.mult, ALU.mult, [xT[k], pkl, rstd], [yT[cc]])
                for si, sgm in enumerate(segs):
                    for s in range(nsub):
                        t0 = sgm["col0"] + s * c
                        pT = bank()
                        for cc in range(4):
                            MM(pT[:c, cc * 128:(cc + 1) * 128], yT[cc][:, t0:t0 + c], ident_f, True, True, [yT[cc], cst], [pT])
                        yt = ytk[yrr[0] % 3]
                        yrr[0] += 1
                        CP(yt[:c, :], pT[:c, :], [pT], [yt], eng="act" if yrr[0] % 2 else "dve")
                        DMA("pool", sgm["yout"][s * c:(s + 1) * c, g * 512:(g + 1) * 512], yt[:c, :], [yt], [])

    def load_x(segs, c, nsub):
        xs = [alloc([128, D], F32, "xs%d" % i) for i in range(2)]
        rr = 0
        for si, sgm in enumerate(segs):
            for s in range(nsub):
                t0 = sgm["col0"] + s * c
                st = xs[rr % 2]
                rr += 1
                DMA("pool", st[:c, :], sgm["xin"][s * c:(s + 1) * c, :], [], [st])
                for g in range(4):
                    pT = bank()
                    for cc in range(4):
                        k = g * 4 + cc
                        MM(pT[:, cc * c:(cc + 1) * c], st[:c, k * 128:(k + 1) * 128], ident_f[:c, :c], True, True, [st, cst], [pT])
                    for cc in range(4):
                        k = g * 4 + cc
                        CP(xT[k][:, t0:t0 + c], pT[:, cc * c:(cc + 1) * c], [pT], [xT[k]], eng="act" if cc % 2 else "dve")

    for i in range(NT):
        sgm = dict(seq=seq_p, col0=0, L=TP, blk0=i * (TP // 128),
                   xin=x_p[i * TP:(i + 1) * TP, :], yout=y_p[i * TP:(i + 1) * TP, :],
                   kout=[fk_p[l, i * TP:(i + 1) * TP, :] for l in range(DEPTH)],
                   vout=[fv_p[l, i * TP:(i + 1) * TP, :] for l in range(DEPTH)],
                   lout=[fl_p[l, i * TP:(i + 1) * TP, :] for l in range(DEPTH)],
                   kbuf=[kvK[l][i] for l in range(DEPTH)], vbuf=[kvV[l][i] for l in range(DEPTH)],
                   gs_out=[gs_p[l] for l in range(DEPTH)], gc_out=[gc_p[l] for l in range(DEPTH)],
                   lh_out=[lh_p[l] for l in range(DEPTH)], lc_out=[lc_p[l] for l in range(DEPTH)], fc_out=[fc_p[l] for l in range(DEPTH)])
        for l in range(DEPTH):
            run_tile([sgm], TP, 128, l, (lambda sgm=sgm: load_x([sgm], 128, TP // 128)) if l == 0 else None,
                     l == DEPTH - 1, None, i == NT - 1)
    dummyK = [Buf("dk") for _ in range(DEPTH)]
    dummyV = [Buf("dv") for _ in range(DEPTH)]
    ssegs = []
    for si in range(2):
        ssegs.append(dict(seq=seq_s[si], col0=si * LS, L=LS, blk0=PB,
                          xin=x_s[si], yout=y_s[si],
                          kout=[fk_s[l, si] for l in range(DEPTH)], vout=[fv_s[l, si] for l in range(DEPTH)],
                          lout=[fl_s[l, si] for l in range(DEPTH)], kbuf=dummyK, vbuf=dummyV,
                          gs_out=[gs_s[l, si] for l in range(DEPTH)], gc_out=[gc_s[l, si] for l in range(DEPTH)],
                          lh_out=[lh_s[l, si] for l in range(DEPTH)], lc_out=[lc_s[l, si] for l in range(DEPTH)],
                          fc_out=[fc_s[l, si] for l in range(DEPTH)]))
    for l in range(DEPTH):
        run_tile(ssegs, 2 * LS, LS, l, (lambda: load_x(ssegs, LS, 1)) if l == 0 else None, l == DEPTH - 1, None, True)
    P.emit()
    return nc


def _pack_small(inp, l):
    pk = np.zeros((128, NPK), np.float32)

    def fm(v, n):
        return np.ascontiguousarray(v.reshape(n, 128).T)
    pk[:, PK_GMIX:PK_GMIX + 16] = fm(inp["norm_mix_g"][l], 16)
    pk[:, PK_GFFN:PK_GFFN + 16] = fm(inp["norm_ffn_g"][l], 16)
    pk[:, PK_GFIN:PK_GFIN + 16] = fm(inp["final_norm_g"], 16)
    pk[:, PK_GCW:PK_GCW + 48] = inp["gdn_conv_w"][l].reshape(4, 12, 128).transpose(2, 1, 0).reshape(128, 48)
    pk[:, PK_LCW:PK_LCW + 16] = inp["lru_conv_w"][l].reshape(4, 4, 128).transpose(2, 1, 0).reshape(128, 16)
    pk[:, PK_LCB:PK_LCB + 4] = fm(inp["lru_conv_b"][l], 4)
    pk[:, PK_LBA:PK_LBA + 4] = fm(inp["lru_b_a"][l], 4)
    pk[:, PK_LBX:PK_LBX + 4] = fm(inp["lru_b_x"][l], 4)
    pk[:, PK_LAM:PK_LAM + 4] = fm(inp["lru_lambda"][l], 4)
    pk[:, PK_LNG:PK_LNG + 4] = fm(inp["lru_norm_g"][l], 4)
    pk[:, PK_FCW:PK_FCW + 132] = inp["ffn_conv_w"][l].reshape(3, 44, 128).transpose(2, 1, 0).reshape(128, 132)
    pk[:, PK_GNG4:PK_GNG4 + 512] = np.tile(inp["gdn_norm_g"][l], 4)[None, :]
    pk[:, PK_FNG:PK_FNG + 1024] = inp["fox_norm_g"][l].reshape(1024)[None, :]
    pk[:, PK_ALOG:PK_ALOG + 4] = inp["gdn_a_log"][l][None, :]
    pk[:, PK_DTB:PK_DTB + 4] = inp["gdn_dt_bias"][l][None, :]
    pk[:, PK_FB:PK_FB + 8] = inp["fox_f_bias"][l][None, :]
    return pk


def _consts():
    c = np.zeros((128, NCST), np.float32)
    c[:, C_ID:C_ID + 128] = np.eye(128)
    c[:, C_TRI:C_TRI + 128] = np.triu(np.ones((128, 128)))
    c[:, C_ONE:C_ONE + 128] = 1.0
    c[:, C_MLS:C_MLS + 128] = np.tril(np.ones((128, 128)), -1)
    return c


_NC_CACHE = {}


def run(inp, n_cores=8, SEQ=4096, PAST=4096):
    inp = {k: np.asarray(v) for k, v in inp.items()}
    DEPTH = 2
    key = (SEQ, PAST)
    if key not in _NC_CACHE:
        _NC_CACHE[key] = build(SEQ=SEQ, PAST=PAST)
    nc = _NC_CACHE[key]
    BP = inp["x_prompt"].shape[0]
    BS = inp["x_sample"].shape[0]
    pk = np.stack([_pack_small(inp, l) for l in range(DEPTH)])
    cst = _consts()
    c32 = np.ascontiguousarray

    def fmT(a, n, j):
        sh = a.shape[:-2]
        return c32(a.reshape(sh + (j, n, 128)).transpose(tuple(range(len(sh))) + (len(sh) + 2, len(sh) + 1, len(sh))).reshape(sh + (128, n * j)))

    if n_cores >= 2 * BP:
        work = [2 * i for i in range(BP)]
    else:
        work = list(range(min(n_cores, BP)))
    in_maps = []
    zero_map = None
    for core in range(n_cores):
        if core in work:
            wi = work.index(core)
            b = wi % BP
            ss = [(2 * (wi % (BS // 2))), (2 * (wi % (BS // 2)) + 1)]
            m = {
                "x_p": c32(inp["x_prompt"][b]),
                "x_s": c32(inp["x_sample"][ss]),
                "ck": c32(inp["cache_fox_k"][:, ss].reshape(DEPTH, 2, PAST, 1024)),
                "cv": c32(inp["cache_fox_v"][:, ss].reshape(DEPTH, 2, PAST, 1024)),
                "cl": c32(inp["cache_fox_logf"][:, ss]),
                "sg": c32(inp["state_gdn"][:, ss]),
                "sgc": fmT(inp["state_gdn_conv"][:, ss], 12, 3),
                "sl": fmT(inp["state_lru"][:, ss][:, :, None, :], 4, 1),
                "slc": fmT(inp["state_lru_conv"][:, ss], 4, 3),
                "sfc": fmT(inp["state_ffn_conv"][:, ss], 44, 2),
                "w_in": inp["w_in"], "w_out": inp["w_out"], "w_up": inp["ffn_w_up"], "w_dn": inp["ffn_w_down"],
                "lwa": inp["lru_w_a"], "lwx": inp["lru_w_x"], "pk": pk, "cst": cst,
            }
            if zero_map is None:
                zero_map = {kk: np.zeros(vv.shape, np.float32) for kk, vv in m.items()}
                zero_map["cst"] = cst
            in_maps.append(m)
        else:
            in_maps.append(None)
    in_maps = [m if m is not None else zero_map for m in in_maps]
    res = run_bass_kernel_spmd(nc, in_maps, core_ids=list(range(n_cores)))
    R = res.results

    def unfm(a, n, j):
        sh = a.shape[:-2]
        return c32(a.reshape(sh + (128, n, j)).transpose(tuple(range(len(sh))) + (len(sh) + 2, len(sh) + 1, len(sh))).reshape(sh + (j, n * 128)))

    pc = [work[b] for b in range(BP)]
    sc = [work[i] for i in range(BS // 2)]
    y_p = np.stack([R[c]["y_p"] for c in pc])
    y_s = np.concatenate([R[c]["y_s"] for c in sc])
    fk_p = np.stack([R[c]["fk_p"] for c in pc], 1).reshape(DEPTH, BP, SEQ, 8, 128)
    fv_p = np.stack([R[c]["fv_p"] for c in pc], 1).reshape(DEPTH, BP, SEQ, 8, 128)
    fl_p = np.stack([R[c]["fl_p"] for c in pc], 1)
    gs_p = np.stack([R[c]["gs_p"] for c in pc], 1)
    gc_p = unfm(np.stack([R[c]["gc_p"] for c in pc], 1), 12, 3)
    lh_p = unfm(np.stack([R[c]["lh_p"] for c in pc], 1), 4, 1)[:, :, 0, :]
    lc_p = unfm(np.stack([R[c]["lc_p"] for c in pc], 1), 4, 3)
    fc_p = unfm(np.stack([R[c]["fc_p"] for c in pc], 1), 44, 2)
    fk_s = np.concatenate([R[c]["fk_s"] for c in sc], 1).reshape(DEPTH, BS, 16, 8, 128)
    fv_s = np.concatenate([R[c]["fv_s"] for c in sc], 1).reshape(DEPTH, BS, 16, 8, 128)
    fl_s = np.concatenate([R[c]["fl_s"] for c in sc], 1)
    gs_s = np.concatenate([R[c]["gs_s"] for c in sc], 1)
    gc_s = unfm(np.concatenate([R[c]["gc_s"] for c in sc], 1), 12, 3)
    lh_s = unfm(np.concatenate([R[c]["lh_s"] for c in sc], 1), 4, 1)[:, :, 0, :]
    lc_s = unfm(np.concatenate([R[c]["lc_s"] for c in sc], 1), 4, 3)
    fc_s = unfm(np.concatenate([R[c]["fc_s"] for c in sc], 1), 44, 2)
    outs = (y_p, y_s, fk_p, fv_p, fl_p, gs_p, gc_p, lh_p, lc_p, fc_p, fk_s, fv_s, fl_s, gs_s, gc_s, lh_s, lc_s, fc_s)
    return tuple(np.ascontiguousarray(o, dtype=np.float32) for o in outs)


def kernel(**inputs):
    return run(inputs, n_cores=8, SEQ=4096, PAST=4096)
```
